# Optimizing a Trainium2 kernel written in Bass

```python
import jax, jax.numpy as jnp
from jax import lax
import numpy as np

D_MODEL = 2048
BATCH = 16
SEQ = 256
DEPTH = 1
DEC_BATCH = 8
DEC_SEQ = 1024
PAST_LEN = 512

GRID_W = 64
N_HEADS = 16
N_KV_HEADS = 4
HEAD_DIM = 128
ROPE_THETA = 10000.0
Q_BLOCK = 128
N_RET_HEADS = 8
RET_KEY_DIM = 128
RET_VAL_DIM = 256
RET_CHUNK = 128
D_FF = ((8 * D_MODEL // 3 + 255) // 256) * 256
EPS = 1e-6

ATTN_Q = N_HEADS * HEAD_DIM
ATTN_KV = N_KV_HEADS * HEAD_DIM
RET_QK = N_RET_HEADS * RET_KEY_DIM
RET_V = N_RET_HEADS * RET_VAL_DIM
IN_SPLITS = (ATTN_Q, ATTN_KV, ATTN_KV, RET_QK, RET_QK, RET_V, RET_V, D_MODEL, D_MODEL)
IN_OFFSETS = tuple(sum(IN_SPLITS[:i + 1]) for i in range(len(IN_SPLITS) - 1))
N_IN = sum(IN_SPLITS)

kernel_name = 'hybrid_gqa_retention_prefix_dit_step'


def rms_norm(x, g):
    xf = x.astype(jnp.float32)
    y = xf * lax.rsqrt(jnp.mean(xf * xf, axis=-1, keepdims=True) + EPS)
    return (y * g.astype(jnp.float32)).astype(x.dtype)


def rope_2d(x):
    L = x.shape[1]
    rows = L // GRID_W
    row = jnp.repeat(jnp.arange(rows, dtype=jnp.float32), GRID_W)
    col = jnp.tile(jnp.arange(GRID_W, dtype=jnp.float32), rows)
    half = HEAD_DIM // 2
    quarter = half // 2
    inv_freq = ROPE_THETA ** (-jnp.arange(quarter, dtype=jnp.float32) / quarter)
    xf = x.astype(jnp.float32)

    def rot(xs, pos):
        ang = pos[:, None] * inv_freq[None, :]
        cos = jnp.cos(ang)[None, :, None, :]
        sin = jnp.sin(ang)[None, :, None, :]
        x1, x2 = xs[..., :quarter], xs[..., quarter:]
        return jnp.concatenate([x1 * cos - x2 * sin, x2 * cos + x1 * sin], axis=-1)

    out = jnp.concatenate([rot(xf[..., :half], row), rot(xf[..., half:], col)], axis=-1)
    return out.astype(x.dtype)


def block_attention(q, k, v):
    b, Lq, _, d = q.shape
    grp = N_HEADS // N_KV_HEADS
    n_blk = Lq // Q_BLOCK
    scale = d ** -0.5
    qb = q.reshape(b, n_blk, Q_BLOCK, N_KV_HEADS, grp, d).transpose(1, 0, 2, 3, 4, 5)

    def one_block(q_blk):
        s = jnp.einsum('bqkgd,bskd->bkgqs', q_blk, k).astype(jnp.float32) * scale
        p = jax.nn.softmax(s, axis=-1)
        return jnp.einsum('bkgqs,bskd->bqkgd', p.astype(v.dtype), v)

    o = lax.map(one_block, qb)
    return o.transpose(1, 0, 2, 3, 4, 5).reshape(b, Lq, N_HEADS * d)


def retention_dir(q, k, v, log_g, s0):
    b, h, L, dk = q.shape
    dv = v.shape[-1]
    n_c = L // RET_CHUNK
    qc = q.reshape(b, h, n_c, RET_CHUNK, dk)
    kc = k.reshape(b, h, n_c, RET_CHUNK, dk)
    vc = v.reshape(b, h, n_c, RET_CHUNK, dv)
    idx = jnp.arange(RET_CHUNK, dtype=jnp.float32)
    diff = idx[:, None] - idx[None, :]
    dmat = jnp.where(diff >= 0, jnp.exp(jnp.maximum(diff, 0.0)[None] * log_g[:, None, None]), 0.0)
    scores = jnp.einsum('bhcid,bhcjd->bhcij', qc, kc) * dmat[None, :, None]
    inner = jnp.einsum('bhcij,bhcje->bhcie', scores, vc)
    zeta = jnp.exp((RET_CHUNK - 1 - idx)[None, :] * log_g[:, None])
    u = jnp.einsum('bhcjd,bhcje->cbhde', kc * zeta[None, :, None, :, None], vc)
    g_chunk = jnp.exp(RET_CHUNK * log_g)[None, :, None, None]

    def step(s, u_c):
        return g_chunk * s + u_c, s

    s_fin, s_prev = lax.scan(step, s0, u)
    xi = jnp.exp((idx + 1.0)[None, :] * log_g[:, None])
    cross = jnp.einsum('bhcid,cbhde->bhcie', qc, s_prev) * xi[None, :, None, :, None]
    return (inner + cross).reshape(b, h, L, dv), s_fin


def bidir_retention(q, k, v, log_g_f, log_g_b, s0_f, s0_b):
    o_f, s_f = retention_dir(q, k, v, log_g_f, s0_f)
    o_b, s_b = retention_dir(jnp.flip(q, 2), jnp.flip(k, 2), jnp.flip(v, 2), log_g_b, s0_b)
    return o_f + jnp.flip(o_b, 2), s_f, s_b


def token_mixer(h, w_in, q_norm, k_norm, dec_f, dec_b, ret_norm, w_ba, w_br, w_out,
                ctx_k, ctx_v, s0_f, s0_b, latent):
    b, L, _ = h.shape
    z = h @ w_in
    q_a, k_a, v_a, q_r, k_r, v_r, g_r, gate_a, gate_r = jnp.split(z, IN_OFFSETS, axis=-1)
    q_a = rms_norm(q_a.reshape(b, L, N_HEADS, HEAD_DIM), q_norm)
    k_a = rms_norm(k_a.reshape(b, L, N_KV_HEADS, HEAD_DIM), k_norm)
    v_a = v_a.reshape(b, L, N_KV_HEADS, HEAD_DIM)
    if latent:
        q_use = rope_2d(q_a)
        k_all = jnp.concatenate([ctx_k.astype(k_a.dtype), rope_2d(k_a)], axis=1)
        v_all = jnp.concatenate([ctx_v.astype(v_a.dtype), v_a], axis=1)
    else:
        q_use, k_all, v_all = q_a, k_a, v_a
    o_a = block_attention(q_use, k_all, v_all)
    qr = q_r.reshape(b, L, N_RET_HEADS, RET_KEY_DIM).transpose(0, 2, 1, 3).astype(jnp.float32)
    kr = (k_r.reshape(b, L, N_RET_HEADS, RET_KEY_DIM).transpose(0, 2, 1, 3).astype(jnp.float32)
          * (RET_KEY_DIM ** -0.5))
    vr = v_r.reshape(b, L, N_RET_HEADS, RET_VAL_DIM).transpose(0, 2, 1, 3).astype(jnp.float32)
    if latent:
        s0_f = s0_f.astype(jnp.float32)
        s0_b = s0_b.astype(jnp.float32)
    else:
        s0_f = jnp.zeros((b, N_RET_HEADS, RET_KEY_DIM, RET_VAL_DIM), jnp.float32)
        s0_b = s0_f
    log_g_f = jax.nn.log_sigmoid(dec_f.astype(jnp.float32))
    log_g_b = jax.nn.log_sigmoid(dec_b.astype(jnp.float32))
    o_r, s_f, s_b = bidir_retention(qr, kr, vr, log_g_f, log_g_b, s0_f, s0_b)
    mu = jnp.mean(o_r, axis=-1, keepdims=True)
    var = jnp.mean(jnp.square(o_r - mu), axis=-1, keepdims=True)
    o_r = ((o_r - mu) * lax.rsqrt(var + EPS)).transpose(0, 2, 1, 3).reshape(b, L, RET_V)
    o_r = (o_r * ret_norm.astype(jnp.float32)).astype(h.dtype) * jax.nn.silu(g_r)
    merged = jax.nn.sigmoid(gate_a) * (o_a @ w_ba) + jax.nn.sigmoid(gate_r) * (o_r @ w_br)
    return merged @ w_out, k_a, v_a, s_f.astype(h.dtype), s_b.astype(h.dtype)


def swiglu(h, w_g, w_u, w_d):
    return (jax.nn.silu(h @ w_g) * (h @ w_u)) @ w_d


def layer(x, mod, norm_a, norm_f, w_in, q_norm, k_norm, dec_f, dec_b, ret_norm, w_ba, w_br, w_out,
          w_g, w_u, w_d, ctx_k, ctx_v, s0_f, s0_b, latent):
    sh_a, sc_a, g_a, sh_f, sc_f, g_f = jnp.split(mod, 6, axis=-1)
    h = rms_norm(x, norm_a) * (1.0 + sc_a) + sh_a
    mix, k_a, v_a, s_f, s_b = token_mixer(h, w_in, q_norm, k_norm, dec_f, dec_b, ret_norm, w_ba, w_br,
                                          w_out, ctx_k, ctx_v, s0_f, s0_b, latent)
    x = x + g_a * mix
    h = rms_norm(x, norm_f) * (1.0 + sc_f) + sh_f
    x = x + g_f * swiglu(h, w_g, w_u, w_d)
    return x, k_a, v_a, s_f, s_b


def setup_inputs(seed: int = 0) -> dict:
    key = jax.random.key(seed)
    ks = jax.random.split(key, 32)
    f32 = jnp.float32

    def nrm(k, shape, scale):
        return jax.random.normal(k, shape, f32) * scale

    base_decay = jnp.log(2.0 ** (5.0 + jnp.arange(N_RET_HEADS, dtype=f32)) - 1.0)
    return {
        'x_prompt': nrm(ks[0], (BATCH, SEQ, D_MODEL), 1.0),
        'x_sample': nrm(ks[1], (DEC_BATCH, DEC_SEQ, D_MODEL), 1.0),
        'cache_attn_k': nrm(ks[2], (DEC_BATCH, DEPTH, PAST_LEN, N_KV_HEADS, HEAD_DIM), 1.0),
        'cache_attn_v': nrm(ks[3], (DEC_BATCH, DEPTH, PAST_LEN, N_KV_HEADS, HEAD_DIM), 1.0),
        'state_ret_fwd': nrm(ks[4], (DEC_BATCH, DEPTH, N_RET_HEADS, RET_KEY_DIM, RET_VAL_DIM), 0.5),
        'state_ret_bwd': nrm(ks[5], (DEC_BATCH, DEPTH, N_RET_HEADS, RET_KEY_DIM, RET_VAL_DIM), 0.5),
        'c': nrm(ks[6], (DEC_BATCH, D_MODEL), 1.0),
        'c_ctx': nrm(ks[7], (D_MODEL,), 1.0),
        'norm_attn': 1.0 + nrm(ks[8], (DEPTH, D_MODEL), 0.01),
        'norm_ffn': 1.0 + nrm(ks[9], (DEPTH, D_MODEL), 0.01),
        'w_mod': nrm(ks[10], (DEPTH, D_MODEL, 6 * D_MODEL), 0.5 * D_MODEL ** -0.5),
        'b_mod': nrm(ks[11], (DEPTH, 6 * D_MODEL), 0.01),
        'w_in': nrm(ks[12], (DEPTH, D_MODEL, N_IN), D_MODEL ** -0.5),
        'q_norm': 1.0 + nrm(ks[13], (DEPTH, HEAD_DIM), 0.01),
        'k_norm': 1.0 + nrm(ks[14], (DEPTH, HEAD_DIM), 0.01),
        'ret_decay_fwd': base_decay[None, :] + nrm(ks[15], (DEPTH, N_RET_HEADS), 0.01),
        'ret_decay_bwd': base_decay[None, :] + nrm(ks[16], (DEPTH, N_RET_HEADS), 0.01),
        'ret_norm': 1.0 + nrm(ks[17], (DEPTH, RET_V), 0.01),
        'w_branch_attn': nrm(ks[18], (DEPTH, ATTN_Q, D_MODEL), ATTN_Q ** -0.5),
        'w_branch_ret': nrm(ks[19], (DEPTH, RET_V, D_MODEL), RET_V ** -0.5),
        'w_out': nrm(ks[20], (DEPTH, D_MODEL, D_MODEL), D_MODEL ** -0.5),
        'w_ffn_gate': nrm(ks[21], (DEPTH, D_MODEL, D_FF), D_MODEL ** -0.5),
        'w_ffn_up': nrm(ks[22], (DEPTH, D_MODEL, D_FF), D_MODEL ** -0.5),
        'w_ffn_down': nrm(ks[23], (DEPTH, D_FF, D_MODEL), D_FF ** -0.5),
        'final_norm': 1.0 + nrm(ks[24], (D_MODEL,), 0.01),
    }


def reference(x_prompt, x_sample, cache_attn_k, cache_attn_v, state_ret_fwd, state_ret_bwd, c, c_ctx,
              norm_attn, norm_ffn, w_mod, b_mod, w_in, q_norm, k_norm, ret_decay_fwd, ret_decay_bwd,
              ret_norm, w_branch_attn, w_branch_ret, w_out, w_ffn_gate, w_ffn_up, w_ffn_down, final_norm):
    xp, xs = x_prompt, x_sample
    new_k, new_v, new_sf, new_sb = [], [], [], []
    for l in range(DEPTH):
        mod_ctx = (jax.nn.silu(c_ctx) @ w_mod[l] + b_mod[l])[None, None, :]
        mod_lat = (jax.nn.silu(c) @ w_mod[l] + b_mod[l])[:, None, :]
        xp, k_l, v_l, sf_l, sb_l = layer(
            xp, mod_ctx, norm_attn[l], norm_ffn[l], w_in[l], q_norm[l], k_norm[l], ret_decay_fwd[l],
            ret_decay_bwd[l], ret_norm[l], w_branch_attn[l], w_branch_ret[l], w_out[l], w_ffn_gate[l],
            w_ffn_up[l], w_ffn_down[l], None, None, None, None, False)
        xs, _, _, _, _ = layer(
            xs, mod_lat, norm_attn[l], norm_ffn[l], w_in[l], q_norm[l], k_norm[l], ret_decay_fwd[l],
            ret_decay_bwd[l], ret_norm[l], w_branch_attn[l], w_branch_ret[l], w_out[l], w_ffn_gate[l],
            w_ffn_up[l], w_ffn_down[l], cache_attn_k[:, l], cache_attn_v[:, l], state_ret_fwd[:, l],
            state_ret_bwd[:, l], True)
        new_k.append(k_l)
        new_v.append(v_l)
        new_sf.append(sf_l)
        new_sb.append(sb_l)
    y_prompt = rms_norm(xp, final_norm)
    y_sample = rms_norm(xs, final_norm)
    return (y_prompt, y_sample, jnp.stack(new_k, axis=1), jnp.stack(new_v, axis=1),
            jnp.stack(new_sf, axis=1), jnp.stack(new_sb, axis=1))
```

```python
import math
import bisect
from contextlib import ExitStack

import numpy as np
import concourse.bass as bass
import concourse.mybir as mybir
from concourse.bass_utils import run_bass_kernel_spmd

F32 = mybir.dt.float32
BF16 = mybir.dt.bfloat16
AF = mybir.ActivationFunctionType
ALU = mybir.AluOpType

D = 2048
DFF = 5632
NIN = 13312
EPS = 1e-6
NTOK = 1536
NSLOT = 4
STOP = 99
SLOTE = 4096

PC_BMOD, PC_NA, PC_NF, PC_FN, PC_RN, PC_C, PC_CC, PC_QN, PC_KN, PC_DF, PC_DB, NPC = 0, 96, 112, 128, 144, 160, 176, 192, 193, 194, 202, 210
CT_ID, CT_RM, CT_DF, CT_MF, CT_DB, CT_MB, CT_I2, CT_XF, CT_XB, CT_ZF, CT_ZB, CT_COS, CT_SIN, NCT = (
    0, 128, 256, 384, 512, 640, 768, 896, 1024, 1152, 1153, 1154, 2178, 3202)


FFN_GROUPS = [(0, 5), (5, 5), (10, 4), (14, 4), (18, 4)]


def _w_in_tiles():
    t = [(0, 16, 256 * i, 256) for i in range(12)]
    t += [(0, 16, 3072 + 128 * i, 128) for i in range(16)]
    t += [(0, 16, 5120 + 256 * i, 256) for i in range(32)]
    return t


def w_in_index(c0):
    if c0 < 3072:
        return c0 // 256
    if c0 < 5120:
        return 12 + (c0 - 3072) // 128
    return 28 + (c0 - 5120) // 256


TILES = {
    "w_mod": [(0, 16, 256 * i, 256) for i in range(48)],
    "w_in": _w_in_tiles(),
    "w_ba": [(0, 16, 256 * i, 256) for i in range(8)],
    "w_br": [(0, 16, 256 * i, 256) for i in range(8)],
    "w_out": [(0, 16, 256 * i, 256) for i in range(8)],
    "w_g": [(0, 16, 256 * i, 256) for i in range(22)],
    "w_u": [(0, 16, 256 * i, 256) for i in range(22)],
    "w_d": [(256 * tl0, 2 * ntl, 256 * cgo, 256) for (tl0, ntl) in FFN_GROUPS for cgo in range(8)],
}


def tile_offsets(name):
    offs, o = [], 0
    for (_, nk, _, nc_) in TILES[name]:
        offs.append(o)
        o += 128 * nk * nc_
    return offs, o


def pack_weight(name, W):
    W = np.asarray(W, dtype=np.float32)
    offs, total = tile_offsets(name)
    out = np.empty((total,), np.float32)
    for (r0, nk, c0, nc_), o in zip(TILES[name], offs):
        blk = W[r0:r0 + nk * 128, c0:c0 + nc_].reshape(nk, 128, nc_).transpose(1, 0, 2)
        out[o:o + 128 * nk * nc_] = blk.reshape(-1)
    return out


class Buf:
    __slots__ = ("name", "last_w", "readers", "excl", "by_eng")

    def __init__(self, name="", excl=False):
        self.name = name
        self.last_w = None
        self.readers = []
        self.excl = excl
        self.by_eng = {}


class DSem:
    __slots__ = ("h", "n", "name", "group", "exempt", "last")

    def __init__(self, h, name, group, exempt):
        self.h = h
        self.n = 0
        self.name = name
        self.group = group
        self.exempt = exempt
        self.last = None


class Op:
    __slots__ = ("eng", "fn", "deps", "signal", "cnt", "dsem", "dval", "is_dma", "epoch", "idx", "cdep")

    def __init__(self, eng, fn, epoch, idx):
        self.eng = eng
        self.fn = fn
        self.deps = []
        self.cdep = {}
        self.signal = False
        self.cnt = 0
        self.dsem = None
        self.dval = 0
        self.is_dma = False
        self.epoch = epoch
        self.idx = idx


class SigRef:
    __slots__ = ("eng", "cnt", "idx", "is_dma", "epoch", "signal")

    def __init__(self, eng, cnt, idx):
        self.eng = eng
        self.cnt = cnt
        self.idx = idx
        self.is_dma = False
        self.epoch = -1
        self.signal = True


class Kern:
    ENGS = ("pe", "act", "dve", "pool", "sp")

    def __init__(self, nc, stack):
        self.nc = nc
        self.stack = stack
        self.h = {"pe": nc.tensor, "act": nc.scalar, "dve": nc.vector, "pool": nc.gpsimd, "sp": nc.sync}
        self.sem = {e: stack.enter_context(nc.semaphore("s_" + e)) for e in self.ENGS}
        self.pending = {e: [] for e in self.ENGS}
        self.sigcnt = {e: 0 for e in self.ENGS}
        self.known = {e: {} for e in self.ENGS}
        self.lastop = {e: None for e in self.ENGS}
        self.dsems = []
        self.out_dmas = []
        self.epoch = 0
        self.finals = {}
        self.bar_deps = []
        self.bar_pending = set()
        self.nops = 0
        self.nwait = 0
        self.siglist = {e: ([], []) for e in self.ENGS}

    def dsem(self, name, group=False, exempt=False):
        d = DSem(self.stack.enter_context(self.nc.semaphore("d_" + name)), name, group, exempt)
        self.dsems.append(d)
        return d

    def _adddep(self, op, d):
        if d is op:
            return
        if (not d.is_dma) and d.epoch != op.epoch:
            idxs, cnts = self.siglist[d.eng]
            p = bisect.bisect_left(idxs, d.idx)
            if p >= len(idxs):
                return
            d = SigRef(d.eng, cnts[p], idxs[p])
        if d.is_dma:
            if op.is_dma and d.dsem is op.dsem and d.dsem.group:
                return
            if d not in op.deps:
                op.deps.append(d)
            return
        if op.eng == "pe" and d.eng == "pe":
            return
        cur = op.cdep.get(d.eng)
        if cur is None or cur.idx < d.idx:
            op.cdep[d.eng] = d

    def _track(self, op, reads, writes, bar):
        if op.eng in self.bar_pending or bar:
            self.bar_pending.discard(op.eng)
            for d in self.bar_deps:
                self._adddep(op, d)
        for b in reads:
            if b.last_w is not None:
                self._adddep(op, b.last_w)
        for b in writes:
            if b.last_w is not None:
                self._adddep(op, b.last_w)
            for r in b.readers:
                self._adddep(op, r)
        for b in list(reads) + list(writes):
            if b.excl:
                for e, o in b.by_eng.items():
                    if e != op.eng:
                        self._adddep(op, o)
                b.by_eng[op.eng] = op
        for b in reads:
            b.readers.append(op)
        for b in writes:
            b.last_w = op
            b.readers = []
        for d in op.cdep.values():
            op.deps.append(d)
            if d.epoch == op.epoch:
                d.signal = True

    def op(self, eng, fn, reads=(), writes=()):
        o = Op(eng, fn, self.epoch, self.nops)
        self.nops += 1
        self._track(o, reads, writes, False)
        self.pending[eng].append(o)
        self.lastop[eng] = o
        return o

    def dma(self, queue, fn, dsem, reads=(), writes=(), is_out=False, bar=False):
        o = Op(queue, fn, self.epoch, self.nops)
        o.is_dma = True
        self.nops += 1
        o.dsem = dsem
        dsem.n += 1
        o.dval = 16 * dsem.n
        dsem.last = o
        self._track(o, reads, writes, bar)
        self.pending[queue].append(o)
        if is_out:
            self.out_dmas.append(o)
        return o

    def flush(self):
        for e in self.ENGS:
            for o in self.pending[e]:
                if (not o.is_dma) and o.signal:
                    self.sigcnt[e] += 1
                    o.cnt = self.sigcnt[e]
                    self.siglist[e][0].append(o.idx)
                    self.siglist[e][1].append(o.cnt)
        for e in self.ENGS:
            h = self.h[e]
            known = self.known[e]
            for o in self.pending[e]:
                need = {}
                for d in o.deps:
                    if d.is_dma:
                        key, val = d.dsem.h, (16 * d.dsem.n if d.dsem.group else d.dval)
                    else:
                        key, val = self.sem[d.eng], d.cnt
                    if need.get(key, 0) < val:
                        need[key] = val
                for key, val in need.items():
                    if known.get(key, 0) < val:
                        h.wait_ge(key, val)
                        known[key] = val
                        self.nwait += 1
                ins = o.fn()
                if o.is_dma:
                    ins.then_inc(o.dsem.h, 16)
                elif o.signal:
                    ins.then_inc(self.sem[e], 1)
            self.pending[e] = []

    def barrier(self):
        fin = {}
        for e in ("pe", "act", "dve", "pool"):
            o = self.lastop[e]
            if o is not None and not o.is_dma and o.epoch == self.epoch:
                o.signal = True
                fin[e] = o
        self.finals[self.epoch] = fin
        deps = list(fin.values())
        for d in self.dsems:
            if not d.exempt and d.last is not None:
                deps.append(d.last)
        self.flush()
        self.epoch += 1
        self.bar_deps = deps
        self.bar_pending = {"pe", "act", "dve", "sp"}

    def finish(self):
        self.flush()
        h = self.h["sp"]
        fin = {}
        for o in self.out_dmas:
            if fin.get(o.dsem.h, 0) < o.dval:
                fin[o.dsem.h] = o.dval
        for key, val in fin.items():
            h.wait_ge(key, val)


def build():
    nc = bass.Bass("TRN2", target_bir_lowering=False)

    def din(name, shape):
        return nc.dram_tensor(name, list(shape), F32, kind="ExternalInput").ap()

    def dout(name, shape):
        return nc.dram_tensor(name, list(shape), F32, kind="ExternalOutput").ap()

    x_d = din("x", [NTOK, D])
    ck_d = din("ck", [512, 512])
    cv_d = din("cv", [512, 512])
    sf_d = din("sf0", [8, 128, 256])
    sb_d = din("sb0", [8, 128, 256])
    pcols_d = din("pcols", [128, NPC])
    ctab_d = din("ctab", [128, NCT])
    wflat = {}
    woffs = {}
    for nm in TILES:
        woffs[nm], tot = tile_offsets(nm)
        wflat[nm] = din(nm, [tot])

    def wt(nm, idx):
        (_, nk, _, nc_) = TILES[nm][idx]
        o = woffs[nm][idx]
        return wflat[nm][o:o + 128 * nk * nc_].rearrange("(p e) -> p e", p=128)

    def win_t(c0):
        return wt("w_in", w_in_index(c0))
    y_d = dout("y", [NTOK, D])
    nk_d = dout("nk", [512, 512])
    nv_d = dout("nv", [512, 512])
    nsf_d = dout("nsf", [2, 8, 128, 256])
    nsb_d = dout("nsb", [2, 8, 128, 256])


    with ExitStack() as st0:
        K = Kern(nc, st0)

        uniq = [0]

        def sbt(st, name, shape, dt):
            uniq[0] += 1
            return st.enter_context(nc.sbuf_tensor(f"sb_{name}_{uniq[0]}", list(shape), dt))

        def MM(out, lhsT, rhs, start, stop, reads, writes):
            K.op("pe", lambda: nc.tensor.matmul(out, lhsT=lhsT, rhs=rhs, start=start, stop=stop), reads, writes)

        def TRP(out, in_, ident, reads, writes):
            K.op("pe", lambda: nc.tensor.transpose(out, in_, ident), reads, writes)

        def ACT(out, in_, func, reads, writes, scale=1.0, bias=0.0, accum_out=None):
            if accum_out is None:
                K.op("act", lambda: nc.scalar.activation(out=out, in_=in_, func=func, bias=bias, scale=scale), reads, writes)
            else:
                K.op("act", lambda: nc.scalar.activation(out=out, in_=in_, func=func, bias=bias, scale=scale, accum_out=accum_out), reads, writes)

        def TS(out, in0, s1, s2, op0, op1, reads, writes):
            if s2 is None:
                K.op("dve", lambda: nc.vector.tensor_scalar(out=out, in0=in0, scalar1=s1, scalar2=None, op0=op0), reads, writes)
            else:
                K.op("dve", lambda: nc.vector.tensor_scalar(out=out, in0=in0, scalar1=s1, scalar2=s2, op0=op0, op1=op1), reads, writes)

        def TT(out, in0, in1, op, reads, writes):
            K.op("dve", lambda: nc.vector.tensor_tensor(out=out, in0=in0, in1=in1, op=op), reads, writes)

        def STT(out, in0, scalar, in1, op0, op1, reads, writes):
            K.op("dve", lambda: nc.vector.scalar_tensor_tensor(out=out, in0=in0, scalar=scalar, in1=in1, op0=op0, op1=op1), reads, writes)

        def CPV(out, in_, reads, writes):
            K.op("dve", lambda: nc.vector.tensor_copy(out=out, in_=in_), reads, writes)

        def CPA(out, in_, reads, writes):
            K.op("act", lambda: nc.scalar.copy(out=out, in_=in_), reads, writes)

        def DMA(queue, out, in_, dsem, reads, writes, is_out=False, bar=False):
            eng = nc.sync if queue == "sp" else nc.gpsimd
            K.dma(queue, lambda: eng.dma_start(out=out, in_=in_), dsem, reads, writes, is_out=is_out, bar=bar)

        ctab = sbt(st0, "ctab", [128, NCT], F32)
        pcol = sbt(st0, "pcol", [128, NPC], F32)
        identb = sbt(st0, "identb", [128, 128], BF16)
        onesb = sbt(st0, "onesb", [128, 128], BF16)
        sct = sbt(st0, "sct", [128, 16, 2], BF16)
        modt = sbt(st0, "modt", [128, 96, 2], F32)
        gsa = sbt(st0, "gsa", [128, 16, 2], F32)
        gsf = sbt(st0, "gsf", [128, 16, 2], F32)
        lg = sbt(st0, "lg", [128, 16], F32)
        gch = sbt(st0, "gch", [128, 16], F32)
        zt = sbt(st0, "zt", [128, 16], F32)
        tmp16 = sbt(st0, "tmp16", [128, 16], F32)
        wsl = [sbt(st0, f"wsl{i}", [128, SLOTE], BF16) for i in range(NSLOT)]
        oaT = sbt(st0, "oaT", [128, 16, 1024], BF16)
        orT = sbt(st0, "orT", [128, 16, 1024], BF16)
        pbk = [st0.enter_context(nc.psum_tensor(f"pbk{i}", [128, 512], F32)) for i in range(8)]
        pbb = [Buf(f"pbk{i}", excl=True) for i in range(8)]
        ident = ctab[:, CT_ID:CT_ID + 128]

        cb = Buf("const")
        csem = K.dsem("const", group=True)
        mb = Buf("mod")
        wbuf = [Buf(f"w{i}") for i in range(NSLOT)]
        wsem = [K.dsem(f"w{i}", exempt=True) for i in range(NSLOT)]
        oab = [[Buf() for _ in range(2)] for _ in range(16)]
        orb = [[Buf() for _ in range(8)] for _ in range(16)]
        osem = K.dsem("out")
        xsem = [K.dsem("x0"), K.dsem("x1")]
        ssem = [K.dsem(f"st{i}") for i in range(4)]
        cvsem = K.dsem("cv")

        rot = {"mm": 0, "aux": 0, "tr": 0, "w": 0, "mm2": 0, "bo": 0}

        def bank(group):
            if group == "mm":
                i = rot["mm"] % 4
            elif group == "mm2":
                i = (0, 1, 5)[rot["mm2"] % 3]
            elif group == "bo":
                i = 2 + rot["bo"] % 2
            elif group == "aux":
                i = 4 + rot["aux"] % 2
            else:
                i = 6 + rot["tr"] % 2
            rot[group] += 1
            return pbk[i], pbb[i]

        def wload(parts):
            s = rot["w"] % NSLOT
            rot["w"] += 1
            for src, off in parts:
                ne = src.shape[1]
                DMA("pool", wsl[s][:, off:off + ne], src, wsem[s], [], [wbuf[s]])
            return wsl[s], wbuf[s]

        def wview(slot, off, nk, ncol):
            return slot[:, off:off + nk * ncol].rearrange("p (k n) -> p k n", k=nk)

        def rstd_from(dst, src, n, reads, writes):
            ACT(dst, src, AF.Ln, reads, writes, scale=1.0 / n, bias=EPS)
            ACT(dst, dst, AF.Exp, writes, writes, scale=-0.5)

        DMA("sp", ctab[:], ctab_d[:, :], csem, [], [cb])
        DMA("sp", pcol[:], pcols_d[:, :], csem, [], [cb])
        K.op("dve", lambda: nc.vector.memset(onesb[:], 1.0), [], [cb])
        CPA(identb[:], ident, [cb], [cb])
        if STOP <= -5:
            K.finish()
            return nc
        ACT(sct[:, :, 0], pcol[:, PC_C:PC_C + 16], AF.Silu, [cb], [mb])
        ACT(sct[:, :, 1], pcol[:, PC_CC:PC_CC + 16], AF.Silu, [cb], [mb])
        ACT(tmp16[:], pcol[:, PC_DF:PC_DF + 16], AF.Exp, [cb], [mb], scale=-1.0)
        ACT(tmp16[:], tmp16[:], AF.Ln, [mb], [mb], bias=1.0)
        TS(lg[:], tmp16[:], -1.0, None, ALU.mult, None, [mb], [mb])
        ACT(gch[:], lg[:], AF.Exp, [mb], [mb], scale=128.0)
        if STOP <= -4:
            K.finish()
            return nc
        for h in range(8):
            ACT(zt[:, h:h + 1], ctab[:, CT_ZF:CT_ZF + 1], AF.Exp, [cb, mb], [mb], scale=lg[:, h:h + 1])
            ACT(zt[:, 8 + h:9 + h], ctab[:, CT_ZB:CT_ZB + 1], AF.Exp, [cb, mb], [mb], scale=lg[:, 8 + h:9 + h])
        if STOP <= -3:
            K.finish()
            return nc
        mb2 = Buf("mod2")
        modq = list(range(48 if STOP >= 0 else 0))

        def mod_step(n=1, bk_=None):
            for _ in range(n):
                if not modq:
                    return
                cblk = modq.pop(0)
                slot, wb = wload([(wt("w_mod", cblk), 0)])
                wv = wview(slot, 0, 16, 256)
                ab, abb = bk_ if bk_ is not None else bank("aux")
                mp = ab[:, 0:4].rearrange("p (c w) -> p c w", w=2)
                for j in range(2):
                    for k in range(16):
                        MM(mp[:, j, :], wv[:, k, j * 128:(j + 1) * 128], sct[:, k, :], k == 0, k == 15, [wb, mb], [abb])
                tgt = mb if cblk < 16 else mb2
                for w in range(2):
                    TT(modt[:, 2 * cblk:2 * cblk + 2, w], mp[:, :, w], pcol[:, PC_BMOD + 2 * cblk:PC_BMOD + 2 * cblk + 2], ALU.add,
                       [abb, cb], [tgt])

        gsf_done = [False]

        def mod_flush():
            mod_step(len(modq))
            if not gsf_done[0]:
                gsf_done[0] = True
                for w in range(2):
                    STT(gsf[:, :, w], modt[:, 64:80, w], 1.0, pcol[:, PC_NF:PC_NF + 16], ALU.add, ALU.mult, [mb2, cb], [mb2])

        gsa_done = [False]

        def mod_first(n):
            if gsa_done[0]:
                return
            k_ = min(n, 16 - (48 - len(modq)))
            if k_ > 0:
                mod_step(k_)
            if 48 - len(modq) >= 16:
                gsa_done[0] = True
                for w in range(2):
                    STT(gsa[:, :, w], modt[:, 16:32, w], 1.0, pcol[:, PC_NA:PC_NA + 16], ALU.add, ALU.mult, [mb, cb], [mb])

        def col(t3, j, w):
            return t3[:, j, w:w + 1]

        SEGS = [
            dict(name="S", tok0=0, ntok=1024, w=0, latent=True,
                 qblocks=[(0, 512, list(range(12))), (512, 512, list(range(12)))], nkt=12,
                 seqs=[(0, 8)]),
            dict(name="P", tok0=1024, ntok=512, w=1, latent=False,
                 qblocks=[(0, 256, [0, 1]), (256, 256, [2, 3])], nkt=4,
                 seqs=[(0, 2), (2, 2)]),
        ]
        ATT_SCALE = 128 ** -0.5

        for seg in SEGS:
            if STOP < 1 or (STOP < 5 and seg["name"] == "P"):
                continue
            tok0, ntok, w, latent = seg["tok0"], seg["ntok"], seg["w"], seg["latent"]
            ntile = ntok // 128
            nblk = ntok // 512
            with ExitStack() as stS:
                hT = sbt(stS, "hT", [128, 16, 1024], BF16)
                hTb = [[Buf() for _ in range(16)] for _ in range(8)]

                def hreads(k, nb):
                    return [hTb[4 * nb + i][k] for i in range(4)]

                with ExitStack() as st1:
                    xsl = [sbt(st1, f"xsl{i}", [128, D], F32) for i in range(2)]
                    xn = [sbt(st1, f"xn{i}", [128, D], BF16) for i in range(2)]
                    junk = sbt(st1, "junk", [128, D], BF16)
                    ss = sbt(st1, "ss1", [128, 4], F32)
                    xb = [Buf(), Buf()]
                    xnb = [Buf(), Buf()]
                    ssb = [Buf(), Buf()]
                    jb = Buf()
                    def p1_A(t):
                        s = t % 2
                        DMA("sp", xsl[s][:], x_d[tok0 + t * 128: tok0 + (t + 1) * 128, :], xsem[s], [], [xb[s]])
                        ACT(junk[:], xsl[s][:], AF.Square, [xb[s]], [jb, ssb[s]], accum_out=ss[:, s:s + 1])
                        rstd_from(ss[:, 2 + s:3 + s], ss[:, s:s + 1], D, [ssb[s]], [ssb[s]])
                        TS(xn[s][:], xsl[s][:], ss[:, 2 + s:3 + s], None, ALU.mult, None, [xb[s], ssb[s]], [xnb[s]])

                    def p1_B(t):
                        s = t % 2
                        for k8 in range(2):
                            tb_, tbb = bank("tr")
                            tpv = tb_[:].bitcast(BF16)
                            for kk in range(8):
                                k = k8 * 8 + kk
                                TRP(tpv[:, kk * 128:(kk + 1) * 128], xn[s][:, k * 128:(k + 1) * 128], identb[:], [xnb[s], cb], [tbb])
                            dst = hT[:, k8 * 8:(k8 + 1) * 8, t * 128:(t + 1) * 128]
                            src = tpv.rearrange("p (k t) -> p k t", k=8)
                            hb8 = [hTb[t][k8 * 8 + kk] for kk in range(8)]
                            if k8 == 0:
                                CPA(dst, src, [tbb], hb8)
                            else:
                                CPV(dst, src, [tbb], hb8)
                        mod_first(2)

                    p1_A(0)
                    for t in range(ntile):
                        if t + 1 < ntile:
                            p1_A(t + 1)
                        p1_B(t)
                    mod_first(16)
                    for k in range(16):
                        hk = hT[:, k, 0:ntok]
                        hbk = [hTb[t][k] for t in range(ntile)]
                        if k % 2 == 0:
                            ACT(hk, hk, AF.Identity, hbk + [mb], hbk, scale=col(gsa, k, w), bias=col(modt, k, w))
                        else:
                            TS(hk, hk, col(gsa, k, w), col(modt, k, w), ALU.mult, ALU.add, hbk + [mb], hbk)
                K.barrier()

                def pipeline(units):
                    st_ = [None] * len(units)
                    if units:
                        st_[0] = units[0][0]()
                    for u in range(len(units)):
                        if u + 1 < len(units):
                            st_[u + 1] = units[u + 1][0]()
                        units[u][1](st_[u])

                def acc_fm(bk, bkb, ncols, lhs_fn, rhs_fn, nk, reads_fn):
                    for k in range(nk):
                        MM(bk[:, 0:ncols], lhs_fn(k), rhs_fn(k), k == 0, k == nk - 1, reads_fn(k), [bkb])

                with ExitStack() as st2:
                  if STOP >= 2:
                      nkt = seg["nkt"]
                      kT = sbt(st2, "kT", [128, 4, nkt * 128], BF16)
                      vall = sbt(st2, "vall", [128, nkt, 512], BF16)
                      qT = sbt(st2, "qT", [128, 4, ntok], BF16)
                      sqt_ = [sbt(st2, f"sqt{i}", [128, 512], BF16) for i in range(2)]
                      rst_ = [sbt(st2, f"rst{i}", [128, 512], F32) for i in range(2)]
                      knt_ = [sbt(st2, f"knt{i}", [128, 512], F32) for i in range(2)]
                      t1_ = [sbt(st2, f"t1{i}", [128, 512], F32) for i in range(2)]
                      t2_ = [sbt(st2, f"t2{i}", [128, 512], F32) for i in range(2)]
                      nrm_i = [0]
                      pT = [sbt(st2, f"pT{i}", [128, 512], BF16) for i in range(3)]
                      rdt = [sbt(st2, f"rdt{i}", [128, 512], F32) for i in range(2)]
                      if latent:
                          cks = sbt(st2, "cks", [128, 512], F32)
                      else:
                          kout = sbt(st2, "kout", [128, 4, 512], F32)
                          vout = sbt(st2, "vout", [128, 4, 512], F32)
                      kTb = [[Buf() for _ in range(12)] for _ in range(4)]
                      vb = [[Buf() for _ in range(2)] for _ in range(12)]
                      qTb = [[Buf() for _ in range(2)] for _ in range(4)]
                      ckb, koutb, voutb = (Buf() for _ in range(3))
                      sqb_, rsb_, knb_, t1b_, t2b_ = ([Buf(), Buf()] for _ in range(5))
                      rdb = [Buf(), Buf()]
                      pTb = [Buf() for _ in range(3)]
                      cksem = cvsem

                      def qk_norm_A(bk, bkb, normcol):
                          n = 512
                          pi = nrm_i[0] % 2
                          nrm_i[0] += 1
                          sqt, rst, knt = sqt_[pi], rst_[pi], knt_[pi]
                          sqb, rsb, knb = sqb_[pi], rsb_[pi], knb_[pi]
                          ACT(sqt[:], bk[:, 0:n], AF.Square, [bkb], [sqb])
                          ab, abb = bank("aux")
                          MM(ab[:, 0:n], onesb[:], sqt[:], True, True, [sqb, cb], [abb])
                          rstd_from(rst[:], ab[:, 0:n], 128, [abb], [rsb])
                          STT(knt[:], bk[:, 0:n], normcol, rst[:], ALU.mult, ALU.mult, [bkb, rsb, cb], [knb])
                          return pi

                      def qk_norm_B(pi, dst, dstbufs, tokc0, is_k_out=None):
                          n = 512
                          knt, t1, t2 = knt_[pi], t1_[pi], t2_[pi]
                          knb, t1b, t2b = knb_[pi], t1b_[pi], t2b_[pi]
                          if latent:
                              ab2, abb2 = bank("aux")
                              MM(ab2[:, 0:n], ctab[:, CT_RM:CT_RM + 128], knt[:], True, True, [knb, cb], [abb2])
                              TT(t1[:], knt[:], ctab[:, CT_COS + tokc0:CT_COS + tokc0 + n], ALU.mult, [knb, cb], [t1b])
                              TT(t2[:], ab2[:, 0:n], ctab[:, CT_SIN + tokc0:CT_SIN + tokc0 + n], ALU.mult, [abb2, cb], [t2b])
                              TT(dst, t1[:], t2[:], ALU.add, [t1b, t2b], dstbufs)
                          else:
                              CPA(dst, knt[:], [knb], dstbufs)
                              if is_k_out is not None:
                                  g = is_k_out
                                  tb_, tbb = bank("tr")
                                  for t in range(4):
                                      TRP(tb_[:, t * 128:(t + 1) * 128], knt[:, t * 128:(t + 1) * 128], ident, [knb, cb], [tbb])
                                  CPV(kout[:, :, g * 128:(g + 1) * 128], tb_[:].rearrange("p (t d) -> p t d", t=4), [tbb], [koutb])

                      def pipeline3(units):
                          n_ = len(units)
                          hs = [None] * n_
                          as_ = [None] * n_
                          if n_:
                              hs[0] = units[0][0]()
                          for u in range(n_):
                              if u + 1 < n_:
                                  hs[u + 1] = units[u + 1][0]()
                              as_[u] = units[u][1](hs[u])
                              if u >= 1:
                                  units[u - 1][2](as_[u - 1])
                          if n_:
                              units[n_ - 1][2](as_[n_ - 1])

                      def pipeline(units):
                          st_ = [None] * len(units)
                          if units:
                              st_[0] = units[0][0]()
                          for u in range(len(units)):
                              if u + 1 < len(units):
                                  st_[u + 1] = units[u + 1][0]()
                              units[u][1](st_[u])

                      for j in range(2):
                          slot, wb = wload([(win_t(2048 + 256 * j), 0)])
                          wv = wview(slot, 0, 16, 256)
                          units = []
                          for hh in range(2):
                              for nb in range(nblk):
                                  def head_fn(hh=hh, nb=nb, wv=wv, wb=wb):
                                      bk, bkb = bank("mm")
                                      acc_fm(bk, bkb, 512, lambda k: wv[:, k, hh * 128:(hh + 1) * 128],
                                             lambda k: hT[:, k, nb * 512:(nb + 1) * 512], 16, lambda k: [wb] + hreads(k, nb))
                                      return bk, bkb

                                  def tailA_fn(st_):
                                      return qk_norm_A(st_[0], st_[1], pcol[:, PC_KN:PC_KN + 1])

                                  def tailB_fn(pi, hh=hh, nb=nb, j=j):
                                      g = 2 * j + hh
                                      qk_norm_B(pi, kT[:, g, nb * 512:(nb + 1) * 512], [kTb[g][4 * nb + i] for i in range(4)], nb * 512,
                                                is_k_out=(None if latent else g))
                                  units.append((head_fn, tailA_fn, tailB_fn))
                          pipeline3(units)
                          if latent:
                              mod_step(1)
                      if not latent:
                          DMA("sp", nk_d.rearrange("(t p) c -> p t c", p=128), kout[:], osem, [koutb], [], is_out=True)
                      for j in range(2):
                          slot, wb = wload([(win_t(2560 + 256 * j), 0)])
                          wv = wview(slot, 0, 16, 256)
                          for t in range(ntile):
                              bk, bkb = bank("mm")
                              for k in range(16):
                                  MM(bk[:, 0:256], hT[:, k, t * 128:(t + 1) * 128], wv[:, k, :], k == 0, k == 15, [wb, hTb[t][k]], [bkb])
                              CPA(vall[:, t, j * 256:(j + 1) * 256], bk[:, 0:256], [bkb], [vb[t][j]])
                              if not latent:
                                  CPV(vout[:, t, j * 256:(j + 1) * 256], bk[:, 0:256], [bkb], [voutb])
                          if latent:
                              mod_step(1)
                      if not latent:
                          DMA("sp", nv_d.rearrange("(t p) c -> p t c", p=128), vout[:], osem, [voutb], [], is_out=True)
                      if latent:
                          DMA("pool", vall[:, 8:12, :], cv_d.rearrange("(c p) n -> p c n", p=128), cksem, [],
                              [vb[8 + c][jj] for c in range(4) for jj in range(2)], bar=True)
                          for c in range(4):
                              DMA("sp", cks[:], ck_d[c * 128:(c + 1) * 128, :], xsem[0], [], [ckb])
                              tb_, tbb = bank("tr")
                              for g in range(4):
                                  TRP(tb_[:, g * 128:(g + 1) * 128], cks[:, g * 128:(g + 1) * 128], ident, [ckb, cb], [tbb])
                              CPV(kT[:, :, 1024 + c * 128: 1024 + (c + 1) * 128], tb_[:].rearrange("p (g t) -> p g t", g=4),
                                  [tbb], [kTb[g][8 + c] for g in range(4)])
                      obanks = [(pbk[3], pbb[3]), (pbk[6], pbb[6])]
                      dbanks = [(pbk[4], pbb[4]), (pbk[5], pbb[5])]
                      ucount = [0]

                      def q_proj_chain(g, j):
                          if True:
                              slot, wb = wload([(win_t(512 * g + 256 * j), 0)])
                              wv = wview(slot, 0, 16, 256)
                              units = []
                              for hh in range(2):
                                  for nb in range(nblk):
                                      def head_fn(hh=hh, nb=nb, wv=wv, wb=wb):
                                          bk, bkb = bank("mm")
                                          acc_fm(bk, bkb, 512, lambda k: wv[:, k, hh * 128:(hh + 1) * 128],
                                                 lambda k: hT[:, k, nb * 512:(nb + 1) * 512], 16, lambda k: [wb] + hreads(k, nb))
                                          return bk, bkb

                                      def tailA_fn(st_):
                                          return qk_norm_A(st_[0], st_[1], pcol[:, PC_QN:PC_QN + 1])

                                      def tailB_fn(pi, hh=hh, nb=nb, j=j):
                                          hl = 2 * j + hh
                                          qk_norm_B(pi, qT[:, hl, nb * 512:(nb + 1) * 512], [qTb[hl][nb]], nb * 512)
                                      units.append((head_fn, tailA_fn, tailB_fn))
                              pipeline3(units)

                      def q_attn(g, j):
                          if True:
                              steps = []
                              for hh in range(2):
                                  hl = 2 * j + hh
                                  for qi, (q0, qn, kcs) in enumerate(seg["qblocks"]):
                                      un = ucount[0]
                                      ucount[0] += 1
                                      for i, kc in enumerate(kcs):
                                          steps.append((hl, q0, qn, i, kc, len(kcs), un))

                              def emit_st(idx):
                                  hl, q0, qn, i, kc, nkc, un = steps[idx]
                                  MM(pbk[idx % 3][:, 0:qn], kT[:, g, kc * 128:(kc + 1) * 128], qT[:, hl, q0:q0 + qn], True, True,
                                     [kTb[g][kc], qTb[hl][q0 // 512]], [pbb[idx % 3]])
                              for idx in range(min(2, len(steps))):
                                  emit_st(idx)
                              for idx in range(len(steps)):
                                  hl, q0, qn, i, kc, nkc, un = steps[idx]
                                  if idx + 2 < len(steps):
                                      emit_st(idx + 2)
                                  ob, obb = obanks[un % 2]
                                  db, dbb = dbanks[un % 2]
                                  p_ = pT[idx % 3]
                                  ACT(p_[:, 0:qn], pbk[idx % 3][:, 0:qn], AF.Exp, [pbb[idx % 3]], [pTb[idx % 3]], scale=ATT_SCALE)
                                  MM(ob[:, 0:qn], vall[:, kc, g * 128:(g + 1) * 128], p_[:, 0:qn], i == 0, i == nkc - 1,
                                     [vb[kc][g // 2], pTb[idx % 3]], [obb])
                                  MM(db[:, 0:qn], onesb[:], p_[:, 0:qn], i == 0, i == nkc - 1, [cb, pTb[idx % 3]], [dbb])
                                  if i == nkc - 1:
                                      head = 4 * g + hl
                                      rd_ = rdt[un % 2]
                                      K.op("dve", lambda db=db, qn=qn, rd_=rd_: nc.vector.reciprocal(out=rd_[:, 0:qn], in_=db[:, 0:qn]),
                                           [dbb], [rdb[un % 2]])
                                      TT(oaT[:, head, q0:q0 + qn], ob[:, 0:qn], rd_[:, 0:qn], ALU.mult, [obb, rdb[un % 2]],
                                         [oab[head][q0 // 512]])
                              if latent:
                                  mod_step(1)
                      qtiles = [(g, j) for g in range(4) for j in range(2)]
                      q_proj_chain(*qtiles[0])
                      for qi_, (g, j) in enumerate(qtiles):
                          if qi_ + 1 < len(qtiles):
                              q_proj_chain(*qtiles[qi_ + 1])
                          q_attn(g, j)
                K.barrier()

                with ExitStack() as st3:
                  if STOP >= 3:
                      def two(name, shape, dt):
                          return [sbt(st3, f"{name}{i}", shape, dt) for i in range(2)]
                      qrT = two("qrT", [128, 1024], BF16)
                      qf = two("qf", [128, 1024], BF16)
                      qbw = two("qbw", [128, 1024], BF16)
                      krT = two("krT", [128, 1024], BF16)
                      kzf = two("kzf", [128, 8, 128], BF16)
                      kzb = two("kzb", [128, 8, 128], BF16)
                      vr = two("vr", [128, 8, 256], BF16)
                      sgT = two("sgT", [128, 2, 1024], BF16)
                      DC4 = two("DC4", [128, 4, 128], F32)
                      XF4 = sbt(st3, "XF4", [128, 4, 128], F32)
                      XB4 = sbt(st3, "XB4", [128, 4, 128], F32)
                      tmA = sbt(st3, "tmA", [128, 128], F32)
                      tmB = sbt(st3, "tmB", [128, 128], F32)
                      Sfb = sbt(st3, "Sfb", [128, 8, 256], BF16)
                      Sbb = sbt(st3, "Sbb", [128, 8, 256], BF16)
                      S32 = [sbt(st3, f"S32_{i}", [128, 256], F32) for i in range(4)]
                      sc = [sbt(st3, f"sc{i}", [128, 512], BF16) for i in range(2)]
                      ont = [sbt(st3, f"ont{i}", [128, 256], BF16) for i in range(2)]
                      st6 = sbt(st3, "st6", [128, 2, 6], F32)
                      mv = sbt(st3, "mv", [128, 2, 4], F32)

                      def twob(n=None):
                          if n is None:
                              return [Buf(), Buf()]
                          return [[Buf() for _ in range(n)] for _ in range(2)]
                      qrb, qfb, qbb, krb = twob(2), twob(2), twob(2), twob(2)
                      kzfb, kzbb, dcb = twob(), twob(), twob()
                      vrb = twob(8)
                      sgb = [[[Buf() for _ in range(2)] for _ in range(2)] for _ in range(2)]
                      xfb, tmb = Buf(), Buf()
                      Sfbb = [Buf() for _ in range(8)]
                      Sbbb = [Buf() for _ in range(8)]
                      S32b = [Buf() for _ in range(4)]
                      scb = [Buf(), Buf()]
                      onb = [Buf(), Buf()]
                      stb_ = [Buf(), Buf()]
                      nchunk = ntok // 128

                      def proj_gen(h):
                          pp = h % 2
                          lgf = lg[:, h:h + 1]
                          lgb = lg[:, 8 + h:9 + h]
                          ACT(XF4[:], ctab[:, CT_XF:CT_XF + 128].unsqueeze(1).to_broadcast([128, 4, 128]), AF.Exp, [cb, mb], [xfb], scale=lgf)
                          ACT(XB4[:], ctab[:, CT_XB:CT_XB + 128].unsqueeze(1).to_broadcast([128, 4, 128]), AF.Exp, [cb, mb], [xfb], scale=lgb)
                          ACT(tmA[:], ctab[:, CT_DF:CT_DF + 128], AF.Exp, [cb, mb], [tmb], scale=lgf)
                          ACT(tmB[:], ctab[:, CT_DB:CT_DB + 128], AF.Exp, [cb, mb], [tmb], scale=lgb)
                          TT(tmA[:], tmA[:], ctab[:, CT_MF:CT_MF + 128], ALU.mult, [tmb, cb], [tmb])
                          TT(tmB[:], tmB[:], ctab[:, CT_MB:CT_MB + 128], ALU.mult, [tmb, cb], [tmb])
                          TT(tmA[:], tmA[:], tmB[:], ALU.add, [tmb], [tmb])
                          TT(tmA[:], tmA[:], ctab[:, CT_I2:CT_I2 + 128], ALU.add, [tmb, cb], [tmb])
                          CPV(DC4[pp][:], tmA[:].unsqueeze(1).to_broadcast([128, 4, 128]), [tmb], [dcb[pp]])
                          slot1, wb1 = wload([(win_t(3072 + 128 * h), 0),
                                              (win_t(4096 + 128 * h), 2048)])
                          wq = wview(slot1, 0, 16, 128)
                          wk = wview(slot1, 2048, 16, 128)
                          for nb in range(nblk):
                              blk = slice(nb * 512, (nb + 1) * 512)
                              bk, bkb = bank("mm2")
                              acc_fm(bk, bkb, 512, lambda k: wq[:, k, :], lambda k: hT[:, k, blk], 16, lambda k: [wb1] + hreads(k, nb))
                              CPA(qrT[pp][:, blk], bk[:], [bkb], [qrb[pp][nb]])
                              TT(qf[pp][:, blk], bk[:], XF4[:].rearrange("p a b -> p (a b)"), ALU.mult, [bkb, xfb], [qfb[pp][nb]])
                              TT(qbw[pp][:, blk], bk[:], XB4[:].rearrange("p a b -> p (a b)"), ALU.mult, [bkb, xfb], [qbb[pp][nb]])
                              yield
                              bk, bkb = bank("mm2")
                              acc_fm(bk, bkb, 512, lambda k: wk[:, k, :], lambda k: hT[:, k, blk], 16, lambda k: [wb1] + hreads(k, nb))
                              ACT(krT[pp][:, blk], bk[:], AF.Identity, [bkb], [krb[pp][nb]], scale=128 ** -0.5)
                              yield
                          if latent:
                              mod_step(1, (pbk[4], pbb[4]))
                          slot2, wb2 = wload([(win_t(5120 + 256 * h), 0)])
                          wv = wview(slot2, 0, 16, 256)
                          for t in range(ntile):
                              bk, bkb = bank("mm2")
                              for k in range(16):
                                  MM(bk[:, 0:256], hT[:, k, t * 128:(t + 1) * 128], wv[:, k, :], k == 0, k == 15, [wb2, hTb[t][k]], [bkb])
                              if t % 2 == 0:
                                  CPA(vr[pp][:, t, :], bk[:, 0:256], [bkb], [vrb[pp][t]])
                              else:
                                  CPV(vr[pp][:, t, :], bk[:, 0:256], [bkb], [vrb[pp][t]])
                              yield
                          if latent:
                              mod_step(1, (pbk[4], pbb[4]))
                          slot3, wb3 = wload([(win_t(7168 + 256 * h), 0)])
                          wgv = wview(slot3, 0, 16, 256)
                          for e in range(2):
                              for nb in range(nblk):
                                  blk = slice(nb * 512, (nb + 1) * 512)
                                  bk, bkb = bank("mm2")
                                  acc_fm(bk, bkb, 512, lambda k: wgv[:, k, e * 128:(e + 1) * 128], lambda k: hT[:, k, blk], 16,
                                         lambda k: [wb3] + hreads(k, nb))
                                  ACT(sgT[pp][:, e, blk], bk[:], AF.Silu, [bkb], [sgb[pp][e][nb]])
                                  yield
                          if latent:
                              mod_step(1, (pbk[4], pbb[4]))
                          tb_, tbb = bank("tr")
                          tpv = tb_[:].bitcast(BF16)
                          for c in range(nchunk):
                              TRP(tpv[:, c * 128:(c + 1) * 128], krT[pp][:, c * 128:(c + 1) * 128], identb[:], [krb[pp][c // 4], cb], [tbb])
                          TS(kzf[pp][:, 0:nchunk, :], tpv[:, 0:nchunk * 128].rearrange("p (c d) -> p c d", c=nchunk), zt[:, h:h + 1], None,
                             ALU.mult, None, [tbb, mb], [kzfb[pp]])
                          ACT(kzb[pp][:, 0:nchunk, :], tpv[:, 0:nchunk * 128].rearrange("p (c d) -> p c d", c=nchunk), AF.Identity, [tbb, mb],
                              [kzbb[pp]], scale=zt[:, 8 + h:9 + h])
                          yield

                      def chain_gen(h):
                          pp = h % 2
                          for si, (c0, ncs) in enumerate(seg["seqs"]):
                              Sf, Sfx = S32[2 * si], S32b[2 * si]
                              Sb_, Sbx = S32[2 * si + 1], S32b[2 * si + 1]
                              if latent:
                                  DMA("sp", Sf[:], sf_d[h, :, :], ssem[2 * si], [], [Sfx])
                                  DMA("sp", Sb_[:], sb_d[h, :, :], ssem[2 * si + 1], [], [Sbx])
                              else:
                                  K.op("dve", lambda Sf=Sf: nc.vector.memset(Sf[:], 0.0), [], [Sfx])
                                  K.op("dve", lambda Sb_=Sb_: nc.vector.memset(Sb_[:], 0.0), [], [Sbx])
                              CPA(Sfb[:, c0, :], Sf[:], [Sfx], [Sfbb[c0]])
                              CPA(Sbb[:, c0 + ncs - 1, :], Sb_[:], [Sbx], [Sbbb[c0 + ncs - 1]])
                              last_f = ncs if not latent else ncs - 1
                              for i in range(last_f):
                                  c = c0 + i
                                  cb_ = c0 + ncs - 1 - i
                                  ab, abb = pbk[4], pbb[4]
                                  MM(ab[:, 0:256], kzf[pp][:, c, :], vr[pp][:, c, :], True, True, [kzfb[pp], vrb[pp][c]], [abb])
                                  MM(ab[:, 256:512], kzb[pp][:, cb_, :], vr[pp][:, cb_, :], True, True, [kzbb[pp], vrb[pp][cb_]], [abb])
                                  STT(Sf[:], Sf[:], gch[:, h:h + 1], ab[:, 0:256], ALU.mult, ALU.add, [Sfx, abb, mb], [Sfx])
                                  STT(Sb_[:], Sb_[:], gch[:, 8 + h:9 + h], ab[:, 256:512], ALU.mult, ALU.add, [Sbx, abb, mb], [Sbx])
                                  if i + 1 < ncs:
                                      CPA(Sfb[:, c + 1, :], Sf[:], [Sfx], [Sfbb[c + 1]])
                                      CPA(Sbb[:, cb_ - 1, :], Sb_[:], [Sbx], [Sbbb[cb_ - 1]])
                                  yield
                              if not latent:
                                  DMA("sp", nsf_d[si, h, :, :], Sf[:], ssem[2 * si], [Sfx], [], is_out=True)
                                  DMA("sp", nsb_d[si, h, :, :], Sb_[:], ssem[2 * si + 1], [Sbx], [], is_out=True)
                          for c4 in range(nchunk // 4):
                              bk, bkb = bank("bo")
                              for i in range(4):
                                  c = c4 * 4 + i
                                  MM(bk[:, i * 128:(i + 1) * 128], krT[pp][:, c * 128:(c + 1) * 128], qrT[pp][:, c * 128:(c + 1) * 128], True, True,
                                     [krb[pp][c // 4], qrb[pp][c // 4]], [bkb])
                              TT(sc[c4][:], bk[:], DC4[pp][:].rearrange("p a b -> p (a b)"), ALU.mult, [bkb, dcb[pp]], [scb[c4]])
                          yield

                          def head_fn(cp):
                              bo, bob = bank("bo")
                              for e2 in range(2):
                                  c = cp * 2 + e2
                                  c4, i = c // 4, c % 4
                                  o_ = bo[:, e2 * 256:(e2 + 1) * 256]
                                  MM(o_, sc[c4][:, i * 128:(i + 1) * 128], vr[pp][:, c, :], True, False, [scb[c4], vrb[pp][c]], [bob])
                                  MM(o_, qf[pp][:, c * 128:(c + 1) * 128], Sfb[:, c, :], False, False, [qfb[pp][c // 4], Sfbb[c]], [bob])
                                  MM(o_, qbw[pp][:, c * 128:(c + 1) * 128], Sbb[:, c, :], False, True, [qbb[pp][c // 4], Sbbb[c]], [bob])
                              return bo, bob

                          def tail_fn(st_, cp):
                              bo, bob = st_
                              for e2 in range(2):
                                  o_ = bo[:, e2 * 256:(e2 + 1) * 256]
                                  K.op("dve", lambda o_=o_, e2=e2: nc.vector.bn_stats(out=st6[:, e2, :], in_=o_), [bob], [stb_[0]])
                                  K.op("dve", lambda e2=e2: nc.vector.bn_aggr(out=mv[:, e2, 0:2], in_=st6[:, e2, :]), [stb_[0]], [stb_[0]])
                              ACT(mv[:, :, 2:3], mv[:, :, 1:2], AF.Ln, [stb_[0]], [stb_[0]], bias=EPS)
                              ACT(mv[:, :, 2:3], mv[:, :, 2:3], AF.Exp, [stb_[0]], [stb_[0]], scale=-0.5)
                              for e2 in range(2):
                                  o_ = bo[:, e2 * 256:(e2 + 1) * 256]
                                  TS(ont[e2][:], o_, mv[:, e2, 0:1], mv[:, e2, 2:3], ALU.subtract, ALU.mult, [bob, stb_[0]], [onb[e2]])
                              tb_, tbb = bank("tr")
                              tpv = tb_[:].bitcast(BF16)
                              for e2 in range(2):
                                  for e in range(2):
                                      TRP(tpv[:, (2 * e2 + e) * 128:(2 * e2 + e + 1) * 128], ont[e2][:, e * 128:(e + 1) * 128], identb[:],
                                          [onb[e2], cb], [tbb])
                              for e2 in range(2):
                                  c = cp * 2 + e2
                                  for e in range(2):
                                      STT(orT[:, 2 * h + e, c * 128:(c + 1) * 128], tpv[:, (2 * e2 + e) * 128:(2 * e2 + e + 1) * 128],
                                          pcol[:, PC_RN + 2 * h + e:PC_RN + 2 * h + e + 1], sgT[pp][:, e, c * 128:(c + 1) * 128],
                                          ALU.mult, ALU.mult, [tbb, cb, sgb[pp][e][c // 4]], [orb[2 * h + e][c]])
                          ncp = nchunk // 2
                          cur = head_fn(0)
                          for cp in range(ncp):
                              nxt_ = head_fn(cp + 1) if cp + 1 < ncp else None
                              yield
                              tail_fn(cur, cp)
                              yield
                              cur = nxt_

                      g0 = proj_gen(0)
                      for _ in g0:
                          pass
                      for h in range(8):
                          nxt = proj_gen(h + 1) if h + 1 < 8 else None
                          for _ in chain_gen(h):
                              if nxt is not None:
                                  if next(nxt, "done") == "done":
                                      nxt = None
                          if nxt is not None:
                              for _ in nxt:
                                  pass
                K.barrier()

            with ExitStack() as st4:
              if STOP >= 4:
                  xT = sbt(st4, "xT", [128, 16, 512], F32)
                  hB = sbt(st4, "hB", [128, 16, 512], BF16)
                  mT = sbt(st4, "mT", [128, 16, 512], BF16)
                  iot = [sbt(st4, f"iot{i}", [128, D], F32) for i in range(2)]
                  tA = sbt(st4, "tA", [128, 2, 512], F32)
                  sgu = [sbt(st4, f"sgu{i}", [128, 2, 512], BF16) for i in range(2)]
                  sga, sgr = sgu[0], sgu[1]
                  rs4 = sbt(st4, "rs4", [128, 512], F32)
                  tmpf2 = [sbt(st4, f"tmpf{i}", [128, 512], F32) for i in range(2)]
                  tB = tmpf2[0]
                  sq4 = [sbt(st4, f"sq4_{i}", [128, 512], BF16) for i in range(2)]
                  xTb = [Buf() for _ in range(16)]
                  hBb = [Buf() for _ in range(16)]
                  mTb = [Buf() for _ in range(16)]
                  iob = [Buf(), Buf()]
                  rsb4 = Buf()
                  tmpb2 = [Buf(), Buf()]
                  tBb = tmpb2[0]
                  tAb = [Buf(), Buf()]
                  sgub = [Buf(), Buf()]
                  sgab, sgrb = sgub[0], sgub[1]
                  sq4b = [Buf(), Buf()]
                  sqi = [0]
                  ioi = [0]

                  def sumsq_sq(j):
                      s = sqi[0] % 2
                      sqi[0] += 1
                      ACT(sq4[s][:], xT[:, j, :], AF.Square, [xTb[j]], [sq4b[s]])
                      return s

                  def sumsq_mm(s, first, last, ssbk, ssbb):
                      MM(ssbk[:], onesb[:], sq4[s][:], first, last, [sq4b[s], cb], [ssbb])

                  def sumsq_chunk(j, first, last, ssbk, ssbb):
                      sumsq_mm(sumsq_sq(j), first, last, ssbk, ssbb)

                  mod_flush()
                  for tb in range(nblk):
                      t0 = tok0 + tb * 512
                      l0 = tb * 512
                      for t in range(4):
                          s = ioi[0] % 2
                          ioi[0] += 1
                          DMA("sp", iot[s][:], x_d[t0 + t * 128: t0 + (t + 1) * 128, :], xsem[s], [], [iob[s]])
                          for k4 in range(4):
                              tb_, tbb = bank("tr")
                              for kk in range(4):
                                  k = k4 * 4 + kk
                                  TRP(tb_[:, kk * 128:(kk + 1) * 128], iot[s][:, k * 128:(k + 1) * 128], ident, [iob[s], cb], [tbb])
                              dst = xT[:, k4 * 4:(k4 + 1) * 4, t * 128:(t + 1) * 128]
                              src = tb_[:].rearrange("p (k t) -> p k t", k=4)
                              if k4 % 2 == 0:
                                  CPA(dst, src, [tbb], [xTb[k4 * 4 + i] for i in range(4)])
                              else:
                                  CPV(dst, src, [tbb], [xTb[k4 * 4 + i] for i in range(4)])
                      ssbk, ssbb = pbk[5], pbb[5]
                      for j in range(16):
                          sumsq_chunk(j, j == 0, j == 15, ssbk, ssbb)
                      rstd_from(rs4[:], ssbk[:], D, [ssbb], [rsb4])
                      for j in range(16):
                          STT(tmpf2[j % 2][:], xT[:, j, :], col(gsa, j, w), rs4[:], ALU.mult, ALU.mult, [xTb[j], mb, rsb4], [tmpb2[j % 2]])
                          ACT(hB[:, j, :], tmpf2[j % 2][:], AF.Identity, [tmpb2[j % 2], mb], [hBb[j]], bias=col(modt, j, w))
                      for cg in range(8):
                          slot, wb = wload([(win_t(9216 + 256 * cg), 0)])
                          wv = wview(slot, 0, 16, 256)
                          for e in range(2):
                              bk, bkb = bank("mm")
                              acc_fm(bk, bkb, 512, lambda k: wv[:, k, e * 128:(e + 1) * 128], lambda k: hB[:, k, :], 16, lambda k: [wb, hBb[k]])
                              ACT(sga[:, e, :], bk[:], AF.Sigmoid, [bkb], [sgab])
                          slot, wb = wload([(win_t(11264 + 256 * cg), 0)])
                          wv = wview(slot, 0, 16, 256)
                          for e in range(2):
                              bk, bkb = bank("mm")
                              acc_fm(bk, bkb, 512, lambda k: wv[:, k, e * 128:(e + 1) * 128], lambda k: hB[:, k, :], 16, lambda k: [wb, hBb[k]])
                              ACT(sgr[:, e, :], bk[:], AF.Sigmoid, [bkb], [sgrb])
                          slot, wb = wload([(wt("w_ba", cg), 0)])
                          wv = wview(slot, 0, 16, 256)
                          for e in range(2):
                              bk, bkb = bank("mm")
                              acc_fm(bk, bkb, 512, lambda k: wv[:, k, e * 128:(e + 1) * 128], lambda k: oaT[:, k, l0:l0 + 512], 16,
                                     lambda k: [wb, oab[k][tb]])
                              TT(tA[:, e, :], bk[:], sga[:, e, :], ALU.mult, [bkb, sgab], [tAb[e]])
                          slot, wb = wload([(wt("w_br", cg), 0)])
                          wv = wview(slot, 0, 16, 256)
                          for e in range(2):
                              bk, bkb = bank("mm")
                              acc_fm(bk, bkb, 512, lambda k: wv[:, k, e * 128:(e + 1) * 128], lambda k: orT[:, k, l0:l0 + 512], 16,
                                     lambda k: [wb] + [orb[k][4 * tb + i] for i in range(4)])
                              TT(tB[:], bk[:], sgr[:, e, :], ALU.mult, [bkb, sgrb], [tBb])
                              TT(mT[:, 2 * cg + e, :], tA[:, e, :], tB[:], ALU.add, [tAb[e], tBb], [mTb[2 * cg + e]])
                      pend = None
                      for cg in range(8):
                          slot, wb = wload([(wt("w_out", cg), 0)])
                          wv = wview(slot, 0, 16, 256)
                          for e in range(2):
                              j = 2 * cg + e
                              bk, bkb = bank("mm")
                              acc_fm(bk, bkb, 512, lambda k: wv[:, k, e * 128:(e + 1) * 128], lambda k: mT[:, k, :], 16, lambda k: [wb, mTb[k]])
                              if pend is not None:
                                  sumsq_mm(pend[0], pend[1] == 0, False, ssbk, ssbb)
                              STT(xT[:, j, :], bk[:], col(modt, 32 + j, w), xT[:, j, :], ALU.mult, ALU.add, [bkb, mb2, xTb[j]], [xTb[j]])
                              pend = (sumsq_sq(j), j)
                      sumsq_mm(pend[0], False, True, ssbk, ssbb)
                      rstd_from(rs4[:], ssbk[:], D, [ssbb], [rsb4])
                      for j in range(16):
                          STT(tmpf2[j % 2][:], xT[:, j, :], col(gsf, j, w), rs4[:], ALU.mult, ALU.mult, [xTb[j], mb2, rsb4], [tmpb2[j % 2]])
                          ACT(hB[:, j, :], tmpf2[j % 2][:], AF.Identity, [tmpb2[j % 2], mb2], [hBb[j]], bias=col(modt, 48 + j, w))
                      for gi, (tl0, ntl) in enumerate(FFN_GROUPS):
                          nch = 2 * ntl
                          for ti in range(ntl):
                              c0 = 256 * (tl0 + ti)
                              slot, wb = wload([(wt("w_g", tl0 + ti), 0)])
                              wv = wview(slot, 0, 16, 256)
                              su = sgu[ti % 2]
                              for e in range(2):
                                  bk, bkb = bank("mm")
                                  acc_fm(bk, bkb, 512, lambda k: wv[:, k, e * 128:(e + 1) * 128], lambda k: hB[:, k, :], 16, lambda k: [wb, hBb[k]])
                                  ACT(su[:, e, :], bk[:], AF.Silu, [bkb], [sgub[ti % 2]])
                              slot, wb = wload([(wt("w_u", tl0 + ti), 0)])
                              wv = wview(slot, 0, 16, 256)
                              for e in range(2):
                                  ci = 2 * ti + e
                                  bk, bkb = bank("mm")
                                  acc_fm(bk, bkb, 512, lambda k: wv[:, k, e * 128:(e + 1) * 128], lambda k: hB[:, k, :], 16, lambda k: [wb, hBb[k]])
                                  TT(mT[:, ci, :], bk[:], su[:, e, :], ALU.mult, [bkb, sgub[ti % 2]], [mTb[ci]])
                          r0 = 256 * tl0
                          lastg = gi == len(FFN_GROUPS) - 1
                          pend = None
                          for cgo in range(8):
                              slot, wb = wload([(wt("w_d", 8 * gi + cgo), 0)])
                              wv = wview(slot, 0, nch, 256)
                              for e in range(2):
                                  j = 2 * cgo + e
                                  bk, bkb = bank("mm")
                                  acc_fm(bk, bkb, 512, lambda k: wv[:, k, e * 128:(e + 1) * 128], lambda k: mT[:, k, :], nch, lambda k: [wb, mTb[k]])
                                  if pend is not None:
                                      sumsq_mm(pend[0], pend[1] == 0, False, ssbk, ssbb)
                                  STT(xT[:, j, :], bk[:], col(modt, 80 + j, w), xT[:, j, :], ALU.mult, ALU.add, [bkb, mb2, xTb[j]], [xTb[j]])
                                  if lastg:
                                      pend = (sumsq_sq(j), j)
                          if lastg:
                              sumsq_mm(pend[0], False, True, ssbk, ssbb)
                      rstd_from(rs4[:], ssbk[:], D, [ssbb], [rsb4])
                      for j in range(16):
                          STT(xT[:, j, :], xT[:, j, :], pcol[:, PC_FN + j:PC_FN + j + 1], rs4[:], ALU.mult, ALU.mult, [xTb[j], cb, rsb4], [xTb[j]])
                      for t in range(4):
                          s = ioi[0] % 2
                          ioi[0] += 1
                          for j4 in range(4):
                              tb_, tbb = bank("tr")
                              for jj in range(4):
                                  j = j4 * 4 + jj
                                  TRP(tb_[:, jj * 128:(jj + 1) * 128], xT[:, j, t * 128:(t + 1) * 128], ident, [xTb[j], cb], [tbb])
                              if j4 % 2 == 0:
                                  CPA(iot[s][:, j4 * 512:(j4 + 1) * 512], tb_[:], [tbb], [iob[s]])
                              else:
                                  CPV(iot[s][:, j4 * 512:(j4 + 1) * 512], tb_[:], [tbb], [iob[s]])
                          DMA("sp", y_d[t0 + t * 128: t0 + (t + 1) * 128, :], iot[s][:], xsem[s], [iob[s]], [], is_out=True)
            K.barrier()
        K.finish()
        print(f"[kernel] ops={K.nops} waits={K.nwait}", flush=True)
    return nc


def _const_table():
    ct = np.zeros((128, NCT), np.float32)
    p = np.arange(128)
    ct[:, CT_ID:CT_ID + 128] = np.eye(128, dtype=np.float32)
    rm = np.zeros((128, 128), np.float32)
    for m in range(128):
        if (m % 64) < 32:
            rm[m + 32, m] = -1.0
        else:
            rm[m - 32, m] = 1.0
    ct[:, CT_RM:CT_RM + 128] = rm
    j = p[:, None].astype(np.float32)
    i = p[None, :].astype(np.float32)
    ct[:, CT_DF:CT_DF + 128] = np.maximum(i - j, 0)
    ct[:, CT_MF:CT_MF + 128] = (i > j)
    ct[:, CT_DB:CT_DB + 128] = np.maximum(j - i, 0)
    ct[:, CT_MB:CT_MB + 128] = (j > i)
    ct[:, CT_I2:CT_I2 + 128] = 2.0 * np.eye(128)
    ct[:, CT_XF:CT_XF + 128] = np.broadcast_to(i + 1.0, (128, 128))
    ct[:, CT_XB:CT_XB + 128] = np.broadcast_to(128.0 - i, (128, 128))
    ct[:, CT_ZF] = 127.0 - p
    ct[:, CT_ZB] = p
    tok = np.arange(1024)
    row = (tok // 64).astype(np.float32)
    colp = (tok % 64).astype(np.float32)
    inv_freq = (np.float32(10000.0) ** (-np.arange(32, dtype=np.float32) / np.float32(32))).astype(np.float32)
    ang = np.zeros((128, 1024), np.float32)
    for d in range(128):
        pos = row if d < 64 else colp
        ang[d] = pos * inv_freq[d % 32]
    ct[:, CT_COS:CT_COS + 1024] = np.cos(ang)
    ct[:, CT_SIN:CT_SIN + 1024] = np.sin(ang)
    return ct


_NC_CACHE = {}


def kernel(x_prompt, x_sample, cache_attn_k, cache_attn_v, state_ret_fwd, state_ret_bwd, c, c_ctx,
           norm_attn, norm_ffn, w_mod, b_mod, w_in, q_norm, k_norm, ret_decay_fwd, ret_decay_bwd,
           ret_norm, w_branch_attn, w_branch_ret, w_out, w_ffn_gate, w_ffn_up, w_ffn_down, final_norm):
    f = lambda a: np.ascontiguousarray(np.asarray(a, dtype=np.float32))
    x_prompt, x_sample = f(x_prompt), f(x_sample)
    ctab = _const_table()

    def cols(v):
        return np.asarray(v, np.float32).reshape(-1, 128).T

    shared = {
        "ctab": ctab,
        "w_mod": pack_weight("w_mod", w_mod[0]), "w_in": pack_weight("w_in", w_in[0]),
        "w_ba": pack_weight("w_ba", w_branch_attn[0]), "w_br": pack_weight("w_br", w_branch_ret[0]),
        "w_out": pack_weight("w_out", w_out[0]), "w_g": pack_weight("w_g", w_ffn_gate[0]),
        "w_u": pack_weight("w_u", w_ffn_up[0]), "w_d": pack_weight("w_d", w_ffn_down[0]),
    }
    in_maps = []
    for core in range(8):
        pc = np.zeros((128, NPC), np.float32)
        pc[:, PC_BMOD:PC_BMOD + 96] = cols(b_mod[0])
        pc[:, PC_NA:PC_NA + 16] = cols(norm_attn[0])
        pc[:, PC_NF:PC_NF + 16] = cols(norm_ffn[0])
        pc[:, PC_FN:PC_FN + 16] = cols(final_norm)
        pc[:, PC_RN:PC_RN + 16] = cols(ret_norm[0])
        pc[:, PC_C:PC_C + 16] = cols(c[core])
        pc[:, PC_CC:PC_CC + 16] = cols(c_ctx)
        pc[:, PC_QN] = np.asarray(q_norm[0], np.float32)
        pc[:, PC_KN] = np.asarray(k_norm[0], np.float32)
        pc[:, PC_DF:PC_DF + 8] = np.asarray(ret_decay_fwd[0], np.float32)[None, :]
        pc[:, PC_DB:PC_DB + 8] = np.asarray(ret_decay_bwd[0], np.float32)[None, :]
        m = dict(shared)
        m["x"] = np.ascontiguousarray(np.concatenate([x_sample[core], x_prompt[2 * core], x_prompt[2 * core + 1]], axis=0))
        m["ck"] = f(cache_attn_k[core, 0]).reshape(512, 512)
        m["cv"] = f(cache_attn_v[core, 0]).reshape(512, 512)
        m["sf0"] = f(state_ret_fwd[core, 0])
        m["sb0"] = f(state_ret_bwd[core, 0])
        m["pcols"] = pc
        in_maps.append(m)

    if "nc" not in _NC_CACHE:
        _NC_CACHE["nc"] = build()
    nc = _NC_CACHE["nc"]
    res = run_bass_kernel_spmd(nc, in_maps, core_ids=list(range(8)))
    y_prompt = np.zeros((16, 256, D), np.float32)
    y_sample = np.zeros((8, 1024, D), np.float32)
    new_k = np.zeros((16, 1, 256, 4, 128), np.float32)
    new_v = np.zeros((16, 1, 256, 4, 128), np.float32)
    new_sf = np.zeros((16, 1, 8, 128, 256), np.float32)
    new_sb = np.zeros((16, 1, 8, 128, 256), np.float32)
    for core in range(8):
        r = res.results[core]
        y = r["y"]
        y_sample[core] = y[0:1024]
        y_prompt[2 * core] = y[1024:1280]
        y_prompt[2 * core + 1] = y[1280:1536]
        new_k[2 * core:2 * core + 2, 0] = r["nk"].reshape(2, 256, 4, 128)
        new_v[2 * core:2 * core + 2, 0] = r["nv"].reshape(2, 256, 4, 128)
        new_sf[2 * core:2 * core + 2, 0] = r["nsf"]
        new_sb[2 * core:2 * core + 2, 0] = r["nsb"]
    return (y_prompt, y_sample, new_k, new_v, new_sf, new_sb)
```

```python
import math
import bisect
from contextlib import ExitStack

import numpy as np
import concourse.bass as bass
import concourse.mybir as mybir
from concourse.bass_utils import run_bass_kernel_spmd

F32 = mybir.dt.float32
BF16 = mybir.dt.bfloat16
AF = mybir.ActivationFunctionType
ALU = mybir.AluOpType

D = 2048
DFF = 5632
NIN = 13312
EPS = 1e-6
NTOK = 1536
NSLOT = 4
STOP = 99
SLOTE = 4096

PC_BMOD, PC_NA, PC_NF, PC_FN, PC_RN, PC_C, PC_CC, PC_QN, PC_KN, PC_DF, PC_DB, NPC = 0, 96, 112, 128, 144, 160, 176, 192, 193, 194, 202, 210
CT_ID, CT_RM, CT_DF, CT_MF, CT_DB, CT_MB, CT_I2, CT_XF, CT_XB, CT_ZF, CT_ZB, CT_COS, CT_SIN, NCT = (
    0, 128, 256, 384, 512, 640, 768, 896, 1024, 1152, 1153, 1154, 2178, 3202)


FFN_GROUPS = [(0, 5), (5, 5), (10, 4), (14, 4), (18, 4)]


def _w_in_tiles():
    t = [(0, 16, 256 * i, 256) for i in range(12)]
    t += [(0, 16, 3072 + 128 * i, 128) for i in range(16)]
    t += [(0, 16, 5120 + 256 * i, 256) for i in range(32)]
    return t


def w_in_index(c0):
    if c0 < 3072:
        return c0 // 256
    if c0 < 5120:
        return 12 + (c0 - 3072) // 128
    return 28 + (c0 - 5120) // 256


TILES = {
    "w_mod": [(0, 16, 256 * i, 256) for i in range(48)],
    "w_in": _w_in_tiles(),
    "w_ba": [(0, 16, 256 * i, 256) for i in range(8)],
    "w_br": [(0, 16, 256 * i, 256) for i in range(8)],
    "w_out": [(0, 16, 256 * i, 256) for i in range(8)],
    "w_g": [(0, 16, 256 * i, 256) for i in range(22)],
    "w_u": [(0, 16, 256 * i, 256) for i in range(22)],
    "w_d": [(256 * tl0, 2 * ntl, 256 * cgo, 256) for (tl0, ntl) in FFN_GROUPS for cgo in range(8)],
}


def tile_offsets(name):
    offs, o = [], 0
    for (_, nk, _, nc_) in TILES[name]:
        offs.append(o)
        o += 128 * nk * nc_
    return offs, o


def pack_weight(name, W):
    W = np.asarray(W, dtype=np.float32)
    offs, total = tile_offsets(name)
    out = np.empty((total,), np.float32)
    for (r0, nk, c0, nc_), o in zip(TILES[name], offs):
        blk = W[r0:r0 + nk * 128, c0:c0 + nc_].reshape(nk, 128, nc_).transpose(1, 0, 2)
        out[o:o + 128 * nk * nc_] = blk.reshape(-1)
    return out


class Buf:
    __slots__ = ("name", "last_w", "readers", "excl", "by_eng")

    def __init__(self, name="", excl=False):
        self.name = name
        self.last_w = None
        self.readers = []
        self.excl = excl
        self.by_eng = {}


class DSem:
    __slots__ = ("h", "n", "name", "group", "exempt", "last")

    def __init__(self, h, name, group, exempt):
        self.h = h
        self.n = 0
        self.name = name
        self.group = group
        self.exempt = exempt
        self.last = None


class Op:
    __slots__ = ("eng", "fn", "deps", "signal", "cnt", "dsem", "dval", "is_dma", "epoch", "idx", "cdep")

    def __init__(self, eng, fn, epoch, idx):
        self.eng = eng
        self.fn = fn
        self.deps = []
        self.cdep = {}
        self.signal = False
        self.cnt = 0
        self.dsem = None
        self.dval = 0
        self.is_dma = False
        self.epoch = epoch
        self.idx = idx


class SigRef:
    __slots__ = ("eng", "cnt", "idx", "is_dma", "epoch", "signal")

    def __init__(self, eng, cnt, idx):
        self.eng = eng
        self.cnt = cnt
        self.idx = idx
        self.is_dma = False
        self.epoch = -1
        self.signal = True


class Kern:
    ENGS = ("pe", "act", "dve", "pool", "sp")

    def __init__(self, nc, stack):
        self.nc = nc
        self.stack = stack
        self.h = {"pe": nc.tensor, "act": nc.scalar, "dve": nc.vector, "pool": nc.gpsimd, "sp": nc.sync}
        self.sem = {e: stack.enter_context(nc.semaphore("s_" + e)) for e in self.ENGS}
        self.pending = {e: [] for e in self.ENGS}
        self.sigcnt = {e: 0 for e in self.ENGS}
        self.known = {e: {} for e in self.ENGS}
        self.lastop = {e: None for e in self.ENGS}
        self.dsems = []
        self.out_dmas = []
        self.epoch = 0
        self.finals = {}
        self.bar_deps = []
        self.bar_pending = set()
        self.nops = 0
        self.nwait = 0
        self.siglist = {e: ([], []) for e in self.ENGS}

    def dsem(self, name, group=False, exempt=False):
        d = DSem(self.stack.enter_context(self.nc.semaphore("d_" + name)), name, group, exempt)
        self.dsems.append(d)
        return d

    def _adddep(self, op, d):
        if d is op:
            return
        if (not d.is_dma) and d.epoch != op.epoch:
            idxs, cnts = self.siglist[d.eng]
            p = bisect.bisect_left(idxs, d.idx)
            if p >= len(idxs):
                return
            d = SigRef(d.eng, cnts[p], idxs[p])
        if d.is_dma:
            if op.is_dma and d.dsem is op.dsem and d.dsem.group:
                return
            if d not in op.deps:
                op.deps.append(d)
            return
        if op.eng == "pe" and d.eng == "pe":
            return
        cur = op.cdep.get(d.eng)
        if cur is None or cur.idx < d.idx:
            op.cdep[d.eng] = d

    def _track(self, op, reads, writes, bar):
        if op.eng in self.bar_pending or bar:
            self.bar_pending.discard(op.eng)
            for d in self.bar_deps:
                self._adddep(op, d)
        for b in reads:
            if b.last_w is not None:
                self._adddep(op, b.last_w)
        for b in writes:
            if b.last_w is not None:
                self._adddep(op, b.last_w)
            for r in b.readers:
                self._adddep(op, r)
        for b in list(reads) + list(writes):
            if b.excl:
                for e, o in b.by_eng.items():
                    if e != op.eng:
                        self._adddep(op, o)
                b.by_eng[op.eng] = op
        for b in reads:
            b.readers.append(op)
        for b in writes:
            b.last_w = op
            b.readers = []
        for d in op.cdep.values():
            op.deps.append(d)
            if d.epoch == op.epoch:
                d.signal = True

    def op(self, eng, fn, reads=(), writes=()):
        o = Op(eng, fn, self.epoch, self.nops)
        self.nops += 1
        self._track(o, reads, writes, False)
        self.pending[eng].append(o)
        self.lastop[eng] = o
        return o

    def dma(self, queue, fn, dsem, reads=(), writes=(), is_out=False, bar=False):
        o = Op(queue, fn, self.epoch, self.nops)
        o.is_dma = True
        self.nops += 1
        o.dsem = dsem
        dsem.n += 1
        o.dval = 16 * dsem.n
        dsem.last = o
        self._track(o, reads, writes, bar)
        self.pending[queue].append(o)
        if is_out:
            self.out_dmas.append(o)
        return o

    def flush(self):
        for e in self.ENGS:
            for o in self.pending[e]:
                if (not o.is_dma) and o.signal:
                    self.sigcnt[e] += 1
                    o.cnt = self.sigcnt[e]
                    self.siglist[e][0].append(o.idx)
                    self.siglist[e][1].append(o.cnt)
        for e in self.ENGS:
            h = self.h[e]
            known = self.known[e]
            for o in self.pending[e]:
                need = {}
                for d in o.deps:
                    if d.is_dma:
                        key, val = d.dsem.h, (16 * d.dsem.n if d.dsem.group else d.dval)
                    else:
                        key, val = self.sem[d.eng], d.cnt
                    if need.get(key, 0) < val:
                        need[key] = val
                for key, val in need.items():
                    if known.get(key, 0) < val:
                        h.wait_ge(key, val)
                        known[key] = val
                        self.nwait += 1
                ins = o.fn()
                if o.is_dma:
                    ins.then_inc(o.dsem.h, 16)
                elif o.signal:
                    ins.then_inc(self.sem[e], 1)
            self.pending[e] = []

    def barrier(self):
        fin = {}
        for e in ("pe", "act", "dve", "pool"):
            o = self.lastop[e]
            if o is not None and not o.is_dma and o.epoch == self.epoch:
                o.signal = True
                fin[e] = o
        self.finals[self.epoch] = fin
        deps = list(fin.values())
        for d in self.dsems:
            if not d.exempt and d.last is not None:
                deps.append(d.last)
        self.flush()
        self.epoch += 1
        self.bar_deps = deps
        self.bar_pending = {"pe", "act", "dve", "sp"}

    def finish(self):
        self.flush()
        h = self.h["sp"]
        fin = {}
        for o in self.out_dmas:
            if fin.get(o.dsem.h, 0) < o.dval:
                fin[o.dsem.h] = o.dval
        for key, val in fin.items():
            h.wait_ge(key, val)


def build():
    nc = bass.Bass("TRN2", target_bir_lowering=False)

    def din(name, shape):
        return nc.dram_tensor(name, list(shape), F32, kind="ExternalInput").ap()

    def dout(name, shape):
        return nc.dram_tensor(name, list(shape), F32, kind="ExternalOutput").ap()

    x_d = din("x", [NTOK, D])
    ck_d = din("ck", [512, 512])
    cv_d = din("cv", [512, 512])
    sf_d = din("sf0", [8, 128, 256])
    sb_d = din("sb0", [8, 128, 256])
    pcols_d = din("pcols", [128, NPC])
    ctab_d = din("ctab", [128, NCT])
    wflat = {}
    woffs = {}
    for nm in TILES:
        woffs[nm], tot = tile_offsets(nm)
        wflat[nm] = din(nm, [tot])

    def wt(nm, idx):
        (_, nk, _, nc_) = TILES[nm][idx]
        o = woffs[nm][idx]
        return wflat[nm][o:o + 128 * nk * nc_].rearrange("(p e) -> p e", p=128)

    def win_t(c0):
        return wt("w_in", w_in_index(c0))
    y_d = dout("y", [NTOK, D])
    nk_d = dout("nk", [512, 512])
    nv_d = dout("nv", [512, 512])
    nsf_d = dout("nsf", [2, 8, 128, 256])
    nsb_d = dout("nsb", [2, 8, 128, 256])


    with ExitStack() as st0:
        K = Kern(nc, st0)

        uniq = [0]

        def sbt(st, name, shape, dt):
            uniq[0] += 1
            return st.enter_context(nc.sbuf_tensor(f"sb_{name}_{uniq[0]}", list(shape), dt))

        def MM(out, lhsT, rhs, start, stop, reads, writes):
            K.op("pe", lambda: nc.tensor.matmul(out, lhsT=lhsT, rhs=rhs, start=start, stop=stop), reads, writes)

        def TRP(out, in_, ident, reads, writes):
            K.op("pe", lambda: nc.tensor.transpose(out, in_, ident), reads, writes)

        def ACT(out, in_, func, reads, writes, scale=1.0, bias=0.0, accum_out=None):
            if accum_out is None:
                K.op("act", lambda: nc.scalar.activation(out=out, in_=in_, func=func, bias=bias, scale=scale), reads, writes)
            else:
                K.op("act", lambda: nc.scalar.activation(out=out, in_=in_, func=func, bias=bias, scale=scale, accum_out=accum_out), reads, writes)

        def TS(out, in0, s1, s2, op0, op1, reads, writes):
            if s2 is None:
                K.op("dve", lambda: nc.vector.tensor_scalar(out=out, in0=in0, scalar1=s1, scalar2=None, op0=op0), reads, writes)
            else:
                K.op("dve", lambda: nc.vector.tensor_scalar(out=out, in0=in0, scalar1=s1, scalar2=s2, op0=op0, op1=op1), reads, writes)

        def TT(out, in0, in1, op, reads, writes):
            K.op("dve", lambda: nc.vector.tensor_tensor(out=out, in0=in0, in1=in1, op=op), reads, writes)

        def STT(out, in0, scalar, in1, op0, op1, reads, writes):
            K.op("dve", lambda: nc.vector.scalar_tensor_tensor(out=out, in0=in0, scalar=scalar, in1=in1, op0=op0, op1=op1), reads, writes)

        def CPV(out, in_, reads, writes):
            K.op("dve", lambda: nc.vector.tensor_copy(out=out, in_=in_), reads, writes)

        def CPA(out, in_, reads, writes):
            K.op("act", lambda: nc.scalar.copy(out=out, in_=in_), reads, writes)

        def DMA(queue, out, in_, dsem, reads, writes, is_out=False, bar=False):
            eng = nc.sync if queue == "sp" else nc.gpsimd
            K.dma(queue, lambda: eng.dma_start(out=out, in_=in_), dsem, reads, writes, is_out=is_out, bar=bar)

        ctab = sbt(st0, "ctab", [128, NCT], F32)
        pcol = sbt(st0, "pcol", [128, NPC], F32)
        identb = sbt(st0, "identb", [128, 128], BF16)
        onesb = sbt(st0, "onesb", [128, 128], BF16)
        sct = sbt(st0, "sct", [128, 16, 2], BF16)
        modt = sbt(st0, "modt", [128, 96, 2], F32)
        gsa = sbt(st0, "gsa", [128, 16, 2], F32)
        gsf = sbt(st0, "gsf", [128, 16, 2], F32)
        lg = sbt(st0, "lg", [128, 16], F32)
        gch = sbt(st0, "gch", [128, 16], F32)
        zt = sbt(st0, "zt", [128, 16], F32)
        tmp16 = sbt(st0, "tmp16", [128, 16], F32)
        wsl = [sbt(st0, f"wsl{i}", [128, SLOTE], BF16) for i in range(NSLOT)]
        oaT = sbt(st0, "oaT", [128, 16, 1024], BF16)
        orT = sbt(st0, "orT", [128, 16, 1024], BF16)
        pbk = [st0.enter_context(nc.psum_tensor(f"pbk{i}", [128, 512], F32)) for i in range(8)]
        pbb = [Buf(f"pbk{i}", excl=True) for i in range(8)]
        ident = ctab[:, CT_ID:CT_ID + 128]

        cb = Buf("const")
        csem = K.dsem("const", group=True)
        mb = Buf("mod")
        wbuf = [Buf(f"w{i}") for i in range(NSLOT)]
        wsem = [K.dsem(f"w{i}", exempt=True) for i in range(NSLOT)]
        oab = [[Buf() for _ in range(2)] for _ in range(16)]
        orb = [[Buf() for _ in range(8)] for _ in range(16)]
        osem = K.dsem("out")
        xsem = [K.dsem("x0"), K.dsem("x1")]
        ssem = [K.dsem(f"st{i}") for i in range(4)]
        cvsem = K.dsem("cv")

        rot = {"mm": 0, "aux": 0, "tr": 0, "w": 0, "mm2": 0, "bo": 0}

        def bank(group):
            if group == "mm":
                i = rot["mm"] % 4
            elif group == "mm2":
                i = (0, 1, 5)[rot["mm2"] % 3]
            elif group == "bo":
                i = 2 + rot["bo"] % 2
            elif group == "aux":
                i = 4 + rot["aux"] % 2
            else:
                i = 6 + rot["tr"] % 2
            rot[group] += 1
            return pbk[i], pbb[i]

        def wload(parts):
            s = rot["w"] % NSLOT
            rot["w"] += 1
            for src, off in parts:
                ne = src.shape[1]
                DMA("pool", wsl[s][:, off:off + ne], src, wsem[s], [], [wbuf[s]])
            return wsl[s], wbuf[s]

        def wview(slot, off, nk, ncol):
            return slot[:, off:off + nk * ncol].rearrange("p (k n) -> p k n", k=nk)

        def rstd_from(dst, src, n, reads, writes):
            ACT(dst, src, AF.Ln, reads, writes, scale=1.0 / n, bias=EPS)
            ACT(dst, dst, AF.Exp, writes, writes, scale=-0.5)

        DMA("sp", ctab[:], ctab_d[:, :], csem, [], [cb])
        DMA("sp", pcol[:], pcols_d[:, :], csem, [], [cb])
        K.op("dve", lambda: nc.vector.memset(onesb[:], 1.0), [], [cb])
        CPA(identb[:], ident, [cb], [cb])
        if STOP <= -5:
            K.finish()
            return nc
        ACT(sct[:, :, 0], pcol[:, PC_C:PC_C + 16], AF.Silu, [cb], [mb])
        ACT(sct[:, :, 1], pcol[:, PC_CC:PC_CC + 16], AF.Silu, [cb], [mb])
        ACT(tmp16[:], pcol[:, PC_DF:PC_DF + 16], AF.Exp, [cb], [mb], scale=-1.0)
        ACT(tmp16[:], tmp16[:], AF.Ln, [mb], [mb], bias=1.0)
        TS(lg[:], tmp16[:], -1.0, None, ALU.mult, None, [mb], [mb])
        ACT(gch[:], lg[:], AF.Exp, [mb], [mb], scale=128.0)
        if STOP <= -4:
            K.finish()
            return nc
        for h in range(8):
            ACT(zt[:, h:h + 1], ctab[:, CT_ZF:CT_ZF + 1], AF.Exp, [cb, mb], [mb], scale=lg[:, h:h + 1])
            ACT(zt[:, 8 + h:9 + h], ctab[:, CT_ZB:CT_ZB + 1], AF.Exp, [cb, mb], [mb], scale=lg[:, 8 + h:9 + h])
        if STOP <= -3:
            K.finish()
            return nc
        mb2 = Buf("mod2")
        modq = list(range(48 if STOP >= 0 else 0))

        def mod_step(n=1, bk_=None):
            for _ in range(n):
                if not modq:
                    return
                cblk = modq.pop(0)
                slot, wb = wload([(wt("w_mod", cblk), 0)])
                wv = wview(slot, 0, 16, 256)
                ab, abb = bk_ if bk_ is not None else bank("aux")
                mp = ab[:, 0:4].rearrange("p (c w) -> p c w", w=2)
                for j in range(2):
                    for k in range(16):
                        MM(mp[:, j, :], wv[:, k, j * 128:(j + 1) * 128], sct[:, k, :], k == 0, k == 15, [wb, mb], [abb])
                tgt = mb if cblk < 16 else mb2
                for w in range(2):
                    TT(modt[:, 2 * cblk:2 * cblk + 2, w], mp[:, :, w], pcol[:, PC_BMOD + 2 * cblk:PC_BMOD + 2 * cblk + 2], ALU.add,
                       [abb, cb], [tgt])

        gsf_done = [False]

        def mod_flush():
            mod_step(len(modq))
            if not gsf_done[0]:
                gsf_done[0] = True
                for w in range(2):
                    STT(gsf[:, :, w], modt[:, 64:80, w], 1.0, pcol[:, PC_NF:PC_NF + 16], ALU.add, ALU.mult, [mb2, cb], [mb2])

        gsa_done = [False]

        def mod_first(n):
            if gsa_done[0]:
                return
            k_ = min(n, 16 - (48 - len(modq)))
            if k_ > 0:
                mod_step(k_)
            if 48 - len(modq) >= 16:
                gsa_done[0] = True
                for w in range(2):
                    STT(gsa[:, :, w], modt[:, 16:32, w], 1.0, pcol[:, PC_NA:PC_NA + 16], ALU.add, ALU.mult, [mb, cb], [mb])

        def col(t3, j, w):
            return t3[:, j, w:w + 1]

        SEGS = [
            dict(name="S", tok0=0, ntok=1024, w=0, latent=True,
                 qblocks=[(0, 512, list(range(12))), (512, 512, list(range(12)))], nkt=12,
                 seqs=[(0, 8)]),
            dict(name="P", tok0=1024, ntok=512, w=1, latent=False,
                 qblocks=[(0, 256, [0, 1]), (256, 256, [2, 3])], nkt=4,
                 seqs=[(0, 2), (2, 2)]),
        ]
        ATT_SCALE = 128 ** -0.5

        for seg in SEGS:
            if STOP < 1 or (STOP < 5 and seg["name"] == "P"):
                continue
            tok0, ntok, w, latent = seg["tok0"], seg["ntok"], seg["w"], seg["latent"]
            ntile = ntok // 128
            nblk = ntok // 512
            with ExitStack() as stS:
                hT = sbt(stS, "hT", [128, 16, 1024], BF16)
                hTb = [[Buf() for _ in range(16)] for _ in range(8)]

                def hreads(k, nb):
                    return [hTb[4 * nb + i][k] for i in range(4)]

                with ExitStack() as st1:
                    xsl = [sbt(st1, f"xsl{i}", [128, D], F32) for i in range(2)]
                    xn = [sbt(st1, f"xn{i}", [128, D], BF16) for i in range(2)]
                    junk = sbt(st1, "junk", [128, D], BF16)
                    ss = sbt(st1, "ss1", [128, 4], F32)
                    xb = [Buf(), Buf()]
                    xnb = [Buf(), Buf()]
                    ssb = [Buf(), Buf()]
                    jb = Buf()
                    def p1_A(t):
                        s = t % 2
                        DMA("sp", xsl[s][:], x_d[tok0 + t * 128: tok0 + (t + 1) * 128, :], xsem[s], [], [xb[s]])
                        ACT(junk[:], xsl[s][:], AF.Square, [xb[s]], [jb, ssb[s]], accum_out=ss[:, s:s + 1])
                        rstd_from(ss[:, 2 + s:3 + s], ss[:, s:s + 1], D, [ssb[s]], [ssb[s]])
                        TS(xn[s][:], xsl[s][:], ss[:, 2 + s:3 + s], None, ALU.mult, None, [xb[s], ssb[s]], [xnb[s]])

                    def p1_B(t):
                        s = t % 2
                        for k8 in range(2):
                            tb_, tbb = bank("tr")
                            tpv = tb_[:].bitcast(BF16)
                            for kk in range(8):
                                k = k8 * 8 + kk
                                TRP(tpv[:, kk * 128:(kk + 1) * 128], xn[s][:, k * 128:(k + 1) * 128], identb[:], [xnb[s], cb], [tbb])
                            dst = hT[:, k8 * 8:(k8 + 1) * 8, t * 128:(t + 1) * 128]
                            src = tpv.rearrange("p (k t) -> p k t", k=8)
                            hb8 = [hTb[t][k8 * 8 + kk] for kk in range(8)]
                            if k8 == 0:
                                CPA(dst, src, [tbb], hb8)
                            else:
                                CPV(dst, src, [tbb], hb8)
                        mod_first(2)

                    p1_A(0)
                    for t in range(ntile):
                        if t + 1 < ntile:
                            p1_A(t + 1)
                        p1_B(t)
                    mod_first(16)
                    for k in range(16):
                        hk = hT[:, k, 0:ntok]
                        hbk = [hTb[t][k] for t in range(ntile)]
                        if k % 2 == 0:
                            ACT(hk, hk, AF.Identity, hbk + [mb], hbk, scale=col(gsa, k, w), bias=col(modt, k, w))
                        else:
                            TS(hk, hk, col(gsa, k, w), col(modt, k, w), ALU.mult, ALU.add, hbk + [mb], hbk)
                K.barrier()

                def pipeline(units):
                    st_ = [None] * len(units)
                    if units:
                        st_[0] = units[0][0]()
                    for u in range(len(units)):
                        if u + 1 < len(units):
                            st_[u + 1] = units[u + 1][0]()
                        units[u][1](st_[u])

                def acc_fm(bk, bkb, ncols, lhs_fn, rhs_fn, nk, reads_fn):
                    for k in range(nk):
                        MM(bk[:, 0:ncols], lhs_fn(k), rhs_fn(k), k == 0, k == nk - 1, reads_fn(k), [bkb])

                with ExitStack() as st2:
                  if STOP >= 2:
                      nkt = seg["nkt"]
                      kT = sbt(st2, "kT", [128, 4, nkt * 128], BF16)
                      vall = sbt(st2, "vall", [128, nkt, 512], BF16)
                      qT = sbt(st2, "qT", [128, 4, ntok], BF16)
                      sqt_ = [sbt(st2, f"sqt{i}", [128, 512], BF16) for i in range(2)]
                      rst_ = [sbt(st2, f"rst{i}", [128, 512], F32) for i in range(2)]
                      knt_ = [sbt(st2, f"knt{i}", [128, 512], F32) for i in range(2)]
                      t1_ = [sbt(st2, f"t1{i}", [128, 512], F32) for i in range(2)]
                      t2_ = [sbt(st2, f"t2{i}", [128, 512], F32) for i in range(2)]
                      nrm_i = [0]
                      pT = [sbt(st2, f"pT{i}", [128, 512], BF16) for i in range(3)]
                      rdt = [sbt(st2, f"rdt{i}", [128, 512], F32) for i in range(2)]
                      if latent:
                          cks = sbt(st2, "cks", [128, 512], F32)
                      else:
                          kout = sbt(st2, "kout", [128, 4, 512], F32)
                          vout = sbt(st2, "vout", [128, 4, 512], F32)
                      kTb = [[Buf() for _ in range(12)] for _ in range(4)]
                      vb = [[Buf() for _ in range(2)] for _ in range(12)]
                      qTb = [[Buf() for _ in range(2)] for _ in range(4)]
                      ckb, koutb, voutb = (Buf() for _ in range(3))
                      sqb_, rsb_, knb_, t1b_, t2b_ = ([Buf(), Buf()] for _ in range(5))
                      rdb = [Buf(), Buf()]
                      pTb = [Buf() for _ in range(3)]
                      cksem = cvsem

                      def qk_norm_A(bk, bkb, normcol):
                          n = 512
                          pi = nrm_i[0] % 2
                          nrm_i[0] += 1
                          sqt, rst, knt = sqt_[pi], rst_[pi], knt_[pi]
                          sqb, rsb, knb = sqb_[pi], rsb_[pi], knb_[pi]
                          ACT(sqt[:], bk[:, 0:n], AF.Square, [bkb], [sqb])
                          ab, abb = bank("aux")
                          MM(ab[:, 0:n], onesb[:], sqt[:], True, True, [sqb, cb], [abb])
                          rstd_from(rst[:], ab[:, 0:n], 128, [abb], [rsb])
                          STT(knt[:], bk[:, 0:n], normcol, rst[:], ALU.mult, ALU.mult, [bkb, rsb, cb], [knb])
                          return pi

                      def qk_norm_B(pi, dst, dstbufs, tokc0, is_k_out=None):
                          n = 512
                          knt, t1, t2 = knt_[pi], t1_[pi], t2_[pi]
                          knb, t1b, t2b = knb_[pi], t1b_[pi], t2b_[pi]
                          if latent:
                              ab2, abb2 = bank("aux")
                              MM(ab2[:, 0:n], ctab[:, CT_RM:CT_RM + 128], knt[:], True, True, [knb, cb], [abb2])
                              TT(t1[:], knt[:], ctab[:, CT_COS + tokc0:CT_COS + tokc0 + n], ALU.mult, [knb, cb], [t1b])
                              TT(t2[:], ab2[:, 0:n], ctab[:, CT_SIN + tokc0:CT_SIN + tokc0 + n], ALU.mult, [abb2, cb], [t2b])
                              TT(dst, t1[:], t2[:], ALU.add, [t1b, t2b], dstbufs)
                          else:
                              CPA(dst, knt[:], [knb], dstbufs)
                              if is_k_out is not None:
                                  g = is_k_out
                                  tb_, tbb = bank("tr")
                                  for t in range(4):
                                      TRP(tb_[:, t * 128:(t + 1) * 128], knt[:, t * 128:(t + 1) * 128], ident, [knb, cb], [tbb])
                                  CPV(kout[:, :, g * 128:(g + 1) * 128], tb_[:].rearrange("p (t d) -> p t d", t=4), [tbb], [koutb])

                      def pipeline3(units):
                          n_ = len(units)
                          hs = [None] * n_
                          as_ = [None] * n_
                          if n_:
                              hs[0] = units[0][0]()
                          for u in range(n_):
                              if u + 1 < n_:
                                  hs[u + 1] = units[u + 1][0]()
                              as_[u] = units[u][1](hs[u])
                              if u >= 1:
                                  units[u - 1][2](as_[u - 1])
                          if n_:
                              units[n_ - 1][2](as_[n_ - 1])

                      def pipeline(units):
                          st_ = [None] * len(units)
                          if units:
                              st_[0] = units[0][0]()
                          for u in range(len(units)):
                              if u + 1 < len(units):
                                  st_[u + 1] = units[u + 1][0]()
                              units[u][1](st_[u])

                      for j in range(2):
                          slot, wb = wload([(win_t(2048 + 256 * j), 0)])
                          wv = wview(slot, 0, 16, 256)
                          units = []
                          for hh in range(2):
                              for nb in range(nblk):
                                  def head_fn(hh=hh, nb=nb, wv=wv, wb=wb):
                                      bk, bkb = bank("mm")
                                      acc_fm(bk, bkb, 512, lambda k: wv[:, k, hh * 128:(hh + 1) * 128],
                                             lambda k: hT[:, k, nb * 512:(nb + 1) * 512], 16, lambda k: [wb] + hreads(k, nb))
                                      return bk, bkb

                                  def tailA_fn(st_):
                                      return qk_norm_A(st_[0], st_[1], pcol[:, PC_KN:PC_KN + 1])

                                  def tailB_fn(pi, hh=hh, nb=nb, j=j):
                                      g = 2 * j + hh
                                      qk_norm_B(pi, kT[:, g, nb * 512:(nb + 1) * 512], [kTb[g][4 * nb + i] for i in range(4)], nb * 512,
                                                is_k_out=(None if latent else g))
                                  units.append((head_fn, tailA_fn, tailB_fn))
                          pipeline3(units)
                          if latent:
                              mod_step(1)
                      if not latent:
                          DMA("sp", nk_d.rearrange("(t p) c -> p t c", p=128), kout[:], osem, [koutb], [], is_out=True)
                      for j in range(2):
                          slot, wb = wload([(win_t(2560 + 256 * j), 0)])
                          wv = wview(slot, 0, 16, 256)
                          for t in range(ntile):
                              bk, bkb = bank("mm")
                              for k in range(16):
                                  MM(bk[:, 0:256], hT[:, k, t * 128:(t + 1) * 128], wv[:, k, :], k == 0, k == 15, [wb, hTb[t][k]], [bkb])
                              CPA(vall[:, t, j * 256:(j + 1) * 256], bk[:, 0:256], [bkb], [vb[t][j]])
                              if not latent:
                                  CPV(vout[:, t, j * 256:(j + 1) * 256], bk[:, 0:256], [bkb], [voutb])
                          if latent:
                              mod_step(1)
                      if not latent:
                          DMA("sp", nv_d.rearrange("(t p) c -> p t c", p=128), vout[:], osem, [voutb], [], is_out=True)
                      if latent:
                          DMA("pool", vall[:, 8:12, :], cv_d.rearrange("(c p) n -> p c n", p=128), cksem, [],
                              [vb[8 + c][jj] for c in range(4) for jj in range(2)], bar=True)
                          for c in range(4):
                              DMA("sp", cks[:], ck_d[c * 128:(c + 1) * 128, :], xsem[0], [], [ckb])
                              tb_, tbb = bank("tr")
                              for g in range(4):
                                  TRP(tb_[:, g * 128:(g + 1) * 128], cks[:, g * 128:(g + 1) * 128], ident, [ckb, cb], [tbb])
                              CPV(kT[:, :, 1024 + c * 128: 1024 + (c + 1) * 128], tb_[:].rearrange("p (g t) -> p g t", g=4),
                                  [tbb], [kTb[g][8 + c] for g in range(4)])
                      obanks = [(pbk[3], pbb[3]), (pbk[6], pbb[6])]
                      dbanks = [(pbk[4], pbb[4]), (pbk[5], pbb[5])]
                      ucount = [0]

                      def q_proj_chain(g, j):
                          if True:
                              slot, wb = wload([(win_t(512 * g + 256 * j), 0)])
                              wv = wview(slot, 0, 16, 256)
                              units = []
                              for hh in range(2):
                                  for nb in range(nblk):
                                      def head_fn(hh=hh, nb=nb, wv=wv, wb=wb):
                                          bk, bkb = bank("mm")
                                          acc_fm(bk, bkb, 512, lambda k: wv[:, k, hh * 128:(hh + 1) * 128],
                                                 lambda k: hT[:, k, nb * 512:(nb + 1) * 512], 16, lambda k: [wb] + hreads(k, nb))
                                          return bk, bkb

                                      def tailA_fn(st_):
                                          return qk_norm_A(st_[0], st_[1], pcol[:, PC_QN:PC_QN + 1])

                                      def tailB_fn(pi, hh=hh, nb=nb, j=j):
                                          hl = 2 * j + hh
                                          qk_norm_B(pi, qT[:, hl, nb * 512:(nb + 1) * 512], [qTb[hl][nb]], nb * 512)
                                      units.append((head_fn, tailA_fn, tailB_fn))
                              pipeline3(units)

                      def q_attn(g, j):
                          if True:
                              steps = []
                              for hh in range(2):
                                  hl = 2 * j + hh
                                  for qi, (q0, qn, kcs) in enumerate(seg["qblocks"]):
                                      un = ucount[0]
                                      ucount[0] += 1
                                      for i, kc in enumerate(kcs):
                                          steps.append((hl, q0, qn, i, kc, len(kcs), un))

                              def emit_st(idx):
                                  hl, q0, qn, i, kc, nkc, un = steps[idx]
                                  MM(pbk[idx % 3][:, 0:qn], kT[:, g, kc * 128:(kc + 1) * 128], qT[:, hl, q0:q0 + qn], True, True,
                                     [kTb[g][kc], qTb[hl][q0 // 512]], [pbb[idx % 3]])
                              for idx in range(min(2, len(steps))):
                                  emit_st(idx)
                              for idx in range(len(steps)):
                                  hl, q0, qn, i, kc, nkc, un = steps[idx]
                                  if idx + 2 < len(steps):
                                      emit_st(idx + 2)
                                  ob, obb = obanks[un % 2]
                                  db, dbb = dbanks[un % 2]
                                  p_ = pT[idx % 3]
                                  ACT(p_[:, 0:qn], pbk[idx % 3][:, 0:qn], AF.Exp, [pbb[idx % 3]], [pTb[idx % 3]], scale=ATT_SCALE)
                                  MM(ob[:, 0:qn], vall[:, kc, g * 128:(g + 1) * 128], p_[:, 0:qn], i == 0, i == nkc - 1,
                                     [vb[kc][g // 2], pTb[idx % 3]], [obb])
                                  MM(db[:, 0:qn], onesb[:], p_[:, 0:qn], i == 0, i == nkc - 1, [cb, pTb[idx % 3]], [dbb])
                                  if i == nkc - 1:
                                      head = 4 * g + hl
                                      rd_ = rdt[un % 2]
                                      K.op("dve", lambda db=db, qn=qn, rd_=rd_: nc.vector.reciprocal(out=rd_[:, 0:qn], in_=db[:, 0:qn]),
                                           [dbb], [rdb[un % 2]])
                                      TT(oaT[:, head, q0:q0 + qn], ob[:, 0:qn], rd_[:, 0:qn], ALU.mult, [obb, rdb[un % 2]],
                                         [oab[head][q0 // 512]])
                              if latent:
                                  mod_step(1)
                      qtiles = [(g, j) for g in range(4) for j in range(2)]
                      q_proj_chain(*qtiles[0])
                      for qi_, (g, j) in enumerate(qtiles):
                          if qi_ + 1 < len(qtiles):
                              q_proj_chain(*qtiles[qi_ + 1])
                          q_attn(g, j)
                K.barrier()

                with ExitStack() as st3:
                  if STOP >= 3:
                      def two(name, shape, dt):
                          return [sbt(st3, f"{name}{i}", shape, dt) for i in range(2)]
                      qrT = two("qrT", [128, 1024], BF16)
                      qf = two("qf", [128, 1024], BF16)
                      qbw = two("qbw", [128, 1024], BF16)
                      krT = two("krT", [128, 1024], BF16)
                      kzf = two("kzf", [128, 8, 128], BF16)
                      kzb = two("kzb", [128, 8, 128], BF16)
                      vr = two("vr", [128, 8, 256], BF16)
                      sgT = two("sgT", [128, 2, 1024], BF16)
                      DC4 = two("DC4", [128, 4, 128], F32)
                      XF4 = sbt(st3, "XF4", [128, 4, 128], F32)
                      XB4 = sbt(st3, "XB4", [128, 4, 128], F32)
                      tmA = sbt(st3, "tmA", [128, 128], F32)
                      tmB = sbt(st3, "tmB", [128, 128], F32)
                      Sfb = sbt(st3, "Sfb", [128, 8, 256], BF16)
                      Sbb = sbt(st3, "Sbb", [128, 8, 256], BF16)
                      S32 = [sbt(st3, f"S32_{i}", [128, 256], F32) for i in range(4)]
                      sc = [sbt(st3, f"sc{i}", [128, 512], BF16) for i in range(2)]
                      ont = [sbt(st3, f"ont{i}", [128, 256], BF16) for i in range(2)]
                      st6 = sbt(st3, "st6", [128, 2, 6], F32)
                      mv = sbt(st3, "mv", [128, 2, 4], F32)

                      def twob(n=None):
                          if n is None:
                              return [Buf(), Buf()]
                          return [[Buf() for _ in range(n)] for _ in range(2)]
                      qrb, qfb, qbb, krb = twob(2), twob(2), twob(2), twob(2)
                      kzfb, kzbb, dcb = twob(), twob(), twob()
                      vrb = twob(8)
                      sgb = [[[Buf() for _ in range(2)] for _ in range(2)] for _ in range(2)]
                      xfb, tmb = Buf(), Buf()
                      Sfbb = [Buf() for _ in range(8)]
                      Sbbb = [Buf() for _ in range(8)]
                      S32b = [Buf() for _ in range(4)]
                      scb = [Buf(), Buf()]
                      onb = [Buf(), Buf()]
                      stb_ = [Buf(), Buf()]
                      nchunk = ntok // 128

                      def proj_gen(h):
                          pp = h % 2
                          lgf = lg[:, h:h + 1]
                          lgb = lg[:, 8 + h:9 + h]
                          ACT(XF4[:], ctab[:, CT_XF:CT_XF + 128].unsqueeze(1).to_broadcast([128, 4, 128]), AF.Exp, [cb, mb], [xfb], scale=lgf)
                          ACT(XB4[:], ctab[:, CT_XB:CT_XB + 128].unsqueeze(1).to_broadcast([128, 4, 128]), AF.Exp, [cb, mb], [xfb], scale=lgb)
                          ACT(tmA[:], ctab[:, CT_DF:CT_DF + 128], AF.Exp, [cb, mb], [tmb], scale=lgf)
                          ACT(tmB[:], ctab[:, CT_DB:CT_DB + 128], AF.Exp, [cb, mb], [tmb], scale=lgb)
                          TT(tmA[:], tmA[:], ctab[:, CT_MF:CT_MF + 128], ALU.mult, [tmb, cb], [tmb])
                          TT(tmB[:], tmB[:], ctab[:, CT_MB:CT_MB + 128], ALU.mult, [tmb, cb], [tmb])
                          TT(tmA[:], tmA[:], tmB[:], ALU.add, [tmb], [tmb])
                          TT(tmA[:], tmA[:], ctab[:, CT_I2:CT_I2 + 128], ALU.add, [tmb, cb], [tmb])
                          CPV(DC4[pp][:], tmA[:].unsqueeze(1).to_broadcast([128, 4, 128]), [tmb], [dcb[pp]])
                          slot1, wb1 = wload([(win_t(3072 + 128 * h), 0),
                                              (win_t(4096 + 128 * h), 2048)])
                          wq = wview(slot1, 0, 16, 128)
                          wk = wview(slot1, 2048, 16, 128)
                          for nb in range(nblk):
                              blk = slice(nb * 512, (nb + 1) * 512)
                              bk, bkb = bank("mm2")
                              acc_fm(bk, bkb, 512, lambda k: wq[:, k, :], lambda k: hT[:, k, blk], 16, lambda k: [wb1] + hreads(k, nb))
                              CPA(qrT[pp][:, blk], bk[:], [bkb], [qrb[pp][nb]])
                              TT(qf[pp][:, blk], bk[:], XF4[:].rearrange("p a b -> p (a b)"), ALU.mult, [bkb, xfb], [qfb[pp][nb]])
                              TT(qbw[pp][:, blk], bk[:], XB4[:].rearrange("p a b -> p (a b)"), ALU.mult, [bkb, xfb], [qbb[pp][nb]])
                              yield
                              bk, bkb = bank("mm2")
                              acc_fm(bk, bkb, 512, lambda k: wk[:, k, :], lambda k: hT[:, k, blk], 16, lambda k: [wb1] + hreads(k, nb))
                              ACT(krT[pp][:, blk], bk[:], AF.Identity, [bkb], [krb[pp][nb]], scale=128 ** -0.5)
                              yield
                          if latent:
                              mod_step(1, (pbk[4], pbb[4]))
                          slot2, wb2 = wload([(win_t(5120 + 256 * h), 0)])
                          wv = wview(slot2, 0, 16, 256)
                          for t in range(ntile):
                              bk, bkb = bank("mm2")
                              for k in range(16):
                                  MM(bk[:, 0:256], hT[:, k, t * 128:(t + 1) * 128], wv[:, k, :], k == 0, k == 15, [wb2, hTb[t][k]], [bkb])
                              if t % 2 == 0:
                                  CPA(vr[pp][:, t, :], bk[:, 0:256], [bkb], [vrb[pp][t]])
                              else:
                                  CPV(vr[pp][:, t, :], bk[:, 0:256], [bkb], [vrb[pp][t]])
                              yield
                          if latent:
                              mod_step(1, (pbk[4], pbb[4]))
                          slot3, wb3 = wload([(win_t(7168 + 256 * h), 0)])
                          wgv = wview(slot3, 0, 16, 256)
                          for e in range(2):
                              for nb in range(nblk):
                                  blk = slice(nb * 512, (nb + 1) * 512)
                                  bk, bkb = bank("mm2")
                                  acc_fm(bk, bkb, 512, lambda k: wgv[:, k, e * 128:(e + 1) * 128], lambda k: hT[:, k, blk], 16,
                                         lambda k: [wb3] + hreads(k, nb))
                                  ACT(sgT[pp][:, e, blk], bk[:], AF.Silu, [bkb], [sgb[pp][e][nb]])
                                  yield
                          if latent:
                              mod_step(1, (pbk[4], pbb[4]))
                          tb_, tbb = bank("tr")
                          tpv = tb_[:].bitcast(BF16)
                          for c in range(nchunk):
                              TRP(tpv[:, c * 128:(c + 1) * 128], krT[pp][:, c * 128:(c + 1) * 128], identb[:], [krb[pp][c // 4], cb], [tbb])
                          TS(kzf[pp][:, 0:nchunk, :], tpv[:, 0:nchunk * 128].rearrange("p (c d) -> p c d", c=nchunk), zt[:, h:h + 1], None,
                             ALU.mult, None, [tbb, mb], [kzfb[pp]])
                          ACT(kzb[pp][:, 0:nchunk, :], tpv[:, 0:nchunk * 128].rearrange("p (c d) -> p c d", c=nchunk), AF.Identity, [tbb, mb],
                              [kzbb[pp]], scale=zt[:, 8 + h:9 + h])
                          yield

                      def chain_gen(h):
                          pp = h % 2
                          for si, (c0, ncs) in enumerate(seg["seqs"]):
                              Sf, Sfx = S32[2 * si], S32b[2 * si]
                              Sb_, Sbx = S32[2 * si + 1], S32b[2 * si + 1]
                              if latent:
                                  DMA("sp", Sf[:], sf_d[h, :, :], ssem[2 * si], [], [Sfx])
                                  DMA("sp", Sb_[:], sb_d[h, :, :], ssem[2 * si + 1], [], [Sbx])
                              else:
                                  K.op("dve", lambda Sf=Sf: nc.vector.memset(Sf[:], 0.0), [], [Sfx])
                                  K.op("dve", lambda Sb_=Sb_: nc.vector.memset(Sb_[:], 0.0), [], [Sbx])
                              CPA(Sfb[:, c0, :], Sf[:], [Sfx], [Sfbb[c0]])
                              CPA(Sbb[:, c0 + ncs - 1, :], Sb_[:], [Sbx], [Sbbb[c0 + ncs - 1]])
                              last_f = ncs if not latent else ncs - 1
                              for i in range(last_f):
                                  c = c0 + i
                                  cb_ = c0 + ncs - 1 - i
                                  ab, abb = pbk[4], pbb[4]
                                  MM(ab[:, 0:256], kzf[pp][:, c, :], vr[pp][:, c, :], True, True, [kzfb[pp], vrb[pp][c]], [abb])
                                  MM(ab[:, 256:512], kzb[pp][:, cb_, :], vr[pp][:, cb_, :], True, True, [kzbb[pp], vrb[pp][cb_]], [abb])
                                  STT(Sf[:], Sf[:], gch[:, h:h + 1], ab[:, 0:256], ALU.mult, ALU.add, [Sfx, abb, mb], [Sfx])
                                  STT(Sb_[:], Sb_[:], gch[:, 8 + h:9 + h], ab[:, 256:512], ALU.mult, ALU.add, [Sbx, abb, mb], [Sbx])
                                  if i + 1 < ncs:
                                      CPA(Sfb[:, c + 1, :], Sf[:], [Sfx], [Sfbb[c + 1]])
                                      CPA(Sbb[:, cb_ - 1, :], Sb_[:], [Sbx], [Sbbb[cb_ - 1]])
                                  yield
                              if not latent:
                                  DMA("sp", nsf_d[si, h, :, :], Sf[:], ssem[2 * si], [Sfx], [], is_out=True)
                                  DMA("sp", nsb_d[si, h, :, :], Sb_[:], ssem[2 * si + 1], [Sbx], [], is_out=True)
                          for c4 in range(nchunk // 4):
                              bk, bkb = bank("bo")
                              for i in range(4):
                                  c = c4 * 4 + i
                                  MM(bk[:, i * 128:(i + 1) * 128], krT[pp][:, c * 128:(c + 1) * 128], qrT[pp][:, c * 128:(c + 1) * 128], True, True,
                                     [krb[pp][c // 4], qrb[pp][c // 4]], [bkb])
                              TT(sc[c4][:], bk[:], DC4[pp][:].rearrange("p a b -> p (a b)"), ALU.mult, [bkb, dcb[pp]], [scb[c4]])
                          yield

                          def head_fn(cp):
                              bo, bob = bank("bo")
                              for e2 in range(2):
                                  c = cp * 2 + e2
                                  c4, i = c // 4, c % 4
                                  o_ = bo[:, e2 * 256:(e2 + 1) * 256]
                                  MM(o_, sc[c4][:, i * 128:(i + 1) * 128], vr[pp][:, c, :], True, False, [scb[c4], vrb[pp][c]], [bob])
                                  MM(o_, qf[pp][:, c * 128:(c + 1) * 128], Sfb[:, c, :], False, False, [qfb[pp][c // 4], Sfbb[c]], [bob])
                                  MM(o_, qbw[pp][:, c * 128:(c + 1) * 128], Sbb[:, c, :], False, True, [qbb[pp][c // 4], Sbbb[c]], [bob])
                              return bo, bob

                          def tailA_fn(st_, cp):
                              bo, bob = st_
                              for e2 in range(2):
                                  o_ = bo[:, e2 * 256:(e2 + 1) * 256]
                                  K.op("dve", lambda o_=o_, e2=e2: nc.vector.bn_stats(out=st6[:, e2, :], in_=o_), [bob], [stb_[0]])
                                  K.op("dve", lambda e2=e2: nc.vector.bn_aggr(out=mv[:, e2, 0:2], in_=st6[:, e2, :]), [stb_[0]], [stb_[0]])
                              ACT(mv[:, :, 2:3], mv[:, :, 1:2], AF.Ln, [stb_[0]], [stb_[0]], bias=EPS)
                              ACT(mv[:, :, 2:3], mv[:, :, 2:3], AF.Exp, [stb_[0]], [stb_[0]], scale=-0.5)
                              for e2 in range(2):
                                  o_ = bo[:, e2 * 256:(e2 + 1) * 256]
                                  TS(ont[e2][:], o_, mv[:, e2, 0:1], mv[:, e2, 2:3], ALU.subtract, ALU.mult, [bob, stb_[0]], [onb[e2]])

                          def tailB_fn(cp):
                              tb_, tbb = bank("tr")
                              tpv = tb_[:].bitcast(BF16)
                              for e2 in range(2):
                                  for e in range(2):
                                      TRP(tpv[:, (2 * e2 + e) * 128:(2 * e2 + e + 1) * 128], ont[e2][:, e * 128:(e + 1) * 128], identb[:],
                                          [onb[e2], cb], [tbb])
                              for e2 in range(2):
                                  c = cp * 2 + e2
                                  for e in range(2):
                                      STT(orT[:, 2 * h + e, c * 128:(c + 1) * 128], tpv[:, (2 * e2 + e) * 128:(2 * e2 + e + 1) * 128],
                                          pcol[:, PC_RN + 2 * h + e:PC_RN + 2 * h + e + 1], sgT[pp][:, e, c * 128:(c + 1) * 128],
                                          ALU.mult, ALU.mult, [tbb, cb, sgb[pp][e][c // 4]], [orb[2 * h + e][c]])
                          ncp = nchunk // 2
                          cur = head_fn(0)
                          for cp in range(ncp):
                              nxt_ = head_fn(cp + 1) if cp + 1 < ncp else None
                              tailA_fn(cur, cp)
                              yield
                              tailB_fn(cp)
                              yield
                              cur = nxt_

                      g0 = proj_gen(0)
                      for _ in g0:
                          pass
                      for h in range(8):
                          nxt = proj_gen(h + 1) if h + 1 < 8 else None
                          for _ in chain_gen(h):
                              if nxt is not None:
                                  if next(nxt, "done") == "done":
                                      nxt = None
                          if nxt is not None:
                              for _ in nxt:
                                  pass
                K.barrier()

            with ExitStack() as st4:
              if STOP >= 4:
                  xT = sbt(st4, "xT", [128, 16, 512], F32)
                  hB = sbt(st4, "hB", [128, 16, 512], BF16)
                  mT = sbt(st4, "mT", [128, 16, 512], BF16)
                  iot = [sbt(st4, f"iot{i}", [128, D], F32) for i in range(2)]
                  tA = sbt(st4, "tA", [128, 2, 512], F32)
                  sgu = [sbt(st4, f"sgu{i}", [128, 2, 512], BF16) for i in range(2)]
                  sga, sgr = sgu[0], sgu[1]
                  rs4 = sbt(st4, "rs4", [128, 512], F32)
                  tmpf2 = [sbt(st4, f"tmpf{i}", [128, 512], F32) for i in range(2)]
                  tB = tmpf2[0]
                  sq4 = [sbt(st4, f"sq4_{i}", [128, 512], BF16) for i in range(2)]
                  xTb = [Buf() for _ in range(16)]
                  hBb = [Buf() for _ in range(16)]
                  mTb = [Buf() for _ in range(16)]
                  iob = [Buf(), Buf()]
                  rsb4 = Buf()
                  tmpb2 = [Buf(), Buf()]
                  tBb = tmpb2[0]
                  tAb = [Buf(), Buf()]
                  sgub = [Buf(), Buf()]
                  sgab, sgrb = sgub[0], sgub[1]
                  sq4b = [Buf(), Buf()]
                  sqi = [0]
                  ioi = [0]

                  def sumsq_sq(j):
                      s = sqi[0] % 2
                      sqi[0] += 1
                      ACT(sq4[s][:], xT[:, j, :], AF.Square, [xTb[j]], [sq4b[s]])
                      return s

                  def sumsq_mm(s, first, last, ssbk, ssbb):
                      MM(ssbk[:], onesb[:], sq4[s][:], first, last, [sq4b[s], cb], [ssbb])

                  def sumsq_chunk(j, first, last, ssbk, ssbb):
                      sumsq_mm(sumsq_sq(j), first, last, ssbk, ssbb)

                  mod_flush()
                  for tb in range(nblk):
                      t0 = tok0 + tb * 512
                      l0 = tb * 512
                      for t in range(4):
                          s = ioi[0] % 2
                          ioi[0] += 1
                          DMA("sp", iot[s][:], x_d[t0 + t * 128: t0 + (t + 1) * 128, :], xsem[s], [], [iob[s]])
                          for k4 in range(4):
                              tb_, tbb = bank("tr")
                              for kk in range(4):
                                  k = k4 * 4 + kk
                                  TRP(tb_[:, kk * 128:(kk + 1) * 128], iot[s][:, k * 128:(k + 1) * 128], ident, [iob[s], cb], [tbb])
                              dst = xT[:, k4 * 4:(k4 + 1) * 4, t * 128:(t + 1) * 128]
                              src = tb_[:].rearrange("p (k t) -> p k t", k=4)
                              if k4 % 2 == 0:
                                  CPA(dst, src, [tbb], [xTb[k4 * 4 + i] for i in range(4)])
                              else:
                                  CPV(dst, src, [tbb], [xTb[k4 * 4 + i] for i in range(4)])
                      ssbk, ssbb = pbk[5], pbb[5]
                      for j in range(16):
                          sumsq_chunk(j, j == 0, j == 15, ssbk, ssbb)
                      rstd_from(rs4[:], ssbk[:], D, [ssbb], [rsb4])
                      for j in range(16):
                          STT(tmpf2[j % 2][:], xT[:, j, :], col(gsa, j, w), rs4[:], ALU.mult, ALU.mult, [xTb[j], mb, rsb4], [tmpb2[j % 2]])
                          ACT(hB[:, j, :], tmpf2[j % 2][:], AF.Identity, [tmpb2[j % 2], mb], [hBb[j]], bias=col(modt, j, w))
                      for cg in range(8):
                          slot, wb = wload([(win_t(9216 + 256 * cg), 0)])
                          wv = wview(slot, 0, 16, 256)
                          for e in range(2):
                              bk, bkb = bank("mm")
                              acc_fm(bk, bkb, 512, lambda k: wv[:, k, e * 128:(e + 1) * 128], lambda k: hB[:, k, :], 16, lambda k: [wb, hBb[k]])
                              ACT(sga[:, e, :], bk[:], AF.Sigmoid, [bkb], [sgab])
                          slot, wb = wload([(win_t(11264 + 256 * cg), 0)])
                          wv = wview(slot, 0, 16, 256)
                          for e in range(2):
                              bk, bkb = bank("mm")
                              acc_fm(bk, bkb, 512, lambda k: wv[:, k, e * 128:(e + 1) * 128], lambda k: hB[:, k, :], 16, lambda k: [wb, hBb[k]])
                              ACT(sgr[:, e, :], bk[:], AF.Sigmoid, [bkb], [sgrb])
                          slot, wb = wload([(wt("w_ba", cg), 0)])
                          wv = wview(slot, 0, 16, 256)
                          for e in range(2):
                              bk, bkb = bank("mm")
                              acc_fm(bk, bkb, 512, lambda k: wv[:, k, e * 128:(e + 1) * 128], lambda k: oaT[:, k, l0:l0 + 512], 16,
                                     lambda k: [wb, oab[k][tb]])
                              TT(tA[:, e, :], bk[:], sga[:, e, :], ALU.mult, [bkb, sgab], [tAb[e]])
                          slot, wb = wload([(wt("w_br", cg), 0)])
                          wv = wview(slot, 0, 16, 256)
                          for e in range(2):
                              bk, bkb = bank("mm")
                              acc_fm(bk, bkb, 512, lambda k: wv[:, k, e * 128:(e + 1) * 128], lambda k: orT[:, k, l0:l0 + 512], 16,
                                     lambda k: [wb] + [orb[k][4 * tb + i] for i in range(4)])
                              TT(tB[:], bk[:], sgr[:, e, :], ALU.mult, [bkb, sgrb], [tBb])
                              TT(mT[:, 2 * cg + e, :], tA[:, e, :], tB[:], ALU.add, [tAb[e], tBb], [mTb[2 * cg + e]])
                      pend = None
                      for cg in range(8):
                          slot, wb = wload([(wt("w_out", cg), 0)])
                          wv = wview(slot, 0, 16, 256)
                          for e in range(2):
                              j = 2 * cg + e
                              bk, bkb = bank("mm")
                              acc_fm(bk, bkb, 512, lambda k: wv[:, k, e * 128:(e + 1) * 128], lambda k: mT[:, k, :], 16, lambda k: [wb, mTb[k]])
                              if pend is not None:
                                  sumsq_mm(pend[0], pend[1] == 0, False, ssbk, ssbb)
                              STT(xT[:, j, :], bk[:], col(modt, 32 + j, w), xT[:, j, :], ALU.mult, ALU.add, [bkb, mb2, xTb[j]], [xTb[j]])
                              pend = (sumsq_sq(j), j)
                      sumsq_mm(pend[0], False, True, ssbk, ssbb)
                      rstd_from(rs4[:], ssbk[:], D, [ssbb], [rsb4])
                      for j in range(16):
                          STT(tmpf2[j % 2][:], xT[:, j, :], col(gsf, j, w), rs4[:], ALU.mult, ALU.mult, [xTb[j], mb2, rsb4], [tmpb2[j % 2]])
                          ACT(hB[:, j, :], tmpf2[j % 2][:], AF.Identity, [tmpb2[j % 2], mb2], [hBb[j]], bias=col(modt, 48 + j, w))
                      for gi, (tl0, ntl) in enumerate(FFN_GROUPS):
                          nch = 2 * ntl
                          for ti in range(ntl):
                              c0 = 256 * (tl0 + ti)
                              slot, wb = wload([(wt("w_g", tl0 + ti), 0)])
                              wv = wview(slot, 0, 16, 256)
                              su = sgu[ti % 2]
                              for e in range(2):
                                  bk, bkb = bank("mm")
                                  acc_fm(bk, bkb, 512, lambda k: wv[:, k, e * 128:(e + 1) * 128], lambda k: hB[:, k, :], 16, lambda k: [wb, hBb[k]])
                                  ACT(su[:, e, :], bk[:], AF.Silu, [bkb], [sgub[ti % 2]])
                              slot, wb = wload([(wt("w_u", tl0 + ti), 0)])
                              wv = wview(slot, 0, 16, 256)
                              for e in range(2):
                                  ci = 2 * ti + e
                                  bk, bkb = bank("mm")
                                  acc_fm(bk, bkb, 512, lambda k: wv[:, k, e * 128:(e + 1) * 128], lambda k: hB[:, k, :], 16, lambda k: [wb, hBb[k]])
                                  TT(mT[:, ci, :], bk[:], su[:, e, :], ALU.mult, [bkb, sgub[ti % 2]], [mTb[ci]])
                          r0 = 256 * tl0
                          lastg = gi == len(FFN_GROUPS) - 1
                          pend = None
                          for cgo in range(8):
                              slot, wb = wload([(wt("w_d", 8 * gi + cgo), 0)])
                              wv = wview(slot, 0, nch, 256)
                              for e in range(2):
                                  j = 2 * cgo + e
                                  bk, bkb = bank("mm")
                                  acc_fm(bk, bkb, 512, lambda k: wv[:, k, e * 128:(e + 1) * 128], lambda k: mT[:, k, :], nch, lambda k: [wb, mTb[k]])
                                  if pend is not None:
                                      sumsq_mm(pend[0], pend[1] == 0, False, ssbk, ssbb)
                                  STT(xT[:, j, :], bk[:], col(modt, 80 + j, w), xT[:, j, :], ALU.mult, ALU.add, [bkb, mb2, xTb[j]], [xTb[j]])
                                  if lastg:
                                      pend = (sumsq_sq(j), j)
                          if lastg:
                              sumsq_mm(pend[0], False, True, ssbk, ssbb)
                      rstd_from(rs4[:], ssbk[:], D, [ssbb], [rsb4])
                      for j in range(16):
                          STT(xT[:, j, :], xT[:, j, :], pcol[:, PC_FN + j:PC_FN + j + 1], rs4[:], ALU.mult, ALU.mult, [xTb[j], cb, rsb4], [xTb[j]])
                      for t in range(4):
                          s = ioi[0] % 2
                          ioi[0] += 1
                          for j4 in range(4):
                              tb_, tbb = bank("tr")
                              for jj in range(4):
                                  j = j4 * 4 + jj
                                  TRP(tb_[:, jj * 128:(jj + 1) * 128], xT[:, j, t * 128:(t + 1) * 128], ident, [xTb[j], cb], [tbb])
                              if j4 % 2 == 0:
                                  CPA(iot[s][:, j4 * 512:(j4 + 1) * 512], tb_[:], [tbb], [iob[s]])
                              else:
                                  CPV(iot[s][:, j4 * 512:(j4 + 1) * 512], tb_[:], [tbb], [iob[s]])
                          DMA("sp", y_d[t0 + t * 128: t0 + (t + 1) * 128, :], iot[s][:], xsem[s], [iob[s]], [], is_out=True)
            K.barrier()
        K.finish()
        print(f"[kernel] ops={K.nops} waits={K.nwait}", flush=True)
    return nc


def _const_table():
    ct = np.zeros((128, NCT), np.float32)
    p = np.arange(128)
    ct[:, CT_ID:CT_ID + 128] = np.eye(128, dtype=np.float32)
    rm = np.zeros((128, 128), np.float32)
    for m in range(128):
        if (m % 64) < 32:
            rm[m + 32, m] = -1.0
        else:
            rm[m - 32, m] = 1.0
    ct[:, CT_RM:CT_RM + 128] = rm
    j = p[:, None].astype(np.float32)
    i = p[None, :].astype(np.float32)
    ct[:, CT_DF:CT_DF + 128] = np.maximum(i - j, 0)
    ct[:, CT_MF:CT_MF + 128] = (i > j)
    ct[:, CT_DB:CT_DB + 128] = np.maximum(j - i, 0)
    ct[:, CT_MB:CT_MB + 128] = (j > i)
    ct[:, CT_I2:CT_I2 + 128] = 2.0 * np.eye(128)
    ct[:, CT_XF:CT_XF + 128] = np.broadcast_to(i + 1.0, (128, 128))
    ct[:, CT_XB:CT_XB + 128] = np.broadcast_to(128.0 - i, (128, 128))
    ct[:, CT_ZF] = 127.0 - p
    ct[:, CT_ZB] = p
    tok = np.arange(1024)
    row = (tok // 64).astype(np.float32)
    colp = (tok % 64).astype(np.float32)
    inv_freq = (np.float32(10000.0) ** (-np.arange(32, dtype=np.float32) / np.float32(32))).astype(np.float32)
    ang = np.zeros((128, 1024), np.float32)
    for d in range(128):
        pos = row if d < 64 else colp
        ang[d] = pos * inv_freq[d % 32]
    ct[:, CT_COS:CT_COS + 1024] = np.cos(ang)
    ct[:, CT_SIN:CT_SIN + 1024] = np.sin(ang)
    return ct


_NC_CACHE = {}


def kernel(x_prompt, x_sample, cache_attn_k, cache_attn_v, state_ret_fwd, state_ret_bwd, c, c_ctx,
           norm_attn, norm_ffn, w_mod, b_mod, w_in, q_norm, k_norm, ret_decay_fwd, ret_decay_bwd,
           ret_norm, w_branch_attn, w_branch_ret, w_out, w_ffn_gate, w_ffn_up, w_ffn_down, final_norm):
    f = lambda a: np.ascontiguousarray(np.asarray(a, dtype=np.float32))
    x_prompt, x_sample = f(x_prompt), f(x_sample)
    ctab = _const_table()

    def cols(v):
        return np.asarray(v, np.float32).reshape(-1, 128).T

    shared = {
        "ctab": ctab,
        "w_mod": pack_weight("w_mod", w_mod[0]), "w_in": pack_weight("w_in", w_in[0]),
        "w_ba": pack_weight("w_ba", w_branch_attn[0]), "w_br": pack_weight("w_br", w_branch_ret[0]),
        "w_out": pack_weight("w_out", w_out[0]), "w_g": pack_weight("w_g", w_ffn_gate[0]),
        "w_u": pack_weight("w_u", w_ffn_up[0]), "w_d": pack_weight("w_d", w_ffn_down[0]),
    }
    in_maps = []
    for core in range(8):
        pc = np.zeros((128, NPC), np.float32)
        pc[:, PC_BMOD:PC_BMOD + 96] = cols(b_mod[0])
        pc[:, PC_NA:PC_NA + 16] = cols(norm_attn[0])
        pc[:, PC_NF:PC_NF + 16] = cols(norm_ffn[0])
        pc[:, PC_FN:PC_FN + 16] = cols(final_norm)
        pc[:, PC_RN:PC_RN + 16] = cols(ret_norm[0])
        pc[:, PC_C:PC_C + 16] = cols(c[core])
        pc[:, PC_CC:PC_CC + 16] = cols(c_ctx)
        pc[:, PC_QN] = np.asarray(q_norm[0], np.float32)
        pc[:, PC_KN] = np.asarray(k_norm[0], np.float32)
        pc[:, PC_DF:PC_DF + 8] = np.asarray(ret_decay_fwd[0], np.float32)[None, :]
        pc[:, PC_DB:PC_DB + 8] = np.asarray(ret_decay_bwd[0], np.float32)[None, :]
        m = dict(shared)
        m["x"] = np.ascontiguousarray(np.concatenate([x_sample[core], x_prompt[2 * core], x_prompt[2 * core + 1]], axis=0))
        m["ck"] = f(cache_attn_k[core, 0]).reshape(512, 512)
        m["cv"] = f(cache_attn_v[core, 0]).reshape(512, 512)
        m["sf0"] = f(state_ret_fwd[core, 0])
        m["sb0"] = f(state_ret_bwd[core, 0])
        m["pcols"] = pc
        in_maps.append(m)

    if "nc" not in _NC_CACHE:
        _NC_CACHE["nc"] = build()
    nc = _NC_CACHE["nc"]
    res = run_bass_kernel_spmd(nc, in_maps, core_ids=list(range(8)))
    y_prompt = np.zeros((16, 256, D), np.float32)
    y_sample = np.zeros((8, 1024, D), np.float32)
    new_k = np.zeros((16, 1, 256, 4, 128), np.float32)
    new_v = np.zeros((16, 1, 256, 4, 128), np.float32)
    new_sf = np.zeros((16, 1, 8, 128, 256), np.float32)
    new_sb = np.zeros((16, 1, 8, 128, 256), np.float32)
    for core in range(8):
        r = res.results[core]
        y = r["y"]
        y_sample[core] = y[0:1024]
        y_prompt[2 * core] = y[1024:1280]
        y_prompt[2 * core + 1] = y[1280:1536]
        new_k[2 * core:2 * core + 2, 0] = r["nk"].reshape(2, 256, 4, 128)
        new_v[2 * core:2 * core + 2, 0] = r["nv"].reshape(2, 256, 4, 128)
        new_sf[2 * core:2 * core + 2, 0] = r["nsf"]
        new_sb[2 * core:2 * core + 2, 0] = r["nsb"]
    return (y_prompt, y_sample, new_k, new_v, new_sf, new_sb)
```

```python
import math
import bisect
from contextlib import ExitStack

import numpy as np
import concourse.bass as bass
import concourse.mybir as mybir
from concourse.bass_utils import run_bass_kernel_spmd

F32 = mybir.dt.float32
BF16 = mybir.dt.bfloat16
AF = mybir.ActivationFunctionType
ALU = mybir.AluOpType

D = 2048
DFF = 5632
NIN = 13312
EPS = 1e-6
NTOK = 1536
NSLOT = 4
STOP = 99
SLOTE = 4096

PC_BMOD, PC_NA, PC_NF, PC_FN, PC_RN, PC_C, PC_CC, PC_QN, PC_KN, PC_DF, PC_DB, NPC = 0, 96, 112, 128, 144, 160, 176, 192, 193, 194, 202, 210
CT_ID, CT_RM, CT_DF, CT_MF, CT_DB, CT_MB, CT_I2, CT_XF, CT_XB, CT_ZF, CT_ZB, CT_COS, CT_SIN, NCT = (
    0, 128, 256, 384, 512, 640, 768, 896, 1024, 1152, 1153, 1154, 2178, 3202)


FFN_GROUPS = [(0, 5), (5, 5), (10, 4), (14, 4), (18, 4)]


def _w_in_tiles():
    t = [(0, 16, 256 * i, 256) for i in range(12)]
    t += [(0, 16, 3072 + 128 * i, 128) for i in range(16)]
    t += [(0, 16, 5120 + 256 * i, 256) for i in range(32)]
    return t


def w_in_index(c0):
    if c0 < 3072:
        return c0 // 256
    if c0 < 5120:
        return 12 + (c0 - 3072) // 128
    return 28 + (c0 - 5120) // 256


TILES = {
    "w_mod": [(0, 16, 256 * i, 256) for i in range(48)],
    "w_in": _w_in_tiles(),
    "w_ba": [(0, 16, 256 * i, 256) for i in range(8)],
    "w_br": [(0, 16, 256 * i, 256) for i in range(8)],
    "w_out": [(0, 16, 256 * i, 256) for i in range(8)],
    "w_g": [(0, 16, 256 * i, 256) for i in range(22)],
    "w_u": [(0, 16, 256 * i, 256) for i in range(22)],
    "w_d": [(256 * tl0, 2 * ntl, 256 * cgo, 256) for (tl0, ntl) in FFN_GROUPS for cgo in range(8)],
}


def tile_offsets(name):
    offs, o = [], 0
    for (_, nk, _, nc_) in TILES[name]:
        offs.append(o)
        o += 128 * nk * nc_
    return offs, o


def pack_weight(name, W):
    W = np.asarray(W, dtype=np.float32)
    offs, total = tile_offsets(name)
    out = np.empty((total,), np.float32)
    for (r0, nk, c0, nc_), o in zip(TILES[name], offs):
        blk = W[r0:r0 + nk * 128, c0:c0 + nc_].reshape(nk, 128, nc_).transpose(1, 0, 2)
        out[o:o + 128 * nk * nc_] = blk.reshape(-1)
    return out


class Buf:
    __slots__ = ("name", "last_w", "readers", "excl", "by_eng")

    def __init__(self, name="", excl=False):
        self.name = name
        self.last_w = None
        self.readers = []
        self.excl = excl
        self.by_eng = {}


class DSem:
    __slots__ = ("h", "n", "name", "group", "exempt", "last")

    def __init__(self, h, name, group, exempt):
        self.h = h
        self.n = 0
        self.name = name
        self.group = group
        self.exempt = exempt
        self.last = None


class Op:
    __slots__ = ("eng", "fn", "deps", "signal", "cnt", "dsem", "dval", "is_dma", "epoch", "idx", "cdep")

    def __init__(self, eng, fn, epoch, idx):
        self.eng = eng
        self.fn = fn
        self.deps = []
        self.cdep = {}
        self.signal = False
        self.cnt = 0
        self.dsem = None
        self.dval = 0
        self.is_dma = False
        self.epoch = epoch
        self.idx = idx


class SigRef:
    __slots__ = ("eng", "cnt", "idx", "is_dma", "epoch", "signal")

    def __init__(self, eng, cnt, idx):
        self.eng = eng
        self.cnt = cnt
        self.idx = idx
        self.is_dma = False
        self.epoch = -1
        self.signal = True


class Kern:
    ENGS = ("pe", "act", "dve", "pool", "sp")

    def __init__(self, nc, stack):
        self.nc = nc
        self.stack = stack
        self.h = {"pe": nc.tensor, "act": nc.scalar, "dve": nc.vector, "pool": nc.gpsimd, "sp": nc.sync}
        self.sem = {e: stack.enter_context(nc.semaphore("s_" + e)) for e in self.ENGS}
        self.pending = {e: [] for e in self.ENGS}
        self.sigcnt = {e: 0 for e in self.ENGS}
        self.known = {e: {} for e in self.ENGS}
        self.lastop = {e: None for e in self.ENGS}
        self.dsems = []
        self.out_dmas = []
        self.epoch = 0
        self.finals = {}
        self.bar_deps = []
        self.bar_pending = set()
        self.nops = 0
        self.nwait = 0
        self.siglist = {e: ([], []) for e in self.ENGS}

    def dsem(self, name, group=False, exempt=False):
        d = DSem(self.stack.enter_context(self.nc.semaphore("d_" + name)), name, group, exempt)
        self.dsems.append(d)
        return d

    def _adddep(self, op, d):
        if d is op:
            return
        if (not d.is_dma) and d.epoch != op.epoch:
            idxs, cnts = self.siglist[d.eng]
            p = bisect.bisect_left(idxs, d.idx)
            if p >= len(idxs):
                return
            d = SigRef(d.eng, cnts[p], idxs[p])
        if d.is_dma:
            if op.is_dma and d.dsem is op.dsem and d.dsem.group:
                return
            if d not in op.deps:
                op.deps.append(d)
            return
        if op.eng == "pe" and d.eng == "pe":
            return
        cur = op.cdep.get(d.eng)
        if cur is None or cur.idx < d.idx:
            op.cdep[d.eng] = d

    def _track(self, op, reads, writes, bar):
        if op.eng in self.bar_pending or bar:
            self.bar_pending.discard(op.eng)
            for d in self.bar_deps:
                self._adddep(op, d)
        for b in reads:
            if b.last_w is not None:
                self._adddep(op, b.last_w)
        for b in writes:
            if b.last_w is not None:
                self._adddep(op, b.last_w)
            for r in b.readers:
                self._adddep(op, r)
        for b in list(reads) + list(writes):
            if b.excl:
                for e, o in b.by_eng.items():
                    if e != op.eng:
                        self._adddep(op, o)
                b.by_eng[op.eng] = op
        for b in reads:
            b.readers.append(op)
        for b in writes:
            b.last_w = op
            b.readers = []
        for d in op.cdep.values():
            op.deps.append(d)
            if d.epoch == op.epoch:
                d.signal = True

    def op(self, eng, fn, reads=(), writes=()):
        o = Op(eng, fn, self.epoch, self.nops)
        self.nops += 1
        self._track(o, reads, writes, False)
        self.pending[eng].append(o)
        self.lastop[eng] = o
        return o

    def dma(self, queue, fn, dsem, reads=(), writes=(), is_out=False, bar=False):
        o = Op(queue, fn, self.epoch, self.nops)
        o.is_dma = True
        self.nops += 1
        o.dsem = dsem
        dsem.n += 1
        o.dval = 16 * dsem.n
        dsem.last = o
        self._track(o, reads, writes, bar)
        self.pending[queue].append(o)
        if is_out:
            self.out_dmas.append(o)
        return o

    def flush(self):
        for e in self.ENGS:
            for o in self.pending[e]:
                if (not o.is_dma) and o.signal:
                    self.sigcnt[e] += 1
                    o.cnt = self.sigcnt[e]
                    self.siglist[e][0].append(o.idx)
                    self.siglist[e][1].append(o.cnt)
        for e in self.ENGS:
            h = self.h[e]
            known = self.known[e]
            for o in self.pending[e]:
                need = {}
                for d in o.deps:
                    if d.is_dma:
                        key, val = d.dsem.h, (16 * d.dsem.n if d.dsem.group else d.dval)
                    else:
                        key, val = self.sem[d.eng], d.cnt
                    if need.get(key, 0) < val:
                        need[key] = val
                for key, val in need.items():
                    if known.get(key, 0) < val:
                        h.wait_ge(key, val)
                        known[key] = val
                        self.nwait += 1
                ins = o.fn()
                if o.is_dma:
                    ins.then_inc(o.dsem.h, 16)
                elif o.signal:
                    ins.then_inc(self.sem[e], 1)
            self.pending[e] = []

    def barrier(self):
        fin = {}
        for e in ("pe", "act", "dve", "pool"):
            o = self.lastop[e]
            if o is not None and not o.is_dma and o.epoch == self.epoch:
                o.signal = True
                fin[e] = o
        self.finals[self.epoch] = fin
        deps = list(fin.values())
        for d in self.dsems:
            if not d.exempt and d.last is not None:
                deps.append(d.last)
        self.flush()
        self.epoch += 1
        self.bar_deps = deps
        self.bar_pending = {"pe", "act", "dve", "sp"}

    def finish(self):
        self.flush()
        h = self.h["sp"]
        fin = {}
        for o in self.out_dmas:
            if fin.get(o.dsem.h, 0) < o.dval:
                fin[o.dsem.h] = o.dval
        for key, val in fin.items():
            h.wait_ge(key, val)


def build():
    nc = bass.Bass("TRN2", target_bir_lowering=False)

    def din(name, shape):
        return nc.dram_tensor(name, list(shape), F32, kind="ExternalInput").ap()

    def dout(name, shape):
        return nc.dram_tensor(name, list(shape), F32, kind="ExternalOutput").ap()

    x_d = din("x", [NTOK, D])
    ck_d = din("ck", [512, 512])
    cv_d = din("cv", [512, 512])
    sf_d = din("sf0", [8, 128, 256])
    sb_d = din("sb0", [8, 128, 256])
    pcols_d = din("pcols", [128, NPC])
    ctab_d = din("ctab", [128, NCT])
    wflat = {}
    woffs = {}
    for nm in TILES:
        woffs[nm], tot = tile_offsets(nm)
        wflat[nm] = din(nm, [tot])

    def wt(nm, idx):
        (_, nk, _, nc_) = TILES[nm][idx]
        o = woffs[nm][idx]
        return wflat[nm][o:o + 128 * nk * nc_].rearrange("(p e) -> p e", p=128)

    def win_t(c0):
        return wt("w_in", w_in_index(c0))
    y_d = dout("y", [NTOK, D])
    nk_d = dout("nk", [512, 512])
    nv_d = dout("nv", [512, 512])
    nsf_d = dout("nsf", [2, 8, 128, 256])
    nsb_d = dout("nsb", [2, 8, 128, 256])


    with ExitStack() as st0:
        K = Kern(nc, st0)

        uniq = [0]

        def sbt(st, name, shape, dt):
            uniq[0] += 1
            return st.enter_context(nc.sbuf_tensor(f"sb_{name}_{uniq[0]}", list(shape), dt))

        def MM(out, lhsT, rhs, start, stop, reads, writes):
            K.op("pe", lambda: nc.tensor.matmul(out, lhsT=lhsT, rhs=rhs, start=start, stop=stop), reads, writes)

        def TRP(out, in_, ident, reads, writes):
            K.op("pe", lambda: nc.tensor.transpose(out, in_, ident), reads, writes)

        def ACT(out, in_, func, reads, writes, scale=1.0, bias=0.0, accum_out=None):
            if accum_out is None:
                K.op("act", lambda: nc.scalar.activation(out=out, in_=in_, func=func, bias=bias, scale=scale), reads, writes)
            else:
                K.op("act", lambda: nc.scalar.activation(out=out, in_=in_, func=func, bias=bias, scale=scale, accum_out=accum_out), reads, writes)

        def TS(out, in0, s1, s2, op0, op1, reads, writes):
            if s2 is None:
                K.op("dve", lambda: nc.vector.tensor_scalar(out=out, in0=in0, scalar1=s1, scalar2=None, op0=op0), reads, writes)
            else:
                K.op("dve", lambda: nc.vector.tensor_scalar(out=out, in0=in0, scalar1=s1, scalar2=s2, op0=op0, op1=op1), reads, writes)

        def TT(out, in0, in1, op, reads, writes):
            K.op("dve", lambda: nc.vector.tensor_tensor(out=out, in0=in0, in1=in1, op=op), reads, writes)

        def STT(out, in0, scalar, in1, op0, op1, reads, writes):
            K.op("dve", lambda: nc.vector.scalar_tensor_tensor(out=out, in0=in0, scalar=scalar, in1=in1, op0=op0, op1=op1), reads, writes)

        def CPV(out, in_, reads, writes):
            K.op("dve", lambda: nc.vector.tensor_copy(out=out, in_=in_), reads, writes)

        def CPA(out, in_, reads, writes):
            K.op("act", lambda: nc.scalar.copy(out=out, in_=in_), reads, writes)

        def DMA(queue, out, in_, dsem, reads, writes, is_out=False, bar=False):
            eng = nc.sync if queue == "sp" else nc.gpsimd
            K.dma(queue, lambda: eng.dma_start(out=out, in_=in_), dsem, reads, writes, is_out=is_out, bar=bar)

        ctab = sbt(st0, "ctab", [128, NCT], F32)
        pcol = sbt(st0, "pcol", [128, NPC], F32)
        identb = sbt(st0, "identb", [128, 128], BF16)
        onesb = sbt(st0, "onesb", [128, 128], BF16)
        sct = sbt(st0, "sct", [128, 16, 2], BF16)
        modt = sbt(st0, "modt", [128, 96, 2], F32)
        gsa = sbt(st0, "gsa", [128, 16, 2], F32)
        gsf = sbt(st0, "gsf", [128, 16, 2], F32)
        lg = sbt(st0, "lg", [128, 16], F32)
        gch = sbt(st0, "gch", [128, 16], F32)
        zt = sbt(st0, "zt", [128, 16], F32)
        tmp16 = sbt(st0, "tmp16", [128, 16], F32)
        wsl = [sbt(st0, f"wsl{i}", [128, SLOTE], BF16) for i in range(NSLOT)]
        oaT = sbt(st0, "oaT", [128, 16, 1024], BF16)
        orT = sbt(st0, "orT", [128, 16, 1024], BF16)
        pbk = [st0.enter_context(nc.psum_tensor(f"pbk{i}", [128, 512], F32)) for i in range(8)]
        pbb = [Buf(f"pbk{i}", excl=True) for i in range(8)]
        ident = ctab[:, CT_ID:CT_ID + 128]

        cb = Buf("const")
        csem = K.dsem("const", group=True)
        mb = Buf("mod")
        wbuf = [Buf(f"w{i}") for i in range(NSLOT)]
        wsem = [K.dsem(f"w{i}", exempt=True) for i in range(NSLOT)]
        oab = [[Buf() for _ in range(2)] for _ in range(16)]
        orb = [[Buf() for _ in range(8)] for _ in range(16)]
        osem = K.dsem("out")
        xsem = [K.dsem("x0"), K.dsem("x1")]
        ssem = [K.dsem(f"st{i}") for i in range(4)]
        cvsem = K.dsem("cv")

        rot = {"mm": 0, "aux": 0, "tr": 0, "w": 0, "mm2": 0, "bo": 0}

        def bank(group):
            if group == "mm":
                i = rot["mm"] % 4
            elif group == "mm2":
                i = (0, 1, 5)[rot["mm2"] % 3]
            elif group == "bo":
                i = 2 + rot["bo"] % 2
            elif group == "aux":
                i = 4 + rot["aux"] % 2
            else:
                i = 6 + rot["tr"] % 2
            rot[group] += 1
            return pbk[i], pbb[i]

        def wload(parts):
            s = rot["w"] % NSLOT
            rot["w"] += 1
            for src, off in parts:
                ne = src.shape[1]
                DMA("pool", wsl[s][:, off:off + ne], src, wsem[s], [], [wbuf[s]])
            return wsl[s], wbuf[s]

        def wview(slot, off, nk, ncol):
            return slot[:, off:off + nk * ncol].rearrange("p (k n) -> p k n", k=nk)

        def rstd_from(dst, src, n, reads, writes):
            ACT(dst, src, AF.Ln, reads, writes, scale=1.0 / n, bias=EPS)
            ACT(dst, dst, AF.Exp, writes, writes, scale=-0.5)

        DMA("sp", ctab[:], ctab_d[:, :], csem, [], [cb])
        DMA("sp", pcol[:], pcols_d[:, :], csem, [], [cb])
        K.op("dve", lambda: nc.vector.memset(onesb[:], 1.0), [], [cb])
        CPA(identb[:], ident, [cb], [cb])
        if STOP <= -5:
            K.finish()
            return nc
        ACT(sct[:, :, 0], pcol[:, PC_C:PC_C + 16], AF.Silu, [cb], [mb])
        ACT(sct[:, :, 1], pcol[:, PC_CC:PC_CC + 16], AF.Silu, [cb], [mb])
        ACT(tmp16[:], pcol[:, PC_DF:PC_DF + 16], AF.Exp, [cb], [mb], scale=-1.0)
        ACT(tmp16[:], tmp16[:], AF.Ln, [mb], [mb], bias=1.0)
        TS(lg[:], tmp16[:], -1.0, None, ALU.mult, None, [mb], [mb])
        ACT(gch[:], lg[:], AF.Exp, [mb], [mb], scale=128.0)
        if STOP <= -4:
            K.finish()
            return nc
        for h in range(8):
            ACT(zt[:, h:h + 1], ctab[:, CT_ZF:CT_ZF + 1], AF.Exp, [cb, mb], [mb], scale=lg[:, h:h + 1])
            ACT(zt[:, 8 + h:9 + h], ctab[:, CT_ZB:CT_ZB + 1], AF.Exp, [cb, mb], [mb], scale=lg[:, 8 + h:9 + h])
        if STOP <= -3:
            K.finish()
            return nc
        mb2 = Buf("mod2")
        modq = list(range(48 if STOP >= 0 else 0))

        def mod_step(n=1, bk_=None):
            for _ in range(n):
                if not modq:
                    return
                cblk = modq.pop(0)
                slot, wb = wload([(wt("w_mod", cblk), 0)])
                wv = wview(slot, 0, 16, 256)
                ab, abb = bk_ if bk_ is not None else bank("aux")
                mp = ab[:, 0:4].rearrange("p (c w) -> p c w", w=2)
                for j in range(2):
                    for k in range(16):
                        MM(mp[:, j, :], wv[:, k, j * 128:(j + 1) * 128], sct[:, k, :], k == 0, k == 15, [wb, mb], [abb])
                tgt = mb if cblk < 16 else mb2
                for w in range(2):
                    TT(modt[:, 2 * cblk:2 * cblk + 2, w], mp[:, :, w], pcol[:, PC_BMOD + 2 * cblk:PC_BMOD + 2 * cblk + 2], ALU.add,
                       [abb, cb], [tgt])

        gsf_done = [False]

        def mod_flush():
            mod_step(len(modq))
            if not gsf_done[0]:
                gsf_done[0] = True
                for w in range(2):
                    STT(gsf[:, :, w], modt[:, 64:80, w], 1.0, pcol[:, PC_NF:PC_NF + 16], ALU.add, ALU.mult, [mb2, cb], [mb2])

        gsa_done = [False]

        def mod_first(n):
            if gsa_done[0]:
                return
            k_ = min(n, 16 - (48 - len(modq)))
            if k_ > 0:
                mod_step(k_)
            if 48 - len(modq) >= 16:
                gsa_done[0] = True
                for w in range(2):
                    STT(gsa[:, :, w], modt[:, 16:32, w], 1.0, pcol[:, PC_NA:PC_NA + 16], ALU.add, ALU.mult, [mb, cb], [mb])

        def col(t3, j, w):
            return t3[:, j, w:w + 1]

        SEGS = [
            dict(name="S", tok0=0, ntok=1024, w=0, latent=True,
                 qblocks=[(0, 512, list(range(12))), (512, 512, list(range(12)))], nkt=12,
                 seqs=[(0, 8)]),
            dict(name="P", tok0=1024, ntok=512, w=1, latent=False,
                 qblocks=[(0, 256, [0, 1]), (256, 256, [2, 3])], nkt=4,
                 seqs=[(0, 2), (2, 2)]),
        ]
        ATT_SCALE = 128 ** -0.5

        for seg in SEGS:
            if STOP < 1 or (STOP < 5 and seg["name"] == "P"):
                continue
            tok0, ntok, w, latent = seg["tok0"], seg["ntok"], seg["w"], seg["latent"]
            ntile = ntok // 128
            nblk = ntok // 512
            with ExitStack() as stS:
                hT = sbt(stS, "hT", [128, 16, 1024], BF16)
                hTb = [[Buf() for _ in range(16)] for _ in range(8)]

                def hreads(k, nb):
                    return [hTb[4 * nb + i][k] for i in range(4)]

                with ExitStack() as st1:
                    xsl = [sbt(st1, f"xsl{i}", [128, D], F32) for i in range(2)]
                    xn = [sbt(st1, f"xn{i}", [128, D], BF16) for i in range(2)]
                    junk = sbt(st1, "junk", [128, D], BF16)
                    ss = sbt(st1, "ss1", [128, 4], F32)
                    xb = [Buf(), Buf()]
                    xnb = [Buf(), Buf()]
                    ssb = [Buf(), Buf()]
                    jb = Buf()
                    def p1_A(t):
                        s = t % 2
                        DMA("sp", xsl[s][:], x_d[tok0 + t * 128: tok0 + (t + 1) * 128, :], xsem[s], [], [xb[s]])
                        ACT(junk[:], xsl[s][:], AF.Square, [xb[s]], [jb, ssb[s]], accum_out=ss[:, s:s + 1])
                        rstd_from(ss[:, 2 + s:3 + s], ss[:, s:s + 1], D, [ssb[s]], [ssb[s]])
                        TS(xn[s][:], xsl[s][:], ss[:, 2 + s:3 + s], None, ALU.mult, None, [xb[s], ssb[s]], [xnb[s]])

                    def p1_B(t):
                        s = t % 2
                        for k8 in range(2):
                            tb_, tbb = bank("tr")
                            tpv = tb_[:].bitcast(BF16)
                            for kk in range(8):
                                k = k8 * 8 + kk
                                TRP(tpv[:, kk * 128:(kk + 1) * 128], xn[s][:, k * 128:(k + 1) * 128], identb[:], [xnb[s], cb], [tbb])
                            dst = hT[:, k8 * 8:(k8 + 1) * 8, t * 128:(t + 1) * 128]
                            src = tpv.rearrange("p (k t) -> p k t", k=8)
                            hb8 = [hTb[t][k8 * 8 + kk] for kk in range(8)]
                            if k8 == 0:
                                CPA(dst, src, [tbb], hb8)
                            else:
                                CPV(dst, src, [tbb], hb8)
                        mod_first(2)

                    p1_A(0)
                    for t in range(ntile):
                        if t + 1 < ntile:
                            p1_A(t + 1)
                        p1_B(t)
                    mod_first(16)
                    for k in range(16):
                        hk = hT[:, k, 0:ntok]
                        hbk = [hTb[t][k] for t in range(ntile)]
                        if k % 2 == 0:
                            ACT(hk, hk, AF.Identity, hbk + [mb], hbk, scale=col(gsa, k, w), bias=col(modt, k, w))
                        else:
                            TS(hk, hk, col(gsa, k, w), col(modt, k, w), ALU.mult, ALU.add, hbk + [mb], hbk)
                K.barrier()

                def pipeline(units):
                    st_ = [None] * len(units)
                    if units:
                        st_[0] = units[0][0]()
                    for u in range(len(units)):
                        if u + 1 < len(units):
                            st_[u + 1] = units[u + 1][0]()
                        units[u][1](st_[u])

                def acc_fm(bk, bkb, ncols, lhs_fn, rhs_fn, nk, reads_fn):
                    for k in range(nk):
                        MM(bk[:, 0:ncols], lhs_fn(k), rhs_fn(k), k == 0, k == nk - 1, reads_fn(k), [bkb])

                with ExitStack() as st2:
                  if STOP >= 2:
                      nkt = seg["nkt"]
                      kT = sbt(st2, "kT", [128, 4, nkt * 128], BF16)
                      vall = sbt(st2, "vall", [128, nkt, 512], BF16)
                      qT = sbt(st2, "qT", [128, 4, ntok], BF16)
                      sqt_ = [sbt(st2, f"sqt{i}", [128, 512], BF16) for i in range(2)]
                      rst_ = [sbt(st2, f"rst{i}", [128, 512], F32) for i in range(2)]
                      knt_ = [sbt(st2, f"knt{i}", [128, 512], F32) for i in range(2)]
                      t1_ = [sbt(st2, f"t1{i}", [128, 512], F32) for i in range(2)]
                      t2_ = [sbt(st2, f"t2{i}", [128, 512], F32) for i in range(2)]
                      nrm_i = [0]
                      pT = [sbt(st2, f"pT{i}", [128, 512], BF16) for i in range(3)]
                      rdt = [sbt(st2, f"rdt{i}", [128, 512], F32) for i in range(2)]
                      if latent:
                          cks = sbt(st2, "cks", [128, 512], F32)
                      else:
                          kout = sbt(st2, "kout", [128, 4, 512], F32)
                          vout = sbt(st2, "vout", [128, 4, 512], F32)
                      kTb = [[Buf() for _ in range(12)] for _ in range(4)]
                      vb = [[Buf() for _ in range(2)] for _ in range(12)]
                      qTb = [[Buf() for _ in range(2)] for _ in range(4)]
                      ckb, koutb, voutb = (Buf() for _ in range(3))
                      sqb_, rsb_, knb_, t1b_, t2b_ = ([Buf(), Buf()] for _ in range(5))
                      rdb = [Buf(), Buf()]
                      pTb = [Buf() for _ in range(3)]
                      cksem = cvsem

                      def qk_norm_A(bk, bkb, normcol):
                          n = 512
                          pi = nrm_i[0] % 2
                          nrm_i[0] += 1
                          sqt, rst, knt = sqt_[pi], rst_[pi], knt_[pi]
                          sqb, rsb, knb = sqb_[pi], rsb_[pi], knb_[pi]
                          ACT(sqt[:], bk[:, 0:n], AF.Square, [bkb], [sqb])
                          ab, abb = pbk[7], pbb[7]
                          MM(ab[:, 0:n], onesb[:], sqt[:], True, True, [sqb, cb], [abb])
                          rstd_from(rst[:], ab[:, 0:n], 128, [abb], [rsb])
                          STT(knt[:], bk[:, 0:n], normcol, rst[:], ALU.mult, ALU.mult, [bkb, rsb, cb], [knb])
                          return pi

                      def qk_norm_B(pi, dst, dstbufs, tokc0, is_k_out=None):
                          n = 512
                          knt, t1, t2 = knt_[pi], t1_[pi], t2_[pi]
                          knb, t1b, t2b = knb_[pi], t1b_[pi], t2b_[pi]
                          if latent:
                              ab2, abb2 = bank("aux")
                              MM(ab2[:, 0:n], ctab[:, CT_RM:CT_RM + 128], knt[:], True, True, [knb, cb], [abb2])
                              TT(t1[:], knt[:], ctab[:, CT_COS + tokc0:CT_COS + tokc0 + n], ALU.mult, [knb, cb], [t1b])
                              TT(t2[:], ab2[:, 0:n], ctab[:, CT_SIN + tokc0:CT_SIN + tokc0 + n], ALU.mult, [abb2, cb], [t2b])
                              TT(dst, t1[:], t2[:], ALU.add, [t1b, t2b], dstbufs)
                          else:
                              CPA(dst, knt[:], [knb], dstbufs)
                              if is_k_out is not None:
                                  g = is_k_out
                                  tb_, tbb = bank("tr")
                                  for t in range(4):
                                      TRP(tb_[:, t * 128:(t + 1) * 128], knt[:, t * 128:(t + 1) * 128], ident, [knb, cb], [tbb])
                                  CPV(kout[:, :, g * 128:(g + 1) * 128], tb_[:].rearrange("p (t d) -> p t d", t=4), [tbb], [koutb])

                      def pipeline3(units):
                          n_ = len(units)
                          hs = [None] * n_
                          as_ = [None] * n_
                          if n_:
                              hs[0] = units[0][0]()
                          for u in range(n_):
                              if u + 1 < n_:
                                  hs[u + 1] = units[u + 1][0]()
                              as_[u] = units[u][1](hs[u])
                              if u >= 1:
                                  units[u - 1][2](as_[u - 1])
                          if n_:
                              units[n_ - 1][2](as_[n_ - 1])

                      def pipeline(units):
                          st_ = [None] * len(units)
                          if units:
                              st_[0] = units[0][0]()
                          for u in range(len(units)):
                              if u + 1 < len(units):
                                  st_[u + 1] = units[u + 1][0]()
                              units[u][1](st_[u])

                      for j in range(2):
                          slot, wb = wload([(win_t(2048 + 256 * j), 0)])
                          wv = wview(slot, 0, 16, 256)
                          units = []
                          for hh in range(2):
                              for nb in range(nblk):
                                  def head_fn(hh=hh, nb=nb, wv=wv, wb=wb):
                                      bk, bkb = bank("mm")
                                      acc_fm(bk, bkb, 512, lambda k: wv[:, k, hh * 128:(hh + 1) * 128],
                                             lambda k: hT[:, k, nb * 512:(nb + 1) * 512], 16, lambda k: [wb] + hreads(k, nb))
                                      return bk, bkb

                                  def tailA_fn(st_):
                                      return qk_norm_A(st_[0], st_[1], pcol[:, PC_KN:PC_KN + 1])

                                  def tailB_fn(pi, hh=hh, nb=nb, j=j):
                                      g = 2 * j + hh
                                      qk_norm_B(pi, kT[:, g, nb * 512:(nb + 1) * 512], [kTb[g][4 * nb + i] for i in range(4)], nb * 512,
                                                is_k_out=(None if latent else g))
                                  units.append((head_fn, tailA_fn, tailB_fn))
                          pipeline3(units)
                          if latent:
                              mod_step(1)
                      if not latent:
                          DMA("sp", nk_d.rearrange("(t p) c -> p t c", p=128), kout[:], osem, [koutb], [], is_out=True)
                      for j in range(2):
                          slot, wb = wload([(win_t(2560 + 256 * j), 0)])
                          wv = wview(slot, 0, 16, 256)
                          for t in range(ntile):
                              bk, bkb = bank("mm")
                              for k in range(16):
                                  MM(bk[:, 0:256], hT[:, k, t * 128:(t + 1) * 128], wv[:, k, :], k == 0, k == 15, [wb, hTb[t][k]], [bkb])
                              CPA(vall[:, t, j * 256:(j + 1) * 256], bk[:, 0:256], [bkb], [vb[t][j]])
                              if not latent:
                                  CPV(vout[:, t, j * 256:(j + 1) * 256], bk[:, 0:256], [bkb], [voutb])
                          if latent:
                              mod_step(1)
                      if not latent:
                          DMA("sp", nv_d.rearrange("(t p) c -> p t c", p=128), vout[:], osem, [voutb], [], is_out=True)
                      if latent:
                          DMA("pool", vall[:, 8:12, :], cv_d.rearrange("(c p) n -> p c n", p=128), cksem, [],
                              [vb[8 + c][jj] for c in range(4) for jj in range(2)], bar=True)
                          for c in range(4):
                              DMA("sp", cks[:], ck_d[c * 128:(c + 1) * 128, :], xsem[0], [], [ckb])
                              tb_, tbb = bank("tr")
                              for g in range(4):
                                  TRP(tb_[:, g * 128:(g + 1) * 128], cks[:, g * 128:(g + 1) * 128], ident, [ckb, cb], [tbb])
                              CPV(kT[:, :, 1024 + c * 128: 1024 + (c + 1) * 128], tb_[:].rearrange("p (g t) -> p g t", g=4),
                                  [tbb], [kTb[g][8 + c] for g in range(4)])
                      obanks = [(pbk[3], pbb[3]), (pbk[6], pbb[6])]
                      dbanks = [(pbk[4], pbb[4]), (pbk[5], pbb[5])]
                      ucount = [0]

                      def q_proj_chain(g, j):
                          if True:
                              slot, wb = wload([(win_t(512 * g + 256 * j), 0)])
                              wv = wview(slot, 0, 16, 256)
                              units = []
                              for hh in range(2):
                                  for nb in range(nblk):
                                      def head_fn(hh=hh, nb=nb, wv=wv, wb=wb):
                                          bk, bkb = bank("mm")
                                          acc_fm(bk, bkb, 512, lambda k: wv[:, k, hh * 128:(hh + 1) * 128],
                                                 lambda k: hT[:, k, nb * 512:(nb + 1) * 512], 16, lambda k: [wb] + hreads(k, nb))
                                          return bk, bkb

                                      def tailA_fn(st_):
                                          return qk_norm_A(st_[0], st_[1], pcol[:, PC_QN:PC_QN + 1])

                                      def tailB_fn(pi, hh=hh, nb=nb, j=j):
                                          hl = 2 * j + hh
                                          qk_norm_B(pi, qT[:, hl, nb * 512:(nb + 1) * 512], [qTb[hl][nb]], nb * 512)
                                      units.append((head_fn, tailA_fn, tailB_fn))
                              pipeline3(units)

                      def q_attn(g, j):
                          if True:
                              steps = []
                              for hh in range(2):
                                  hl = 2 * j + hh
                                  for qi, (q0, qn, kcs) in enumerate(seg["qblocks"]):
                                      un = ucount[0]
                                      ucount[0] += 1
                                      for i, kc in enumerate(kcs):
                                          steps.append((hl, q0, qn, i, kc, len(kcs), un))

                              def emit_st(idx):
                                  hl, q0, qn, i, kc, nkc, un = steps[idx]
                                  MM(pbk[idx % 3][:, 0:qn], kT[:, g, kc * 128:(kc + 1) * 128], qT[:, hl, q0:q0 + qn], True, True,
                                     [kTb[g][kc], qTb[hl][q0 // 512]], [pbb[idx % 3]])
                              for idx in range(min(2, len(steps))):
                                  emit_st(idx)
                              for idx in range(len(steps)):
                                  hl, q0, qn, i, kc, nkc, un = steps[idx]
                                  if idx + 2 < len(steps):
                                      emit_st(idx + 2)
                                  ob, obb = obanks[un % 2]
                                  db, dbb = dbanks[un % 2]
                                  p_ = pT[idx % 3]
                                  ACT(p_[:, 0:qn], pbk[idx % 3][:, 0:qn], AF.Exp, [pbb[idx % 3]], [pTb[idx % 3]], scale=ATT_SCALE)
                                  MM(ob[:, 0:qn], vall[:, kc, g * 128:(g + 1) * 128], p_[:, 0:qn], i == 0, i == nkc - 1,
                                     [vb[kc][g // 2], pTb[idx % 3]], [obb])
                                  MM(db[:, 0:qn], onesb[:], p_[:, 0:qn], i == 0, i == nkc - 1, [cb, pTb[idx % 3]], [dbb])
                                  if i == nkc - 1:
                                      head = 4 * g + hl
                                      rd_ = rdt[un % 2]
                                      K.op("dve", lambda db=db, qn=qn, rd_=rd_: nc.vector.reciprocal(out=rd_[:, 0:qn], in_=db[:, 0:qn]),
                                           [dbb], [rdb[un % 2]])
                                      TT(oaT[:, head, q0:q0 + qn], ob[:, 0:qn], rd_[:, 0:qn], ALU.mult, [obb, rdb[un % 2]],
                                         [oab[head][q0 // 512]])
                              if latent:
                                  mod_step(1)
                      qtiles = [(g, j) for g in range(4) for j in range(2)]
                      q_proj_chain(*qtiles[0])
                      for qi_, (g, j) in enumerate(qtiles):
                          if qi_ + 1 < len(qtiles):
                              q_proj_chain(*qtiles[qi_ + 1])
                          q_attn(g, j)
                K.barrier()

                with ExitStack() as st3:
                  if STOP >= 3:
                      def two(name, shape, dt):
                          return [sbt(st3, f"{name}{i}", shape, dt) for i in range(2)]
                      qrT = two("qrT", [128, 1024], BF16)
                      qf = two("qf", [128, 1024], BF16)
                      qbw = two("qbw", [128, 1024], BF16)
                      krT = two("krT", [128, 1024], BF16)
                      kzf = two("kzf", [128, 8, 128], BF16)
                      kzb = two("kzb", [128, 8, 128], BF16)
                      vr = two("vr", [128, 8, 256], BF16)
                      sgT = two("sgT", [128, 2, 1024], BF16)
                      DC4 = two("DC4", [128, 4, 128], F32)
                      XF4 = sbt(st3, "XF4", [128, 4, 128], F32)
                      XB4 = sbt(st3, "XB4", [128, 4, 128], F32)
                      tmA = sbt(st3, "tmA", [128, 128], F32)
                      tmB = sbt(st3, "tmB", [128, 128], F32)
                      Sfb = sbt(st3, "Sfb", [128, 8, 256], BF16)
                      Sbb = sbt(st3, "Sbb", [128, 8, 256], BF16)
                      S32 = [sbt(st3, f"S32_{i}", [128, 256], F32) for i in range(4)]
                      sc = [sbt(st3, f"sc{i}", [128, 512], BF16) for i in range(2)]
                      ont = [sbt(st3, f"ont{i}", [128, 256], BF16) for i in range(2)]
                      st6 = sbt(st3, "st6", [128, 2, 6], F32)
                      mv = sbt(st3, "mv", [128, 2, 4], F32)

                      def twob(n=None):
                          if n is None:
                              return [Buf(), Buf()]
                          return [[Buf() for _ in range(n)] for _ in range(2)]
                      qrb, qfb, qbb, krb = twob(2), twob(2), twob(2), twob(2)
                      kzfb, kzbb, dcb = twob(), twob(), twob()
                      vrb = twob(8)
                      sgb = [[[Buf() for _ in range(2)] for _ in range(2)] for _ in range(2)]
                      xfb, tmb = Buf(), Buf()
                      Sfbb = [Buf() for _ in range(8)]
                      Sbbb = [Buf() for _ in range(8)]
                      S32b = [Buf() for _ in range(4)]
                      scb = [Buf(), Buf()]
                      onb = [Buf(), Buf()]
                      stb_ = [Buf(), Buf()]
                      nchunk = ntok // 128

                      def proj_gen(h):
                          pp = h % 2
                          lgf = lg[:, h:h + 1]
                          lgb = lg[:, 8 + h:9 + h]
                          ACT(XF4[:], ctab[:, CT_XF:CT_XF + 128].unsqueeze(1).to_broadcast([128, 4, 128]), AF.Exp, [cb, mb], [xfb], scale=lgf)
                          ACT(XB4[:], ctab[:, CT_XB:CT_XB + 128].unsqueeze(1).to_broadcast([128, 4, 128]), AF.Exp, [cb, mb], [xfb], scale=lgb)
                          ACT(tmA[:], ctab[:, CT_DF:CT_DF + 128], AF.Exp, [cb, mb], [tmb], scale=lgf)
                          ACT(tmB[:], ctab[:, CT_DB:CT_DB + 128], AF.Exp, [cb, mb], [tmb], scale=lgb)
                          TT(tmA[:], tmA[:], ctab[:, CT_MF:CT_MF + 128], ALU.mult, [tmb, cb], [tmb])
                          TT(tmB[:], tmB[:], ctab[:, CT_MB:CT_MB + 128], ALU.mult, [tmb, cb], [tmb])
                          TT(tmA[:], tmA[:], tmB[:], ALU.add, [tmb], [tmb])
                          TT(tmA[:], tmA[:], ctab[:, CT_I2:CT_I2 + 128], ALU.add, [tmb, cb], [tmb])
                          CPV(DC4[pp][:], tmA[:].unsqueeze(1).to_broadcast([128, 4, 128]), [tmb], [dcb[pp]])
                          slot1, wb1 = wload([(win_t(3072 + 128 * h), 0),
                                              (win_t(4096 + 128 * h), 2048)])
                          wq = wview(slot1, 0, 16, 128)
                          wk = wview(slot1, 2048, 16, 128)
                          for nb in range(nblk):
                              blk = slice(nb * 512, (nb + 1) * 512)
                              bk, bkb = bank("mm2")
                              acc_fm(bk, bkb, 512, lambda k: wq[:, k, :], lambda k: hT[:, k, blk], 16, lambda k: [wb1] + hreads(k, nb))
                              CPA(qrT[pp][:, blk], bk[:], [bkb], [qrb[pp][nb]])
                              TT(qf[pp][:, blk], bk[:], XF4[:].rearrange("p a b -> p (a b)"), ALU.mult, [bkb, xfb], [qfb[pp][nb]])
                              TT(qbw[pp][:, blk], bk[:], XB4[:].rearrange("p a b -> p (a b)"), ALU.mult, [bkb, xfb], [qbb[pp][nb]])
                              yield
                              bk, bkb = bank("mm2")
                              acc_fm(bk, bkb, 512, lambda k: wk[:, k, :], lambda k: hT[:, k, blk], 16, lambda k: [wb1] + hreads(k, nb))
                              ACT(krT[pp][:, blk], bk[:], AF.Identity, [bkb], [krb[pp][nb]], scale=128 ** -0.5)
                              yield
                          if latent:
                              mod_step(1, (pbk[4], pbb[4]))
                          slot2, wb2 = wload([(win_t(5120 + 256 * h), 0)])
                          wv = wview(slot2, 0, 16, 256)
                          for t in range(ntile):
                              bk, bkb = bank("mm2")
                              for k in range(16):
                                  MM(bk[:, 0:256], hT[:, k, t * 128:(t + 1) * 128], wv[:, k, :], k == 0, k == 15, [wb2, hTb[t][k]], [bkb])
                              if t % 2 == 0:
                                  CPA(vr[pp][:, t, :], bk[:, 0:256], [bkb], [vrb[pp][t]])
                              else:
                                  CPV(vr[pp][:, t, :], bk[:, 0:256], [bkb], [vrb[pp][t]])
                              yield
                          if latent:
                              mod_step(1, (pbk[4], pbb[4]))
                          slot3, wb3 = wload([(win_t(7168 + 256 * h), 0)])
                          wgv = wview(slot3, 0, 16, 256)
                          for e in range(2):
                              for nb in range(nblk):
                                  blk = slice(nb * 512, (nb + 1) * 512)
                                  bk, bkb = bank("mm2")
                                  acc_fm(bk, bkb, 512, lambda k: wgv[:, k, e * 128:(e + 1) * 128], lambda k: hT[:, k, blk], 16,
                                         lambda k: [wb3] + hreads(k, nb))
                                  ACT(sgT[pp][:, e, blk], bk[:], AF.Silu, [bkb], [sgb[pp][e][nb]])
                                  yield
                          if latent:
                              mod_step(1, (pbk[4], pbb[4]))
                          tb_, tbb = bank("tr")
                          tpv = tb_[:].bitcast(BF16)
                          for c in range(nchunk):
                              TRP(tpv[:, c * 128:(c + 1) * 128], krT[pp][:, c * 128:(c + 1) * 128], identb[:], [krb[pp][c // 4], cb], [tbb])
                          TS(kzf[pp][:, 0:nchunk, :], tpv[:, 0:nchunk * 128].rearrange("p (c d) -> p c d", c=nchunk), zt[:, h:h + 1], None,
                             ALU.mult, None, [tbb, mb], [kzfb[pp]])
                          ACT(kzb[pp][:, 0:nchunk, :], tpv[:, 0:nchunk * 128].rearrange("p (c d) -> p c d", c=nchunk), AF.Identity, [tbb, mb],
                              [kzbb[pp]], scale=zt[:, 8 + h:9 + h])
                          yield

                      def chain_gen(h):
                          pp = h % 2
                          for si, (c0, ncs) in enumerate(seg["seqs"]):
                              Sf, Sfx = S32[2 * si], S32b[2 * si]
                              Sb_, Sbx = S32[2 * si + 1], S32b[2 * si + 1]
                              if latent:
                                  DMA("sp", Sf[:], sf_d[h, :, :], ssem[2 * si], [], [Sfx])
                                  DMA("sp", Sb_[:], sb_d[h, :, :], ssem[2 * si + 1], [], [Sbx])
                              else:
                                  K.op("dve", lambda Sf=Sf: nc.vector.memset(Sf[:], 0.0), [], [Sfx])
                                  K.op("dve", lambda Sb_=Sb_: nc.vector.memset(Sb_[:], 0.0), [], [Sbx])
                              CPA(Sfb[:, c0, :], Sf[:], [Sfx], [Sfbb[c0]])
                              CPA(Sbb[:, c0 + ncs - 1, :], Sb_[:], [Sbx], [Sbbb[c0 + ncs - 1]])
                              last_f = ncs if not latent else ncs - 1
                              for i in range(last_f):
                                  c = c0 + i
                                  cb_ = c0 + ncs - 1 - i
                                  ab, abb = pbk[4], pbb[4]
                                  MM(ab[:, 0:256], kzf[pp][:, c, :], vr[pp][:, c, :], True, True, [kzfb[pp], vrb[pp][c]], [abb])
                                  MM(ab[:, 256:512], kzb[pp][:, cb_, :], vr[pp][:, cb_, :], True, True, [kzbb[pp], vrb[pp][cb_]], [abb])
                                  STT(Sf[:], Sf[:], gch[:, h:h + 1], ab[:, 0:256], ALU.mult, ALU.add, [Sfx, abb, mb], [Sfx])
                                  STT(Sb_[:], Sb_[:], gch[:, 8 + h:9 + h], ab[:, 256:512], ALU.mult, ALU.add, [Sbx, abb, mb], [Sbx])
                                  if i + 1 < ncs:
                                      CPA(Sfb[:, c + 1, :], Sf[:], [Sfx], [Sfbb[c + 1]])
                                      CPA(Sbb[:, cb_ - 1, :], Sb_[:], [Sbx], [Sbbb[cb_ - 1]])
                                  yield
                              if not latent:
                                  DMA("sp", nsf_d[si, h, :, :], Sf[:], ssem[2 * si], [Sfx], [], is_out=True)
                                  DMA("sp", nsb_d[si, h, :, :], Sb_[:], ssem[2 * si + 1], [Sbx], [], is_out=True)
                          for c4 in range(nchunk // 4):
                              bk, bkb = bank("bo")
                              for i in range(4):
                                  c = c4 * 4 + i
                                  MM(bk[:, i * 128:(i + 1) * 128], krT[pp][:, c * 128:(c + 1) * 128], qrT[pp][:, c * 128:(c + 1) * 128], True, True,
                                     [krb[pp][c // 4], qrb[pp][c // 4]], [bkb])
                              TT(sc[c4][:], bk[:], DC4[pp][:].rearrange("p a b -> p (a b)"), ALU.mult, [bkb, dcb[pp]], [scb[c4]])
                          yield

                          def head_fn(cp):
                              bo, bob = bank("bo")
                              for e2 in range(2):
                                  c = cp * 2 + e2
                                  c4, i = c // 4, c % 4
                                  o_ = bo[:, e2 * 256:(e2 + 1) * 256]
                                  MM(o_, sc[c4][:, i * 128:(i + 1) * 128], vr[pp][:, c, :], True, False, [scb[c4], vrb[pp][c]], [bob])
                                  MM(o_, qf[pp][:, c * 128:(c + 1) * 128], Sfb[:, c, :], False, False, [qfb[pp][c // 4], Sfbb[c]], [bob])
                                  MM(o_, qbw[pp][:, c * 128:(c + 1) * 128], Sbb[:, c, :], False, True, [qbb[pp][c // 4], Sbbb[c]], [bob])
                              return bo, bob

                          def tailA_fn(st_, cp):
                              bo, bob = st_
                              for e2 in range(2):
                                  o_ = bo[:, e2 * 256:(e2 + 1) * 256]
                                  K.op("dve", lambda o_=o_, e2=e2: nc.vector.bn_stats(out=st6[:, e2, :], in_=o_), [bob], [stb_[0]])
                                  K.op("dve", lambda e2=e2: nc.vector.bn_aggr(out=mv[:, e2, 0:2], in_=st6[:, e2, :]), [stb_[0]], [stb_[0]])
                              ACT(mv[:, :, 2:3], mv[:, :, 1:2], AF.Ln, [stb_[0]], [stb_[0]], bias=EPS)
                              ACT(mv[:, :, 2:3], mv[:, :, 2:3], AF.Exp, [stb_[0]], [stb_[0]], scale=-0.5)
                              for e2 in range(2):
                                  o_ = bo[:, e2 * 256:(e2 + 1) * 256]
                                  TS(ont[e2][:], o_, mv[:, e2, 0:1], mv[:, e2, 2:3], ALU.subtract, ALU.mult, [bob, stb_[0]], [onb[e2]])

                          def tailB_fn(cp):
                              tb_, tbb = bank("tr")
                              tpv = tb_[:].bitcast(BF16)
                              for e2 in range(2):
                                  for e in range(2):
                                      TRP(tpv[:, (2 * e2 + e) * 128:(2 * e2 + e + 1) * 128], ont[e2][:, e * 128:(e + 1) * 128], identb[:],
                                          [onb[e2], cb], [tbb])
                              for e2 in range(2):
                                  c = cp * 2 + e2
                                  for e in range(2):
                                      STT(orT[:, 2 * h + e, c * 128:(c + 1) * 128], tpv[:, (2 * e2 + e) * 128:(2 * e2 + e + 1) * 128],
                                          pcol[:, PC_RN + 2 * h + e:PC_RN + 2 * h + e + 1], sgT[pp][:, e, c * 128:(c + 1) * 128],
                                          ALU.mult, ALU.mult, [tbb, cb, sgb[pp][e][c // 4]], [orb[2 * h + e][c]])
                          ncp = nchunk // 2
                          cur = head_fn(0)
                          for cp in range(ncp):
                              nxt_ = head_fn(cp + 1) if cp + 1 < ncp else None
                              tailA_fn(cur, cp)
                              yield
                              tailB_fn(cp)
                              yield
                              cur = nxt_

                      g0 = proj_gen(0)
                      for _ in g0:
                          pass
                      for h in range(8):
                          nxt = proj_gen(h + 1) if h + 1 < 8 else None
                          for _ in chain_gen(h):
                              if nxt is not None:
                                  if next(nxt, "done") == "done":
                                      nxt = None
                          if nxt is not None:
                              for _ in nxt:
                                  pass
                K.barrier()

            with ExitStack() as st4:
              if STOP >= 4:
                  xT = sbt(st4, "xT", [128, 16, 512], F32)
                  hB = sbt(st4, "hB", [128, 16, 512], BF16)
                  mT = sbt(st4, "mT", [128, 16, 512], BF16)
                  iot = [sbt(st4, f"iot{i}", [128, D], F32) for i in range(2)]
                  tA = sbt(st4, "tA", [128, 2, 512], F32)
                  sgu = [sbt(st4, f"sgu{i}", [128, 2, 512], BF16) for i in range(2)]
                  sga, sgr = sgu[0], sgu[1]
                  rs4 = sbt(st4, "rs4", [128, 512], F32)
                  tmpf2 = [sbt(st4, f"tmpf{i}", [128, 512], F32) for i in range(2)]
                  tB = tmpf2[0]
                  sq4 = [sbt(st4, f"sq4_{i}", [128, 512], BF16) for i in range(2)]
                  xTb = [Buf() for _ in range(16)]
                  hBb = [Buf() for _ in range(16)]
                  mTb = [Buf() for _ in range(16)]
                  iob = [Buf(), Buf()]
                  rsb4 = Buf()
                  tmpb2 = [Buf(), Buf()]
                  tBb = tmpb2[0]
                  tAb = [Buf(), Buf()]
                  sgub = [Buf(), Buf()]
                  sgab, sgrb = sgub[0], sgub[1]
                  sq4b = [Buf(), Buf()]
                  sqi = [0]
                  ioi = [0]

                  def sumsq_sq(j):
                      s = sqi[0] % 2
                      sqi[0] += 1
                      ACT(sq4[s][:], xT[:, j, :], AF.Square, [xTb[j]], [sq4b[s]])
                      return s

                  def sumsq_mm(s, first, last, ssbk, ssbb):
                      MM(ssbk[:], onesb[:], sq4[s][:], first, last, [sq4b[s], cb], [ssbb])

                  def sumsq_chunk(j, first, last, ssbk, ssbb):
                      sumsq_mm(sumsq_sq(j), first, last, ssbk, ssbb)

                  mod_flush()
                  for tb in range(nblk):
                      t0 = tok0 + tb * 512
                      l0 = tb * 512
                      for t in range(4):
                          s = ioi[0] % 2
                          ioi[0] += 1
                          DMA("sp", iot[s][:], x_d[t0 + t * 128: t0 + (t + 1) * 128, :], xsem[s], [], [iob[s]])
                          for k4 in range(4):
                              tb_, tbb = bank("tr")
                              for kk in range(4):
                                  k = k4 * 4 + kk
                                  TRP(tb_[:, kk * 128:(kk + 1) * 128], iot[s][:, k * 128:(k + 1) * 128], ident, [iob[s], cb], [tbb])
                              dst = xT[:, k4 * 4:(k4 + 1) * 4, t * 128:(t + 1) * 128]
                              src = tb_[:].rearrange("p (k t) -> p k t", k=4)
                              if k4 % 2 == 0:
                                  CPA(dst, src, [tbb], [xTb[k4 * 4 + i] for i in range(4)])
                              else:
                                  CPV(dst, src, [tbb], [xTb[k4 * 4 + i] for i in range(4)])
                      ssbk, ssbb = pbk[5], pbb[5]
                      for j in range(16):
                          sumsq_chunk(j, j == 0, j == 15, ssbk, ssbb)
                      rstd_from(rs4[:], ssbk[:], D, [ssbb], [rsb4])
                      for j in range(16):
                          STT(tmpf2[j % 2][:], xT[:, j, :], col(gsa, j, w), rs4[:], ALU.mult, ALU.mult, [xTb[j], mb, rsb4], [tmpb2[j % 2]])
                          ACT(hB[:, j, :], tmpf2[j % 2][:], AF.Identity, [tmpb2[j % 2], mb], [hBb[j]], bias=col(modt, j, w))
                      for cg in range(8):
                          slot, wb = wload([(win_t(9216 + 256 * cg), 0)])
                          wv = wview(slot, 0, 16, 256)
                          for e in range(2):
                              bk, bkb = bank("mm")
                              acc_fm(bk, bkb, 512, lambda k: wv[:, k, e * 128:(e + 1) * 128], lambda k: hB[:, k, :], 16, lambda k: [wb, hBb[k]])
                              ACT(sga[:, e, :], bk[:], AF.Sigmoid, [bkb], [sgab])
                          slot, wb = wload([(win_t(11264 + 256 * cg), 0)])
                          wv = wview(slot, 0, 16, 256)
                          for e in range(2):
                              bk, bkb = bank("mm")
                              acc_fm(bk, bkb, 512, lambda k: wv[:, k, e * 128:(e + 1) * 128], lambda k: hB[:, k, :], 16, lambda k: [wb, hBb[k]])
                              ACT(sgr[:, e, :], bk[:], AF.Sigmoid, [bkb], [sgrb])
                          slot, wb = wload([(wt("w_ba", cg), 0)])
                          wv = wview(slot, 0, 16, 256)
                          for e in range(2):
                              bk, bkb = bank("mm")
                              acc_fm(bk, bkb, 512, lambda k: wv[:, k, e * 128:(e + 1) * 128], lambda k: oaT[:, k, l0:l0 + 512], 16,
                                     lambda k: [wb, oab[k][tb]])
                              TT(tA[:, e, :], bk[:], sga[:, e, :], ALU.mult, [bkb, sgab], [tAb[e]])
                          slot, wb = wload([(wt("w_br", cg), 0)])
                          wv = wview(slot, 0, 16, 256)
                          for e in range(2):
                              bk, bkb = bank("mm")
                              acc_fm(bk, bkb, 512, lambda k: wv[:, k, e * 128:(e + 1) * 128], lambda k: orT[:, k, l0:l0 + 512], 16,
                                     lambda k: [wb] + [orb[k][4 * tb + i] for i in range(4)])
                              TT(tB[:], bk[:], sgr[:, e, :], ALU.mult, [bkb, sgrb], [tBb])
                              TT(mT[:, 2 * cg + e, :], tA[:, e, :], tB[:], ALU.add, [tAb[e], tBb], [mTb[2 * cg + e]])
                      pend = None
                      for cg in range(8):
                          slot, wb = wload([(wt("w_out", cg), 0)])
                          wv = wview(slot, 0, 16, 256)
                          for e in range(2):
                              j = 2 * cg + e
                              bk, bkb = bank("mm")
                              acc_fm(bk, bkb, 512, lambda k: wv[:, k, e * 128:(e + 1) * 128], lambda k: mT[:, k, :], 16, lambda k: [wb, mTb[k]])
                              if pend is not None:
                                  sumsq_mm(pend[0], pend[1] == 0, False, ssbk, ssbb)
                              STT(xT[:, j, :], bk[:], col(modt, 32 + j, w), xT[:, j, :], ALU.mult, ALU.add, [bkb, mb2, xTb[j]], [xTb[j]])
                              pend = (sumsq_sq(j), j)
                      sumsq_mm(pend[0], False, True, ssbk, ssbb)
                      rstd_from(rs4[:], ssbk[:], D, [ssbb], [rsb4])
                      for j in range(16):
                          STT(tmpf2[j % 2][:], xT[:, j, :], col(gsf, j, w), rs4[:], ALU.mult, ALU.mult, [xTb[j], mb2, rsb4], [tmpb2[j % 2]])
                          ACT(hB[:, j, :], tmpf2[j % 2][:], AF.Identity, [tmpb2[j % 2], mb2], [hBb[j]], bias=col(modt, 48 + j, w))
                      for gi, (tl0, ntl) in enumerate(FFN_GROUPS):
                          nch = 2 * ntl
                          for ti in range(ntl):
                              c0 = 256 * (tl0 + ti)
                              slot, wb = wload([(wt("w_g", tl0 + ti), 0)])
                              wv = wview(slot, 0, 16, 256)
                              su = sgu[ti % 2]
                              for e in range(2):
                                  bk, bkb = bank("mm")
                                  acc_fm(bk, bkb, 512, lambda k: wv[:, k, e * 128:(e + 1) * 128], lambda k: hB[:, k, :], 16, lambda k: [wb, hBb[k]])
                                  ACT(su[:, e, :], bk[:], AF.Silu, [bkb], [sgub[ti % 2]])
                              slot, wb = wload([(wt("w_u", tl0 + ti), 0)])
                              wv = wview(slot, 0, 16, 256)
                              for e in range(2):
                                  ci = 2 * ti + e
                                  bk, bkb = bank("mm")
                                  acc_fm(bk, bkb, 512, lambda k: wv[:, k, e * 128:(e + 1) * 128], lambda k: hB[:, k, :], 16, lambda k: [wb, hBb[k]])
                                  TT(mT[:, ci, :], bk[:], su[:, e, :], ALU.mult, [bkb, sgub[ti % 2]], [mTb[ci]])
                          r0 = 256 * tl0
                          lastg = gi == len(FFN_GROUPS) - 1
                          pend = None
                          for cgo in range(8):
                              slot, wb = wload([(wt("w_d", 8 * gi + cgo), 0)])
                              wv = wview(slot, 0, nch, 256)
                              for e in range(2):
                                  j = 2 * cgo + e
                                  bk, bkb = bank("mm")
                                  acc_fm(bk, bkb, 512, lambda k: wv[:, k, e * 128:(e + 1) * 128], lambda k: mT[:, k, :], nch, lambda k: [wb, mTb[k]])
                                  if pend is not None:
                                      sumsq_mm(pend[0], pend[1] == 0, False, ssbk, ssbb)
                                  STT(xT[:, j, :], bk[:], col(modt, 80 + j, w), xT[:, j, :], ALU.mult, ALU.add, [bkb, mb2, xTb[j]], [xTb[j]])
                                  if lastg:
                                      pend = (sumsq_sq(j), j)
                          if lastg:
                              sumsq_mm(pend[0], False, True, ssbk, ssbb)
                      rstd_from(rs4[:], ssbk[:], D, [ssbb], [rsb4])
                      for j in range(16):
                          STT(xT[:, j, :], xT[:, j, :], pcol[:, PC_FN + j:PC_FN + j + 1], rs4[:], ALU.mult, ALU.mult, [xTb[j], cb, rsb4], [xTb[j]])
                      for t in range(4):
                          s = ioi[0] % 2
                          ioi[0] += 1
                          for j4 in range(4):
                              tb_, tbb = bank("tr")
                              for jj in range(4):
                                  j = j4 * 4 + jj
                                  TRP(tb_[:, jj * 128:(jj + 1) * 128], xT[:, j, t * 128:(t + 1) * 128], ident, [xTb[j], cb], [tbb])
                              if j4 % 2 == 0:
                                  CPA(iot[s][:, j4 * 512:(j4 + 1) * 512], tb_[:], [tbb], [iob[s]])
                              else:
                                  CPV(iot[s][:, j4 * 512:(j4 + 1) * 512], tb_[:], [tbb], [iob[s]])
                          DMA("sp", y_d[t0 + t * 128: t0 + (t + 1) * 128, :], iot[s][:], xsem[s], [iob[s]], [], is_out=True)
            K.barrier()
        K.finish()
        print(f"[kernel] ops={K.nops} waits={K.nwait}", flush=True)
    return nc


def _const_table():
    ct = np.zeros((128, NCT), np.float32)
    p = np.arange(128)
    ct[:, CT_ID:CT_ID + 128] = np.eye(128, dtype=np.float32)
    rm = np.zeros((128, 128), np.float32)
    for m in range(128):
        if (m % 64) < 32:
            rm[m + 32, m] = -1.0
        else:
            rm[m - 32, m] = 1.0
    ct[:, CT_RM:CT_RM + 128] = rm
    j = p[:, None].astype(np.float32)
    i = p[None, :].astype(np.float32)
    ct[:, CT_DF:CT_DF + 128] = np.maximum(i - j, 0)
    ct[:, CT_MF:CT_MF + 128] = (i > j)
    ct[:, CT_DB:CT_DB + 128] = np.maximum(j - i, 0)
    ct[:, CT_MB:CT_MB + 128] = (j > i)
    ct[:, CT_I2:CT_I2 + 128] = 2.0 * np.eye(128)
    ct[:, CT_XF:CT_XF + 128] = np.broadcast_to(i + 1.0, (128, 128))
    ct[:, CT_XB:CT_XB + 128] = np.broadcast_to(128.0 - i, (128, 128))
    ct[:, CT_ZF] = 127.0 - p
    ct[:, CT_ZB] = p
    tok = np.arange(1024)
    row = (tok // 64).astype(np.float32)
    colp = (tok % 64).astype(np.float32)
    inv_freq = (np.float32(10000.0) ** (-np.arange(32, dtype=np.float32) / np.float32(32))).astype(np.float32)
    ang = np.zeros((128, 1024), np.float32)
    for d in range(128):
        pos = row if d < 64 else colp
        ang[d] = pos * inv_freq[d % 32]
    ct[:, CT_COS:CT_COS + 1024] = np.cos(ang)
    ct[:, CT_SIN:CT_SIN + 1024] = np.sin(ang)
    return ct


_NC_CACHE = {}


def kernel(x_prompt, x_sample, cache_attn_k, cache_attn_v, state_ret_fwd, state_ret_bwd, c, c_ctx,
           norm_attn, norm_ffn, w_mod, b_mod, w_in, q_norm, k_norm, ret_decay_fwd, ret_decay_bwd,
           ret_norm, w_branch_attn, w_branch_ret, w_out, w_ffn_gate, w_ffn_up, w_ffn_down, final_norm):
    f = lambda a: np.ascontiguousarray(np.asarray(a, dtype=np.float32))
    x_prompt, x_sample = f(x_prompt), f(x_sample)
    ctab = _const_table()

    def cols(v):
        return np.asarray(v, np.float32).reshape(-1, 128).T

    shared = {
        "ctab": ctab,
        "w_mod": pack_weight("w_mod", w_mod[0]), "w_in": pack_weight("w_in", w_in[0]),
        "w_ba": pack_weight("w_ba", w_branch_attn[0]), "w_br": pack_weight("w_br", w_branch_ret[0]),
        "w_out": pack_weight("w_out", w_out[0]), "w_g": pack_weight("w_g", w_ffn_gate[0]),
        "w_u": pack_weight("w_u", w_ffn_up[0]), "w_d": pack_weight("w_d", w_ffn_down[0]),
    }
    in_maps = []
    for core in range(8):
        pc = np.zeros((128, NPC), np.float32)
        pc[:, PC_BMOD:PC_BMOD + 96] = cols(b_mod[0])
        pc[:, PC_NA:PC_NA + 16] = cols(norm_attn[0])
        pc[:, PC_NF:PC_NF + 16] = cols(norm_ffn[0])
        pc[:, PC_FN:PC_FN + 16] = cols(final_norm)
        pc[:, PC_RN:PC_RN + 16] = cols(ret_norm[0])
        pc[:, PC_C:PC_C + 16] = cols(c[core])
        pc[:, PC_CC:PC_CC + 16] = cols(c_ctx)
        pc[:, PC_QN] = np.asarray(q_norm[0], np.float32)
        pc[:, PC_KN] = np.asarray(k_norm[0], np.float32)
        pc[:, PC_DF:PC_DF + 8] = np.asarray(ret_decay_fwd[0], np.float32)[None, :]
        pc[:, PC_DB:PC_DB + 8] = np.asarray(ret_decay_bwd[0], np.float32)[None, :]
        m = dict(shared)
        m["x"] = np.ascontiguousarray(np.concatenate([x_sample[core], x_prompt[2 * core], x_prompt[2 * core + 1]], axis=0))
        m["ck"] = f(cache_attn_k[core, 0]).reshape(512, 512)
        m["cv"] = f(cache_attn_v[core, 0]).reshape(512, 512)
        m["sf0"] = f(state_ret_fwd[core, 0])
        m["sb0"] = f(state_ret_bwd[core, 0])
        m["pcols"] = pc
        in_maps.append(m)

    if "nc" not in _NC_CACHE:
        _NC_CACHE["nc"] = build()
    nc = _NC_CACHE["nc"]
    res = run_bass_kernel_spmd(nc, in_maps, core_ids=list(range(8)))
    y_prompt = np.zeros((16, 256, D), np.float32)
    y_sample = np.zeros((8, 1024, D), np.float32)
    new_k = np.zeros((16, 1, 256, 4, 128), np.float32)
    new_v = np.zeros((16, 1, 256, 4, 128), np.float32)
    new_sf = np.zeros((16, 1, 8, 128, 256), np.float32)
    new_sb = np.zeros((16, 1, 8, 128, 256), np.float32)
    for core in range(8):
        r = res.results[core]
        y = r["y"]
        y_sample[core] = y[0:1024]
        y_prompt[2 * core] = y[1024:1280]
        y_prompt[2 * core + 1] = y[1280:1536]
        new_k[2 * core:2 * core + 2, 0] = r["nk"].reshape(2, 256, 4, 128)
        new_v[2 * core:2 * core + 2, 0] = r["nv"].reshape(2, 256, 4, 128)
        new_sf[2 * core:2 * core + 2, 0] = r["nsf"]
        new_sb[2 * core:2 * core + 2, 0] = r["nsb"]
    return (y_prompt, y_sample, new_k, new_v, new_sf, new_sb)
```
